# Optimizing a Trainium2 kernel written in Bass

```python
import math
import jax, jax.numpy as jnp
from jax import lax
import numpy as np

D_MODEL = 1024
BATCH = 8
SEQ = 4096
DEPTH = 2

GROUP_WIDTH = D_MODEL // 4
D_MIX = 4 * GROUP_WIDTH
EPS = 1e-6

GLA_HEADS = 4
GLA_DK = GROUP_WIDTH // (2 * GLA_HEADS)
GLA_DV = GROUP_WIDTH // GLA_HEADS
GLA_GATE_RANK = 16
GLA_GATE_NORM = 16.0
GLA_CHUNK = 64

DIFF_HEADS = 4
DIFF_DH = GROUP_WIDTH // (2 * DIFF_HEADS)
DIFF_DV = 2 * DIFF_DH
DIFF_BLOCK = 128
ROPE_THETA = 10000.0

S5_CH = 16
S5_GROUPS = GROUP_WIDTH // S5_CH
S5_STATE = 64

LRU_WIDTH = GROUP_WIDTH
LRU_BLOCKS = 4
LRU_BLOCK = LRU_WIDTH // LRU_BLOCKS
LRU_CONV = 4
LRU_C = 8.0

D_FF = 4 * D_MODEL

SPLITS = (GLA_HEADS * GLA_DK, GLA_HEADS * GLA_DK, GLA_HEADS * GLA_DV, GLA_GATE_RANK, GLA_HEADS * GLA_DV,
          DIFF_HEADS * 2 * DIFF_DH, DIFF_HEADS * 2 * DIFF_DH, DIFF_HEADS * DIFF_DV,
          GROUP_WIDTH,
          LRU_WIDTH, LRU_WIDTH)
D_IN = sum(SPLITS)

kernel_name = "hybrid_parallel_gla_diff_s5_rglru"


def _rmsnorm(x, g):
    xf = x.astype(jnp.float32)
    y = xf * lax.rsqrt(jnp.mean(xf * xf, axis=-1, keepdims=True) + EPS) * g.astype(jnp.float32)
    return y.astype(x.dtype)


def _split_cols(p):
    offs = [int(o) for o in np.cumsum(SPLITS)[:-1]]
    return jnp.split(p, offs, axis=-1)


def _lin_combine(e1, e2):
    a1, b1 = e1
    a2, b2 = e2
    return (a2 * a1, a2 * b1 + b2)


def _gla(q, k, v, glr, og, w_gate, b_gate, norm_g):
    f32 = jnp.float32
    B, S, _ = q.shape
    H, DK, DV, C = GLA_HEADS, GLA_DK, GLA_DV, GLA_CHUNK
    NC = S // C

    def chunks(t, d):
        return t.astype(f32).reshape(B, NC, C, H, d).transpose(0, 3, 1, 2, 4)

    qc = chunks(q, DK) * (DK ** -0.5)
    kc = chunks(k, DK)
    vc = chunks(v, DV)
    g = jax.nn.log_sigmoid(glr.astype(f32) @ w_gate.astype(f32) + b_gate.astype(f32)) / GLA_GATE_NORM
    gc = g.reshape(B, NC, C, H, DK).transpose(0, 3, 1, 2, 4)
    bcum = jnp.cumsum(gc, axis=3)
    blast = bcum[:, :, :, -1:, :]
    q_dec = qc * jnp.exp(bcum)
    k_dec = kc * jnp.exp(-bcum)
    k_st = kc * jnp.exp(blast - bcum)
    mask = jnp.tril(jnp.ones((C, C), dtype=bool))
    attn = jnp.where(mask, jnp.einsum('bhncd,bhnjd->bhncj', q_dec, k_dec), 0.0)
    o = jnp.einsum('bhncj,bhnjv->bhncv', attn, vc)
    kv = jnp.einsum('bhncd,bhncv->bhndv', k_st, vc)
    dec = jnp.exp(blast[:, :, :, 0, :])

    def step(state, inp):
        d, kvn = inp
        return d[..., None] * state + kvn, state

    _, s_prev = lax.scan(step, jnp.zeros((B, H, DK, DV), f32),
                         (jnp.moveaxis(dec, 2, 0), jnp.moveaxis(kv, 2, 0)))
    s_prev = jnp.moveaxis(s_prev, 0, 2)
    o = o + jnp.einsum('bhncd,bhndv->bhncv', q_dec, s_prev)
    o = o.transpose(0, 2, 3, 1, 4).reshape(B, S, H, DV)
    o = _rmsnorm(o, norm_g) * jax.nn.silu(og.astype(f32).reshape(B, S, H, DV))
    return o.reshape(B, S, H * DV)


def _rope_tables(S, dh):
    inv = ROPE_THETA ** (-jnp.arange(0, dh, 2, dtype=jnp.float32) / dh)
    ang = jnp.arange(S, dtype=jnp.float32)[:, None] * inv[None, :]
    emb = jnp.concatenate([ang, ang], axis=-1)
    return jnp.cos(emb), jnp.sin(emb)


def _rope(x, cos, sin):
    c = cos[None, :, None, None, :].astype(x.dtype)
    s = sin[None, :, None, None, :].astype(x.dtype)
    x1, x2 = jnp.split(x, 2, axis=-1)
    return x * c + jnp.concatenate([-x2, x1], axis=-1) * s


def _diff_attn(q, k, v, lq1, lk1, lq2, lk2, norm_g, lam_init):
    f32 = jnp.float32
    B, S, _ = q.shape
    H, DH, DV, BLK = DIFF_HEADS, DIFF_DH, DIFF_DV, DIFF_BLOCK
    NB = S // BLK
    cos, sin = _rope_tables(S, DH)
    q = _rope(q.reshape(B, S, H, 2, DH), cos, sin) * (DH ** -0.5)
    k = _rope(k.reshape(B, S, H, 2, DH), cos, sin)
    v = v.reshape(B, S, H, DV)
    lam = (jnp.exp(jnp.sum(lq1.astype(f32) * lk1.astype(f32)))
           - jnp.exp(jnp.sum(lq2.astype(f32) * lk2.astype(f32))) + lam_init)
    qb = q.reshape(B, NB, BLK, H, 2, DH).transpose(1, 0, 2, 3, 4, 5)
    kpos = jnp.arange(S)

    def block(args):
        q_blk, i = args
        s = jnp.einsum('bqhcd,bkhcd->bhcqk', q_blk, k).astype(f32)
        qpos = i * BLK + jnp.arange(BLK)
        mask = kpos[None, :] <= qpos[:, None]
        p = jax.nn.softmax(jnp.where(mask, s, -jnp.inf), axis=-1)
        w = p[:, :, 0] - lam * p[:, :, 1]
        return jnp.einsum('bhqk,bkhv->bqhv', w.astype(v.dtype), v)

    o = lax.map(block, (qb, jnp.arange(NB)))
    o = o.transpose(1, 0, 2, 3, 4).reshape(B, S, H, DV)
    o = _rmsnorm(o, norm_g) * (1.0 - lam_init)
    return o.reshape(B, S, H * DV)


def _s5(u, log_step, a_re, a_im, b_re, b_im, c_re, c_im, d, w_glu, b_glu):
    f32 = jnp.float32
    B, S, W = u.shape
    G, P, CH = S5_GROUPS, S5_STATE, S5_CH
    uf = u.astype(f32)
    ug = uf.reshape(B, S, G, CH)
    step = jnp.exp(log_step.astype(f32))[:, None]
    lr, li = a_re.astype(f32), a_im.astype(f32)
    mag = jnp.exp(lr * step)
    ab_re = mag * jnp.cos(li * step)
    ab_im = mag * jnp.sin(li * step)
    den = lr * lr + li * li
    nr, ni = ab_re - 1.0, ab_im
    cr = (nr * lr + ni * li) / den
    ci = (ni * lr - nr * li) / den
    br, bi = b_re.astype(f32), b_im.astype(f32)
    bb_re = cr[..., None] * br - ci[..., None] * bi
    bb_im = cr[..., None] * bi + ci[..., None] * br
    bu_re = jnp.einsum('bsgh,gph->bsgp', ug, bb_re)
    bu_im = jnp.einsum('bsgh,gph->bsgp', ug, bb_im)
    ar = jnp.broadcast_to(ab_re[None, None], (1, S, G, P))
    ai = jnp.broadcast_to(ab_im[None, None], (1, S, G, P))

    def combine(e1, e2):
        a1r, a1i, b1r, b1i = e1
        a2r, a2i, b2r, b2i = e2
        return (a2r * a1r - a2i * a1i, a2r * a1i + a2i * a1r,
                a2r * b1r - a2i * b1i + b2r, a2r * b1i + a2i * b1r + b2i)

    _, _, xr, xi = lax.associative_scan(combine, (ar, ai, bu_re, bu_im), axis=1)
    y = (jnp.einsum('gqp,bsgp->bsgq', c_re.astype(f32), xr)
         - jnp.einsum('gqp,bsgp->bsgq', c_im.astype(f32), xi))
    y = y.reshape(B, S, W) + d.astype(f32) * uf
    y = jax.nn.gelu(y)
    return y * jax.nn.sigmoid(y @ w_glu.astype(f32) + b_glu.astype(f32))


def _rglru(xb, gate, conv_w, conv_b, w_a, b_a, w_x, b_x, lam):
    f32 = jnp.float32
    B, S, W = xb.shape
    xc = lax.conv_general_dilated(xb.astype(f32), conv_w.astype(f32)[:, None, :],
                                  window_strides=(1,), padding=[(LRU_CONV - 1, 0)],
                                  dimension_numbers=('NWC', 'WIO', 'NWC'),
                                  feature_group_count=W) + conv_b.astype(f32)
    xr = xc.reshape(B, S, LRU_BLOCKS, LRU_BLOCK)
    r = jax.nn.sigmoid(jnp.einsum('bsnc,ncd->bsnd', xr, w_a.astype(f32)).reshape(B, S, W) + b_a.astype(f32))
    i = jax.nn.sigmoid(jnp.einsum('bsnc,ncd->bsnd', xr, w_x.astype(f32)).reshape(B, S, W) + b_x.astype(f32))
    log_a = -LRU_C * r * jax.nn.softplus(-lam.astype(f32))
    a = jnp.exp(log_a)
    mult = jnp.sqrt(jnp.maximum(-jnp.expm1(2.0 * log_a), 1e-12))
    _, h = lax.associative_scan(_lin_combine, (a, mult * (i * xc)), axis=1)
    return h * jax.nn.gelu(gate.astype(f32))


def setup_inputs(seed: int = 0) -> dict:
    key = jax.random.key(seed)
    ks = iter(jax.random.split(key, 48))
    f32 = jnp.float32
    L = DEPTH

    def nrm(shape, scale):
        return jax.random.normal(next(ks), shape, f32) * scale

    def gain(shape):
        return 1.0 + 0.01 * jax.random.normal(next(ks), shape, f32)

    x = nrm((BATCH, SEQ, D_MODEL), 1.0)
    ln_mix_pre = gain((L, D_MODEL))
    ln_mix_post = gain((L, D_MODEL))
    ln_ffn_pre = gain((L, D_MODEL))
    ln_ffn_post = gain((L, D_MODEL))
    w_in = nrm((L, D_MODEL, D_IN), D_MODEL ** -0.5)
    w_out = nrm((L, D_MIX, D_MODEL), D_MIX ** -0.5)
    gla_w_gate = nrm((L, GLA_GATE_RANK, GLA_HEADS * GLA_DK), GLA_GATE_RANK ** -0.5)
    gla_b_gate = nrm((L, GLA_HEADS * GLA_DK), 0.01)
    gla_norm = gain((L, GLA_DV))
    diff_lq1 = nrm((L, DIFF_DH), 0.1)
    diff_lk1 = nrm((L, DIFF_DH), 0.1)
    diff_lq2 = nrm((L, DIFF_DH), 0.1)
    diff_lk2 = nrm((L, DIFF_DH), 0.1)
    diff_norm = gain((L, DIFF_DV))
    s5_log_step = jax.random.uniform(next(ks), (L, S5_GROUPS), f32,
                                     minval=math.log(1e-3), maxval=math.log(1e-1))
    s5_a_re = -0.5 + nrm((L, S5_GROUPS, S5_STATE), 0.01)
    s5_a_im = math.pi * jnp.arange(S5_STATE, dtype=f32)[None, None, :] + nrm((L, S5_GROUPS, S5_STATE), 0.01)
    s5_b_re = nrm((L, S5_GROUPS, S5_STATE, S5_CH), (2 * S5_CH) ** -0.5)
    s5_b_im = nrm((L, S5_GROUPS, S5_STATE, S5_CH), (2 * S5_CH) ** -0.5)
    s5_c_re = nrm((L, S5_GROUPS, S5_CH, S5_STATE), (2 * S5_STATE) ** -0.5)
    s5_c_im = nrm((L, S5_GROUPS, S5_CH, S5_STATE), (2 * S5_STATE) ** -0.5)
    s5_d = nrm((L, GROUP_WIDTH), 1.0)
    s5_w_glu = nrm((L, GROUP_WIDTH, GROUP_WIDTH), GROUP_WIDTH ** -0.5)
    s5_b_glu = nrm((L, GROUP_WIDTH), 0.01)
    lru_conv_w = nrm((L, LRU_CONV, LRU_WIDTH), LRU_CONV ** -0.5)
    lru_conv_b = nrm((L, LRU_WIDTH), 0.01)
    lru_w_a = nrm((L, LRU_BLOCKS, LRU_BLOCK, LRU_BLOCK), LRU_BLOCK ** -0.5)
    lru_b_a = nrm((L, LRU_WIDTH), 0.01)
    lru_w_x = nrm((L, LRU_BLOCKS, LRU_BLOCK, LRU_BLOCK), LRU_BLOCK ** -0.5)
    lru_b_x = nrm((L, LRU_WIDTH), 0.01)
    a_pow = jax.random.uniform(next(ks), (L, LRU_WIDTH), f32, minval=0.9, maxval=0.999)
    a0 = a_pow ** (1.0 / LRU_C)
    lru_lambda = jnp.log(a0) - jnp.log1p(-a0)
    ffn_w1 = nrm((L, D_MODEL, D_FF), D_MODEL ** -0.5)
    ffn_w2 = nrm((L, D_FF, D_MODEL), D_FF ** -0.5)
    return {"x": x, "ln_mix_pre": ln_mix_pre, "ln_mix_post": ln_mix_post,
            "ln_ffn_pre": ln_ffn_pre, "ln_ffn_post": ln_ffn_post,
            "w_in": w_in, "w_out": w_out,
            "gla_w_gate": gla_w_gate, "gla_b_gate": gla_b_gate, "gla_norm": gla_norm,
            "diff_lq1": diff_lq1, "diff_lk1": diff_lk1, "diff_lq2": diff_lq2, "diff_lk2": diff_lk2,
            "diff_norm": diff_norm,
            "s5_log_step": s5_log_step, "s5_a_re": s5_a_re, "s5_a_im": s5_a_im,
            "s5_b_re": s5_b_re, "s5_b_im": s5_b_im, "s5_c_re": s5_c_re, "s5_c_im": s5_c_im,
            "s5_d": s5_d, "s5_w_glu": s5_w_glu, "s5_b_glu": s5_b_glu,
            "lru_conv_w": lru_conv_w, "lru_conv_b": lru_conv_b, "lru_w_a": lru_w_a, "lru_b_a": lru_b_a,
            "lru_w_x": lru_w_x, "lru_b_x": lru_b_x, "lru_lambda": lru_lambda,
            "ffn_w1": ffn_w1, "ffn_w2": ffn_w2}


def reference(x, ln_mix_pre, ln_mix_post, ln_ffn_pre, ln_ffn_post, w_in, w_out,
              gla_w_gate, gla_b_gate, gla_norm,
              diff_lq1, diff_lk1, diff_lq2, diff_lk2, diff_norm,
              s5_log_step, s5_a_re, s5_a_im, s5_b_re, s5_b_im, s5_c_re, s5_c_im,
              s5_d, s5_w_glu, s5_b_glu,
              lru_conv_w, lru_conv_b, lru_w_a, lru_b_a, lru_w_x, lru_b_x, lru_lambda,
              ffn_w1, ffn_w2):
    for l in range(DEPTH):
        lam_init = 0.8 - 0.6 * math.exp(-0.3 * l)
        h = _rmsnorm(x, ln_mix_pre[l])
        proj = h @ w_in[l]
        (g_q, g_k, g_v, g_lr, g_og, d_q, d_k, d_v, s_u, r_x, r_g) = _split_cols(proj)
        o_a = _gla(g_q, g_k, g_v, g_lr, g_og, gla_w_gate[l], gla_b_gate[l], gla_norm[l])
        o_b = _diff_attn(d_q, d_k, d_v, diff_lq1[l], diff_lk1[l], diff_lq2[l], diff_lk2[l],
                         diff_norm[l], lam_init)
        o_c = _s5(s_u, s5_log_step[l], s5_a_re[l], s5_a_im[l], s5_b_re[l], s5_b_im[l],
                  s5_c_re[l], s5_c_im[l], s5_d[l], s5_w_glu[l], s5_b_glu[l])
        o_d = _rglru(r_x, r_g, lru_conv_w[l], lru_conv_b[l], lru_w_a[l], lru_b_a[l],
                     lru_w_x[l], lru_b_x[l], lru_lambda[l])
        mix = jnp.concatenate([o_a.astype(x.dtype), o_b.astype(x.dtype),
                               o_c.astype(x.dtype), o_d.astype(x.dtype)], axis=-1) @ w_out[l]
        x = x + _rmsnorm(mix, ln_mix_post[l])
        h = _rmsnorm(x, ln_ffn_pre[l])
        f = jnp.square(jax.nn.relu(h @ ffn_w1[l])) @ ffn_w2[l]
        x = x + _rmsnorm(f, ln_ffn_post[l])
    return x
```

```python
import math
import numpy as np
import concourse.bass as bass
import concourse.mybir as mybir
from concourse.bass_utils import run_bass_kernel_spmd

F32 = mybir.dt.float32
BF16 = mybir.dt.bfloat16
I32 = mybir.dt.int32
AF = mybir.ActivationFunctionType
ALU = mybir.AluOpType
AX = mybir.AxisListType

S = 4096
D = 1024
DFF = 4096
L = 2
EPS = 1e-6
NCORES = 8


class Buf:
    __slots__ = ("name", "w", "r", "excl")

    def __init__(self, name, excl=False):
        self.name = name
        self.w = {}
        self.r = {}
        self.excl = excl


class Eng:
    def __init__(self, nc, name, h, sync_self):
        self.name = name
        self.h = h
        self.sem = nc.semaphore("sem_" + name).__enter__()
        self.cnt = 0
        self.waited = {}
        self.sync_self = sync_self


class K:
    def __init__(self, nc, n_dma_sems=20):
        self.nc = nc
        self.pe = Eng(nc, "pe", nc.tensor, False)
        self.act = Eng(nc, "act", nc.scalar, True)
        self.dve = Eng(nc, "dve", nc.vector, True)
        self.pool = Eng(nc, "pool", nc.gpsimd, True)
        self.sp = Eng(nc, "sp", nc.sync, False)
        self.dsem = {}
        for q in ("sp", "pool", "act"):
            self.dsem[q] = [[nc.semaphore("dq_%s_%d" % (q, i)).__enter__(), 0] for i in range(n_dma_sems)]
        self.dnext = {"sp": 0, "pool": 0, "act": 0}
        self._n = 0
        self.stacks = []

    def push(self):
        from contextlib import ExitStack
        self.stacks.append(ExitStack())

    def pop(self):
        self.barrier()
        self.stacks.pop().close()

    def barrier(self):
        engs = [self.pe, self.act, self.dve, self.pool, self.sp]
        deps = [(e.name, e.sem, e.cnt) for e in engs if e.cnt > 0]
        for q, pool in self.dsem.items():
            for j, (sem, cnt) in enumerate(pool):
                if cnt > 0:
                    deps.append(("d_%s_%d" % (q, j), sem, 16 * cnt))
        for e in engs:
            sv = e.sync_self
            e.sync_self = False
            self._wait(e, deps)
            e.sync_self = sv

    def uid(self, p="t"):
        self._n += 1
        return "%s%d" % (p, self._n)

    def sb(self, shape, dtype, name=None):
        cm = self.nc.sbuf_tensor(name or self.uid("sb"), list(shape), dtype)
        if self.stacks:
            return self.stacks[-1].enter_context(cm)
        return cm.__enter__()

    def ps(self, shape, dtype, name=None):
        return self.nc.psum_tensor(name or self.uid("ps"), list(shape), dtype).__enter__()

    def _wait(self, e, deps):
        best = {}
        for (key, sem, val) in deps:
            if key == e.name and not e.sync_self:
                continue
            if key not in best or best[key][1] < val:
                best[key] = (sem, val)
        for key, (sem, val) in best.items():
            if e.waited.get(key, 0) >= val:
                continue
            e.h.wait_ge(sem, val)
            e.waited[key] = val

    def _deps(self, reads, writes):
        deps = []
        for b in reads:
            deps.extend(b.w.values())
            if b.excl:
                deps.extend(b.r.values())
        for b in writes:
            deps.extend(b.w.values())
            deps.extend(b.r.values())
        return deps

    def op(self, e, fn, reads=(), writes=()):
        self._wait(e, self._deps(reads, writes))
        inst = fn(e.h)
        e.cnt += 1
        inst.then_inc(e.sem, 1)
        t = (e.name, e.sem, e.cnt)
        for b in reads:
            b.r[e.name] = t
        for b in writes:
            b.w[e.name] = t
        return t

    def dma(self, q, out, in_, reads=(), writes=()):
        pool = self.dsem[q.name]
        j = self.dnext[q.name]
        self.dnext[q.name] = (j + 1) % len(pool)
        sem, cnt = pool[j]
        key = "d_%s_%d" % (q.name, j)
        deps = self._deps(reads, writes)
        if cnt > 0:
            deps.append((key, sem, 16 * cnt))
        self._wait(q, deps)
        q.h.dma_start(out=out, in_=in_).then_inc(sem, 16)
        pool[j][1] = cnt + 1
        t = (key, sem, 16 * (cnt + 1))
        for b in reads:
            b.r[key] = t
        for b in writes:
            b.w[key] = t
        return t

    def finish(self, bufs):
        deps = []
        for b in bufs:
            deps.extend(b.w.values())
        self._wait(self.sp, deps)


def make_ident(k, nc):
    ident = k.sb([128, 128], BF16, "ident")
    b = Buf("ident")
    k.op(k.pool, lambda h: h.memset(ident[:], 1.0), writes=[b])
    k.op(k.pool, lambda h: h.affine_select(out=ident[:], in_=ident[:], pattern=[[1, 128]],
                                            compare_op=ALU.is_equal, fill=0.0, base=0,
                                            channel_multiplier=-1), reads=[b], writes=[b])
    return ident, b


class NormT:
    def __init__(self, k, ns, ident, ident_b, pT, pT_b, tag):
        self.k = k
        self.ns = ns
        self.ident, self.ident_b = ident, ident_b
        self.pT, self.pT_b = pT, pT_b
        self.xn = k.sb([128, 1024], BF16, tag + "_xn")
        self.xn_b = Buf(tag + "_xn")
        self.junk = k.sb([128, 1024], BF16, tag + "_junk")
        self.junk_b = Buf(tag + "_junk")
        self.st = k.sb([128, 4 * ns], F32, tag + "_st")
        self.st_b = Buf(tag + "_st")

    def run(self, xin, xin_b, gcol, gcol_b, hT, hT_b):
        k, ns = self.k, self.ns
        st = self.st
        for s in range(ns):
            k.op(k.act, lambda h: h.activation(out=self.junk[:], in_=xin[:, s, :], func=AF.Square,
                                               accum_out=st[:, s:s + 1]),
                 reads=[xin_b], writes=[self.junk_b, self.st_b])
        k.op(k.act, lambda h: h.activation(out=st[:, ns:2 * ns], in_=st[:, 0:ns], func=AF.Ln,
                                           scale=1.0 / D, bias=self.k.eps_ap),
             reads=[self.st_b], writes=[self.st_b])
        k.op(k.act, lambda h: h.activation(out=st[:, 2 * ns:3 * ns], in_=st[:, ns:2 * ns], func=AF.Exp,
                                           scale=-0.5),
             reads=[self.st_b], writes=[self.st_b])
        for s in range(ns):
            k.op(k.act, lambda h: h.activation(out=self.xn[:], in_=xin[:, s, :], func=AF.Copy,
                                               scale=st[:, 2 * ns + s:2 * ns + s + 1]),
                 reads=[xin_b, self.st_b], writes=[self.xn_b])
            for kk in range(8):
                k.op(k.pe, lambda h: h.transpose(out=self.pT[:, kk * 128:(kk + 1) * 128],
                                                 in_=self.xn[:, kk * 128:(kk + 1) * 128],
                                                 identity=self.ident[:]),
                     reads=[self.xn_b, self.ident_b], writes=[self.pT_b])
            k.op(k.dve, lambda h: h.tensor_tensor(
                out=hT[:, :, s * 128:(s + 1) * 128],
                in0=self.pT[:].rearrange("p (k t) -> p k t", k=8),
                in1=gcol[:].unsqueeze(2).to_broadcast([128, 8, 128]),
                op=ALU.mult),
                reads=[self.pT_b, gcol_b], writes=[hT_b])


def post_norm_residual(k, ps_halves, ps_bufs, xres_ap, xres_b, gpost, gpost_b, st, st_b, junk, junk_b,
                       tmp, tmp_b, out_ap, out_b):
    for hf in range(2):
        k.op(k.act, lambda h: h.activation(out=junk[:, 0:512], in_=ps_halves[hf], func=AF.Square,
                                           accum_out=st[:, hf:hf + 1]),
             reads=[ps_bufs[hf]], writes=[junk_b, st_b])
    k.op(k.dve, lambda h: h.tensor_tensor(out=st[:, 2:3], in0=st[:, 0:1], in1=st[:, 1:2], op=ALU.add),
         reads=[st_b], writes=[st_b])
    k.op(k.act, lambda h: h.activation(out=st[:, 3:4], in_=st[:, 2:3], func=AF.Ln, scale=1.0 / D, bias=k.eps_ap),
         reads=[st_b], writes=[st_b])
    k.op(k.act, lambda h: h.activation(out=st[:, 4:5], in_=st[:, 3:4], func=AF.Exp, scale=-0.5),
         reads=[st_b], writes=[st_b])
    for hf in range(2):
        k.op(k.dve, lambda h: h.scalar_tensor_tensor(out=tmp[:, hf * 512:(hf + 1) * 512], in0=ps_halves[hf],
                                                     scalar=st[:, 4:5], in1=gpost[:, hf * 512:(hf + 1) * 512],
                                                     op0=ALU.mult, op1=ALU.mult),
             reads=[ps_bufs[hf], st_b, gpost_b], writes=[tmp_b])
    k.op(k.pool, lambda h: h.tensor_tensor(out=out_ap, in0=tmp[:], in1=xres_ap, op=ALU.add),
         reads=[tmp_b, xres_b], writes=[out_b])


def ffn_pass(k, env, l, x_src, x_src_b, x_dst, x_dst_b):
    nc = k.nc
    NS = 2
    TT = NS * 128
    NMT = S // TT
    banks, bank_b = env["banks"], env["bank_b"]
    pT, pT_b = env["pT"], env["pT_b"]
    k.push()
    sbase = nc.sbuf_bytes_remaining

    w1 = k.sb([128, 8, DFF], BF16, "w1_%d" % l)
    w2 = k.sb([128, 32, D], BF16, "w2_%d" % l)
    w1_b = [Buf("w1k%d" % i) for i in range(8)]
    w2_b = [Buf("w2k%d" % i) for i in range(8)]
    w1d = env["ffn_w1"][l].rearrange("(k p) f -> p k f", p=128)
    w2d = env["ffn_w2"][l].rearrange("(k p) f -> p k f", p=128)
    for i in range(8):
        k.dma(k.pool, w1[:, i, :], w1d[:, i, :], writes=[w1_b[i]])
    for i in range(8):
        k.dma(k.pool, w2[:, 4 * i:4 * i + 4, :], w2d[:, 4 * i:4 * i + 4, :], writes=[w2_b[i]])

    gcol = k.sb([128, 8], F32, "f_gcol%d" % l)
    gcol_b = Buf("gcol")
    k.dma(k.sp, gcol[:], env["ln_ffn_pre_col"][l], writes=[gcol_b])
    gpost = k.sb([128, D], F32, "f_gpost%d" % l)
    gpost_b = Buf("gpost")
    k.dma(k.sp, gpost[:], env["ln_ffn_post"][l:l + 1, :].partition_broadcast(128), writes=[gpost_b])

    xin = [k.sb([128, NS, D], F32, "f_xin%d_%d" % (l, i)) for i in range(2)]
    xin_b = [Buf("xin%d" % i) for i in range(2)]
    hT = [k.sb([128, 8, TT], BF16, "f_hT%d_%d" % (l, i)) for i in range(2)]
    hT_b = [Buf("hT%d" % i) for i in range(2)]
    hid = k.sb([128, 32, TT], BF16, "f_hid%d" % l)
    hid_b = [Buf("hid%d" % i) for i in range(4)]
    rl = [k.sb([128, TT], F32, "f_rl%d_%d" % (l, i)) for i in range(2)]
    rl_b = [Buf("rl%d" % i) for i in range(2)]
    tmp = k.sb([128, D], F32, "f_tmp%d" % l)
    tmp_b = Buf("tmp")
    st2 = k.sb([128, 8], F32, "f_st2%d" % l)
    st2_b = Buf("st2")
    nt = NormT(k, NS, env["ident"], env["ident_b"], pT, pT_b, "f_nt%d" % l)

    xs = x_src.rearrange("(m s p) d -> m p s d", s=NS, p=128)
    xd = x_dst.rearrange("(m s p) d -> m p s d", s=NS, p=128)

    def load(mt):
        k.dma(k.sp, xin[mt % 2][:], xs[mt], reads=[x_src_b[mt]], writes=[xin_b[mt % 2]])

    load(0)
    for mt in range(NMT):
        sl = mt % 2
        if mt + 1 < NMT:
            load(mt + 1)
        nt.run(xin[sl], xin_b[sl], gcol, gcol_b, hT[sl], hT_b[sl])
        for fc in range(32):
            pb = fc % 2
            for kk in range(8):
                k.op(k.pe, lambda h: h.matmul(banks[pb][:, 0:TT], lhsT=w1[:, kk, fc * 128:(fc + 1) * 128],
                                              rhs=hT[sl][:, kk, :], start=(kk == 0), stop=(kk == 7)),
                     reads=[w1_b[kk], hT_b[sl]], writes=[bank_b[pb]])
            k.op(k.act, lambda h: h.activation(out=rl[pb][:], in_=banks[pb][:, 0:TT], func=AF.Relu),
                 reads=[bank_b[pb]], writes=[rl_b[pb]])
            k.op(k.pool, lambda h: h.tensor_tensor(out=hid[:, fc, :], in0=rl[pb][:], in1=rl[pb][:], op=ALU.mult),
                 reads=[rl_b[pb]], writes=[hid_b[fc // 8]])
        for s in range(NS):
            pbs = [2 + 2 * (s % 2), 3 + 2 * (s % 2)]
            for hf in range(2):
                for fc in range(32):
                    k.op(k.pe, lambda h: h.matmul(banks[pbs[hf]][:, 0:512], lhsT=hid[:, fc, s * 128:(s + 1) * 128],
                                                  rhs=w2[:, fc, hf * 512:(hf + 1) * 512],
                                                  start=(fc == 0), stop=(fc == 31)),
                         reads=[hid_b[fc // 8], w2_b[fc // 4]], writes=[bank_b[pbs[hf]]])
            post_norm_residual(k, [banks[pbs[0]][:, 0:512], banks[pbs[1]][:, 0:512]],
                               [bank_b[pbs[0]], bank_b[pbs[1]]],
                               xin[sl][:, s, :], xin_b[sl], gpost, gpost_b, st2, st2_b,
                               nt.junk, nt.junk_b, tmp, tmp_b, xin[sl][:, s, :], xin_b[sl])
        k.dma(k.sp, xd[mt], xin[sl][:], reads=[xin_b[sl]], writes=[x_dst_b[mt]])
    env["sbuf_used_ffn"] = sbase - nc.sbuf_bytes_remaining
    k.pop()


def TT(k, e, out, in0, in1, op, R, W):
    return k.op(e, lambda h: h.tensor_tensor(out=out, in0=in0, in1=in1, op=op), reads=R, writes=W)


def TS(k, e, out, in0, s1, op0, R, W, s2=None, op1=None):
    if op1 is None:
        return k.op(e, lambda h: h.tensor_scalar(out=out, in0=in0, scalar1=s1, scalar2=None, op0=op0), reads=R, writes=W)
    return k.op(e, lambda h: h.tensor_scalar(out=out, in0=in0, scalar1=s1, scalar2=s2, op0=op0, op1=op1), reads=R, writes=W)


def STT(k, out, in0, scalar, in1, op0, op1, R, W):
    return k.op(k.dve, lambda h: h.scalar_tensor_tensor(out=out, in0=in0, scalar=scalar, in1=in1, op0=op0, op1=op1),
                reads=R, writes=W)


def ACT(k, out, in_, func, R, W, scale=1.0, bias=None):
    if bias is None:
        return k.op(k.act, lambda h: h.activation(out=out, in_=in_, func=func, scale=scale), reads=R, writes=W)
    return k.op(k.act, lambda h: h.activation(out=out, in_=in_, func=func, scale=scale, bias=bias), reads=R, writes=W)


def CP(k, e, out, in_, R, W):
    return k.op(e, lambda h: h.tensor_copy(out=out, in_=in_), reads=R, writes=W)


def MM(k, out, lhsT, rhs, start, stop, R, W, sgc=False):
    return k.op(k.pe, lambda h: h.matmul(out, lhsT=lhsT, rhs=rhs, start=start, stop=stop, skip_group_check=sgc),
                reads=R, writes=W)


def TR(k, out, in_, ident, R, W):
    return k.op(k.pe, lambda h: h.transpose(out=out, in_=in_, identity=ident), reads=R, writes=W)


def MSET(k, e, ap, val, W):
    return k.op(e, lambda h: h.memset(ap, val), writes=W)


def ASEL(k, out, in_, pattern, op, base, cm, R, W):
    return k.op(k.pool, lambda h: h.affine_select(out=out, in_=in_, pattern=pattern, compare_op=op, fill=0.0,
                                                  base=base, channel_multiplier=cm), reads=R, writes=W)


OFF_GQ, OFF_GK = 0, 128
OFF_DQ, OFF_DQS, OFF_DK, OFF_DKS = 256, 512, 768, 1024
OFF_SU, OFF_RX, OFF_RG, OFF_GLR = 1280, 1536, 1792, 2048
OFF_TMG, OFF_DV = 2112, 2624
NCOL = 2880
TWO_PI = 2.0 * math.pi
CW1 = 6.28125
CW2 = TWO_PI - 6.28125


def make_consts(k, env):
    c = {}
    b = Buf("consts")
    c["b"] = b
    one = k.sb([128, 1], F32, "c_one"); MSET(k, k.pool, one[:], 1.0, [b]); c["one"] = one
    sgn = k.sb([128, 1], F32, "c_sgn"); MSET(k, k.pool, sgn[0:64, :], 1.0, [b]); MSET(k, k.pool, sgn[64:128, :], -1.0, [b])
    c["sgn"] = sgn
    tri = k.sb([128, 128], BF16, "c_tri"); MSET(k, k.pool, tri[:], 1.0, [b])
    ASEL(k, tri[:], tri[:], [[1, 128]], ALU.is_ge, 0, -1, [b], [b]); c["tri"] = tri
    tri64 = k.sb([64, 64], F32, "c_tri64"); MSET(k, k.pool, tri64[:], 1.0, [b])
    ASEL(k, tri64[:], tri64[:], [[1, 64]], ALU.is_ge, 0, -1, [b], [b]); c["tri64"] = tri64
    hm = k.sb([128, 4], F32, "c_hm"); MSET(k, k.pool, hm[:], 1.0, [b])
    ASEL(k, hm[:], hm[:], [[-32, 4]], ALU.is_ge, 0, 1, [b], [b])
    ASEL(k, hm[:], hm[:], [[32, 4]], ALU.is_ge, 31, -1, [b], [b]); c["hm"] = hm
    rm = k.sb([128, 8], F32, "c_rm16"); MSET(k, k.pool, rm[:], 1.0, [b])
    ASEL(k, rm[:], rm[:], [[-16, 8]], ALU.is_ge, 0, 1, [b], [b])
    ASEL(k, rm[:], rm[:], [[16, 8]], ALU.is_ge, 15, -1, [b], [b]); c["rm16"] = rm
    jf = k.sb([128, 128], F32, "c_jf"); j2 = k.sb([128, 128], F32, "c_j2")
    MSET(k, k.pool, jf[:], 1.0, [b]); MSET(k, k.pool, j2[:], 1.0, [b])
    ASEL(k, jf[:], jf[:], [[1, 128]], ALU.is_equal, -64, -1, [b], [b])
    ASEL(k, j2[:], j2[:], [[1, 128]], ALU.is_equal, 64, -1, [b], [b])
    TT(k, k.pool, jf[:], jf[:], j2[:], ALU.add, [b], [b]); c["jf"] = jf
    rs = k.sb([128, 256], F32, "c_reset"); MSET(k, k.pool, rs[:], 1.0, [b])
    for i in range(4):
        MSET(k, k.pool, rs[:, 64 * i:64 * i + 1], 0.0, [b])
    c["reset"] = rs
    return c


def mixer_prep(k, env, l, C):
    nc = k.nc
    P = {}
    cb = C["b"]
    lam_init = 0.8 - 0.6 * math.exp(-0.3 * l)
    pb = Buf("prep")
    P["b"] = pb
    banks, bank_b = env["banks"], env["bank_b"]
    pT1, pT1_b = env["pT1"], env["pT1_b"]

    def sbt(shape, dt, nm):
        return k.sb(shape, dt, "p%d_%s" % (l, nm))

    wgate = sbt([16, 128], BF16, "wgate"); negb = sbt([128, 1], F32, "negb"); gnorm = sbt([128, 64], F32, "gnorm")
    lw = sbt([128, 8], F32, "lw"); dnorm = sbt([128, 64], F32, "dnorm")
    cw = sbt([128, 2, 4], F32, "cw"); lv = sbt([128, 5, 2], F32, "lv"); bglu = sbt([128, 2], F32, "bglu")
    wa = sbt([128, 2, 128], BF16, "wa"); wx = sbt([128, 2, 128], BF16, "wx"); lsp = sbt([128, 4, 2], F32, "lsp")
    wglu = sbt([128, 2, 256], BF16, "wglu"); rho = sbt([128, 16], F32, "rho")
    bpad = sbt([128, 2, 16, 128], BF16, "bpad"); cpad = sbt([128, 2, 16, 128], BF16, "cpad")
    wl = sbt([128, 9, 2, 16], F32, "wl")
    tabc = sbt([128, 16, 64], F32, "tabc"); tabs = sbt([128, 16, 64], F32, "tabs")
    rhot = sbt([128, 16, 64], F32, "rhot")
    k.push()

    def ld(t_ap, src, q=None):
        k.dma(q or k.sp, t_ap, src, writes=[pb])

    ld(wgate[:], env["gla_w_gate"][l], k.pool)
    ld(negb[:], env["gla_b_gate_col"][l])
    TS(k, k.dve, negb[:], negb[:], -1.0, ALU.mult, [pb], [pb])
    ld(gnorm[:], env["gla_norm"][l:l + 1, :].partition_broadcast(128))
    P.update(wgate=wgate, negb=negb, gnorm=gnorm)
    lq = sbt([128, 4, 32], F32, "lq")
    for i, nm in enumerate(("diff_lq1", "diff_lk1", "diff_lq2", "diff_lk2")):
        ld(lq[:, i, :], env[nm][l:l + 1, :].partition_broadcast(128))
    pr = sbt([128, 2, 32], F32, "lpr")
    TT(k, k.dve, pr[:, 0, :], lq[:, 0, :], lq[:, 1, :], ALU.mult, [pb], [pb])
    TT(k, k.dve, pr[:, 1, :], lq[:, 2, :], lq[:, 3, :], ALU.mult, [pb], [pb])
    k.op(k.dve, lambda h: h.reduce_sum(out=lw[:, 0:2], in_=pr[:], axis=AX.X), reads=[pb], writes=[pb])
    ACT(k, lw[:, 2:4], lw[:, 0:2], AF.Exp, [pb], [pb])
    TT(k, k.dve, lw[:, 4:5], lw[:, 2:3], lw[:, 3:4], ALU.subtract, [pb], [pb])
    TS(k, k.dve, lw[:, 5:6], lw[:, 4:5], lam_init, ALU.add, [pb], [pb], s2=-1.0, op1=ALU.mult)
    ld(dnorm[:], env["diff_norm"][l:l + 1, :].partition_broadcast(128))
    TS(k, k.dve, dnorm[:], dnorm[:], 1.0 - lam_init, ALU.mult, [pb], [pb])
    P.update(nlam=lw[:, 5:6], dnorm=dnorm)
    ld(cw[:], env["lru_conv_w_col"][l])
    for i, nm in enumerate(("lru_conv_b_col", "lru_b_a_col", "lru_b_x_col", "lru_lambda_col", "s5_d_col")):
        ld(lv[:, i, :], env[nm][l])
    ld(bglu[:], env["s5_b_glu_col"][l])
    MSET(k, k.pool, wa[:], 0.0, [pb]); MSET(k, k.pool, wx[:], 0.0, [pb])
    for n in range(4):
        r0 = (n % 2) * 64
        ld(wa[r0:r0 + 64, n // 2, r0:r0 + 64], env["lru_w_a"][l, n], k.pool)
        ld(wx[r0:r0 + 64, n // 2, r0:r0 + 64], env["lru_w_x"][l, n], k.pool)
    ACT(k, lsp[:, 0, :], lv[:, 3, :], AF.Exp, [pb], [pb], scale=-1.0)
    ACT(k, lsp[:, 1, :], lsp[:, 0, :], AF.Ln, [pb, cb], [pb], bias=C["one"][:])
    TS(k, k.dve, lsp[:, 2, :], lsp[:, 1, :], -8.0, ALU.mult, [pb], [pb])
    TS(k, k.dve, lsp[:, 3, :], lsp[:, 1, :], -16.0, ALU.mult, [pb], [pb])
    ld(wglu[:], env["s5_w_glu"][l].rearrange("(k p) n -> p k n", p=128), k.pool)
    P.update(cw=cw, lv=lv, bglu=bglu, wa=wa, wx=wx, lsp=lsp, wglu=wglu)
    lr = sbt([128, 16], F32, "lr"); li = sbt([128, 16], F32, "li"); ls = sbt([128, 16], F32, "ls")
    for hlf in range(2):
        ld(lr[hlf * 64:(hlf + 1) * 64, :], env["s5_a_re_T"][l])
        ld(li[hlf * 64:(hlf + 1) * 64, :], env["s5_a_im_T"][l])
    ld(ls[:], env["s5_log_step"][l:l + 1, :].partition_broadcast(128))
    bri = sbt([128, 2, 256], F32, "bri"); cri = sbt([128, 2, 256], F32, "cri")
    for hlf in range(2):
        sl_ = slice(hlf * 64, (hlf + 1) * 64)
        ld(bri[sl_, 0, :], env["s5_b_re_p"][l]); ld(bri[sl_, 1, :], env["s5_b_im_p"][l])
        ld(cri[sl_, 0, :], env["s5_c_re_p"][l]); ld(cri[sl_, 1, :], env["s5_c_im_p"][l])
    w = sbt([128, 24, 16], F32, "s5w")
    W_ = lambda i: w[:, i, :]
    R, Wr = [pb], [pb]
    ACT(k, W_(0), ls[:], AF.Exp, R, Wr)
    TT(k, k.dve, W_(1), lr[:], W_(0), ALU.mult, R, Wr)
    TT(k, k.dve, W_(2), li[:], W_(0), ALU.mult, R, Wr)
    ACT(k, rho[:], W_(1), AF.Exp, R, Wr)
    ki = sbt([128, 16], I32, "ki")

    def sin_reduced(dst, ang):
        TS(k, k.dve, W_(3), ang, 1.0 / TWO_PI, ALU.mult, R, Wr)
        CP(k, k.dve, ki[:], W_(3), R, Wr)
        CP(k, k.dve, W_(4), ki[:], R, Wr)
        STT(k, W_(5), W_(4), -CW1, ang, ALU.mult, ALU.add, R, Wr)
        STT(k, W_(5), W_(4), -CW2, W_(5), ALU.mult, ALU.add, R, Wr)
        TS(k, k.dve, W_(5), W_(5), 3.1415925, ALU.min, R, Wr, s2=-3.1415925, op1=ALU.max)
        ACT(k, dst, W_(5), AF.Sin, R, Wr)

    sin_reduced(W_(6), W_(2))
    TS(k, k.dve, W_(7), W_(2), math.pi / 2.0, ALU.add, R, Wr)
    sin_reduced(W_(8), W_(7))
    TT(k, k.dve, W_(9), W_(6), W_(6), ALU.mult, R, Wr)
    TT(k, k.dve, W_(10), W_(8), W_(8), ALU.mult, R, Wr)
    TT(k, k.dve, W_(9), W_(9), W_(10), ALU.add, R, Wr)
    TS(k, k.dve, W_(9), W_(9), -0.5, ALU.mult, R, Wr, s2=1.5, op1=ALU.add)
    TT(k, k.dve, W_(6), W_(6), W_(9), ALU.mult, R, Wr)
    TT(k, k.dve, W_(8), W_(8), W_(9), ALU.mult, R, Wr)
    TT(k, k.dve, W_(11), rho[:], W_(8), ALU.mult, R, Wr)
    TT(k, k.dve, W_(12), rho[:], W_(6), ALU.mult, R, Wr)
    TT(k, k.dve, W_(13), lr[:], lr[:], ALU.mult, R, Wr)
    TT(k, k.dve, W_(14), li[:], li[:], ALU.mult, R, Wr)
    TT(k, k.dve, W_(13), W_(13), W_(14), ALU.add, R, Wr)
    k.op(k.dve, lambda h: h.reciprocal(out=W_(13), in_=W_(13)), reads=R, writes=Wr)
    TS(k, k.dve, W_(11), W_(11), -1.0, ALU.add, R, Wr)
    TT(k, k.dve, W_(14), W_(11), lr[:], ALU.mult, R, Wr)
    TT(k, k.dve, W_(15), W_(12), li[:], ALU.mult, R, Wr)
    TT(k, k.dve, W_(14), W_(14), W_(15), ALU.add, R, Wr)
    TT(k, k.dve, W_(14), W_(14), W_(13), ALU.mult, R, Wr)
    TT(k, k.dve, W_(15), W_(12), lr[:], ALU.mult, R, Wr)
    TT(k, k.dve, W_(16), W_(11), li[:], ALU.mult, R, Wr)
    TT(k, k.dve, W_(15), W_(15), W_(16), ALU.subtract, R, Wr)
    TT(k, k.dve, W_(15), W_(15), W_(13), ALU.mult, R, Wr)
    crb = W_(14).unsqueeze(2).to_broadcast([128, 16, 16])
    cib = W_(15).unsqueeze(2).to_broadcast([128, 16, 16])
    bre = bri[:, 0, :].rearrange("p (g h) -> p g h", g=16)
    bim = bri[:, 1, :].rearrange("p (g h) -> p g h", g=16)
    bb = sbt([128, 4, 256], F32, "bb")
    B3 = lambda i: bb[:, i, :].rearrange("p (g h) -> p g h", g=16)
    TT(k, k.dve, B3(0), bre, crb, ALU.mult, R, Wr)
    TT(k, k.dve, B3(1), bim, cib, ALU.mult, R, Wr)
    TT(k, k.dve, B3(0), B3(0), B3(1), ALU.subtract, R, Wr)
    TT(k, k.dve, B3(2), bim, crb, ALU.mult, R, Wr)
    TT(k, k.dve, B3(3), bre, cib, ALU.mult, R, Wr)
    TT(k, k.dve, B3(2), B3(2), B3(3), ALU.add, R, Wr)
    bst = sbt([128, 2, 256], BF16, "bst")
    CP(k, k.dve, bst[0:64, 0, :], bb[0:64, 0, :], R, Wr)
    CP(k, k.dve, bst[64:128, 0, :], bb[64:128, 2, :], R, Wr)
    CP(k, k.dve, bst[0:64, 1, :], bb[0:64, 2, :], R, Wr)
    CP(k, k.dve, bst[64:128, 1, :], bb[64:128, 0, :], R, Wr)
    for v in range(2):
        for ch in range(2):
            TR(k, pT1[:, 0:128], bst[:, v, ch * 128:(ch + 1) * 128], env["ident"][:], [pb, env["ident_b"]], [pT1_b])
            for gg in range(8):
                TS(k, k.dve, bpad[:, v, ch * 8 + gg, :], pT1[:, 0:128], C["rm16"][:, gg:gg + 1], ALU.mult,
                   [pT1_b, cb], [pb])
    cst = sbt([128, 2, 256], F32, "cst")
    CP(k, k.dve, cst[0:64, 0, :], cri[0:64, 0, :], R, Wr)
    TS(k, k.dve, cst[64:128, 0, :], cri[64:128, 1, :], -1.0, ALU.mult, R, Wr)
    TS(k, k.dve, cst[0:64, 1, :], cri[0:64, 1, :], -1.0, ALU.mult, R, Wr)
    CP(k, k.dve, cst[64:128, 1, :], cri[64:128, 0, :], R, Wr)
    MSET(k, k.pool, cpad[:], 0.0, Wr)
    for v in range(2):
        for g in range(16):
            CP(k, k.pool, cpad[:, v, g, (g % 8) * 16:(g % 8) * 16 + 16], cst[:, v, g * 16:(g + 1) * 16], R, Wr)
    CP(k, k.dve, wl[:, 0, 0, :], W_(8), R, Wr)
    TS(k, k.dve, wl[:, 0, 1, :], W_(6), C["sgn"][:, 0:1], ALU.mult, R + [cb], Wr)
    MSET(k, k.pool, tabc[:, :, 0:1], 1.0, Wr); MSET(k, k.pool, tabs[:, :, 0:1], 0.0, Wr)
    t1 = sbt([128, 16, 32], F32, "dbl1"); t2 = sbt([128, 16, 32], F32, "dbl2")
    n = 1
    for lev in range(6):
        wc = wl[:, lev, 0, :].unsqueeze(2).to_broadcast([128, 16, n])
        ws = wl[:, lev, 1, :].unsqueeze(2).to_broadcast([128, 16, n])
        TT(k, k.pool, t1[:, :, 0:n], tabs[:, :, 0:n], ws, ALU.mult, R, Wr)
        TT(k, k.pool, t2[:, :, 0:n], tabc[:, :, 0:n], wc, ALU.mult, R, Wr)
        TT(k, k.pool, tabc[:, :, n:2 * n], t2[:, :, 0:n], t1[:, :, 0:n], ALU.subtract, R, Wr)
        TT(k, k.pool, t1[:, :, 0:n], tabc[:, :, 0:n], ws, ALU.mult, R, Wr)
        TT(k, k.pool, t2[:, :, 0:n], tabs[:, :, 0:n], wc, ALU.mult, R, Wr)
        TT(k, k.pool, tabs[:, :, n:2 * n], t1[:, :, 0:n], t2[:, :, 0:n], ALU.add, R, Wr)
        TT(k, k.dve, W_(17), wl[:, lev, 0, :], wl[:, lev, 0, :], ALU.mult, R, Wr)
        TT(k, k.dve, W_(18), wl[:, lev, 1, :], wl[:, lev, 1, :], ALU.mult, R, Wr)
        TT(k, k.dve, wl[:, lev + 1, 0, :], W_(17), W_(18), ALU.subtract, R, Wr)
        STT(k, wl[:, lev + 1, 1, :], wl[:, lev, 0, :], 2.0, wl[:, lev, 1, :], ALU.mult, ALU.mult, R, Wr)
        n *= 2
    P.update(rho=rho, bpad=bpad, cpad=cpad, tabc=tabc, tabs=tabs, c64=wl[:, 6, 0, :], s64=wl[:, 6, 1, :], rhot=rhot)
    CP(k, k.pool, rhot[:], rho[:].unsqueeze(2).to_broadcast([128, 16, 64]), R, Wr)
    MSET(k, k.pool, rhot[:, :, 0:1], 0.0, Wr)
    k.pop()
    return P


def mixer_pass(k, env, l, x_src, x_src_b, x_dst, x_dst_b, C, dbg=None):
    nc = k.nc
    NS, TT_, NMT = 2, 256, 16
    banks, bank_b = env["banks"], env["bank_b"]
    pT, pT_b, pT1, pT1_b = env["pT"], env["pT_b"], env["pT1"], env["pT1_b"]
    ident, ident_b = env["ident"], env["ident_b"]
    cb = C["b"]
    hm, tri, tri64 = C["hm"], C["tri"], C["tri64"]
    k.push()
    sbase = nc.sbuf_bytes_remaining
    nm = lambda s_: "m%d_%s" % (l, s_)

    win = k.sb([128, 8, NCOL], BF16, nm("win")); win_b = [Buf("win%d" % i) for i in range(8)]
    wind = env["w_in_ext"][l].rearrange("(k p) n -> p k n", p=128)
    for i in range(8):
        k.dma(k.pool, win[:, i, :], wind[:, i, :], writes=[win_b[i]])
    wout = k.sb([128, 8, D], BF16, nm("wout")); wout_b = Buf("wout")
    k.dma(k.pool, wout[:], env["w_out"][l].rearrange("(k p) n -> p k n", p=128), writes=[wout_b])
    gcol = k.sb([128, 8], F32, nm("gcol")); gcol_b = Buf("gcol")
    k.dma(k.sp, gcol[:], env["ln_mix_pre_col"][l], writes=[gcol_b])
    gpost = k.sb([128, D], F32, nm("gpost")); gpost_b = Buf("gpost")
    k.dma(k.sp, gpost[:], env["ln_mix_post"][l:l + 1, :].partition_broadcast(128), writes=[gpost_b])
    kc = k.sb([128, 2, S], BF16, nm("kc")); kc_b = [Buf("kc%d" % i) for i in range(NMT)]
    vc = k.sb([128, 32, 4, 65], BF16, nm("vc")); vc_b = [Buf("vc%d" % i) for i in range(NMT)]
    vc1_b = Buf("vc_ones")
    MSET(k, k.pool, vc[:, :, :, 64:65], 1.0, [vc1_b])
    P = mixer_prep(k, env, l, C) if "prep" not in env.get("skip", ()) else {"b": Buf("prep")}
    pb = P["b"]

    def T(shape, dt, s_):
        return k.sb(shape, dt, nm(s_)), Buf(s_)

    xin, xin_b = T([128, NS, D], F32, "xin")
    hT, hT_b = T([128, 8, TT_], BF16, "hT")
    mixT, mixT_b = T([128, 8, TT_], BF16, "mixT")
    nt = NormT(k, NS, ident, ident_b, pT, pT_b, nm("nt"))
    rc, rc_b = T([128, TT_], F32, "rc"); rs, rs_b = T([128, TT_], F32, "rs")
    tmpo, tmpo_b = T([128, D], F32, "tmpo")
    st2, st2_b = T([128, 8], F32, "st2")
    f1, f1_b = T([128, TT_], F32, "f1"); f2, f2_b = T([128, TT_], F32, "f2"); f3, f3_b = T([128, TT_], F32, "f3")
    f4, f4_b = T([128, TT_], F32, "f4"); f5, f5_b = T([128, TT_], F32, "f5")
    gq, gq_b = T([128, TT_], F32, "gq"); gk, gk_b = T([128, TT_], F32, "gk")
    glr, glr_b = T([16, TT_], BF16, "glr")
    qm, qm_b = T([128, 2, 4, TT_], BF16, "qm")
    uf, uf_b = T([128, 2, TT_], F32, "uf"); ub, ub_b = T([128, 2, TT_], BF16, "ub")
    rxh, rxh_b = T([128, 2, 3 + TT_], F32, "rxh"); rg, rg_b = T([128, 2, TT_], F32, "rg")
    gvc, gvc_b = T([64, 4, 256], BF16, "gvc"); ogc, ogc_b = T([64, 4, 256], F32, "ogc")
    qdm, qdm_b = T([128, 4, TT_], BF16, "qdm"); kd, kd_b = T([128, TT_], BF16, "kd"); ks, ks_b = T([128, TT_], BF16, "ks")
    dec, dec_b = T([128, 4], F32, "dec")
    Sst, Sst_b = T([128, 64], F32, "Sst"); Sbf, Sbf_b = T([128, 2, 64], BF16, "Sbf")
    kst, kst_b = T([64, 128], BF16, "kst"); AT, AT_b = T([64, 4, 64], BF16, "AT")
    kvr, kvr_b = T([128, 64], F32, "kvr")
    osb, osb_b = T([64, 256], F32, "osb"); sil, sil_b = T([64, 256], F32, "sil")
    gst, gst_b = T([64, 12], F32, "gst"); otk, otk_b = T([64, 256], BF16, "otk")
    pt = [T([128, 2 * TT_], BF16, "pt%d" % i) for i in range(2)]
    od, od_b = T([128, 2, 256], F32, "od"); odb, odb_b = T([128, 2, 256], BF16, "odb")
    dst, dst_b = T([128, 32], F32, "dst"); dt1, dt1_b = T([128, 64], F32, "dt1")
    zlast, zlast_b = T([128, 16], F32, "zlast"); zin, zin_b = T([128, 16], F32, "zin"); zt, zt_b = T([128, 16], F32, "zt")
    m1, m1_b = T([128, 512], F32, "m1"); m2, m2_b = T([128, 512], F32, "m2"); zz, zz_b = T([128, 512], F32, "zz")
    zc, zc_b = T([128, 512], BF16, "zc"); zs, zs_b = T([128, 512], BF16, "zs")
    yb, yb_b = T([128, 2, TT_], BF16, "yb")
    xcb, xcb_b = T([128, TT_], BF16, "xcb"); hst, hst_b = T([128, 2], F32, "hst")
    odq, odq_b = tmpo[:, 0:512].rearrange("p (s f) -> p s f", s=2), tmpo_b
    kvm, kvm_b = tmpo[:, 512:768], tmpo_b
    osq, osq_b = tmpo[0:64, 768:1024], tmpo_b
    MSET(k, k.pool, Sst[:], 0.0, [Sst_b]); MSET(k, k.pool, Sbf[:], 0.0, [Sbf_b])
    MSET(k, k.pool, zlast[:], 0.0, [zlast_b]); MSET(k, k.pool, hst[:], 0.0, [hst_b]); MSET(k, k.pool, rxh[:, :, 0:3], 0.0, [rxh_b])
    env["sbuf_used_mix"] = sbase - nc.sbuf_bytes_remaining

    xs = x_src.rearrange("(m s p) d -> m p s d", s=NS, p=128)
    xd = x_dst.rearrange("(m s p) d -> m p s d", s=NS, p=128)
    B = lambda i: banks[i]
    Bb = lambda i: bank_b[i]
    QS = 32.0 ** -0.5
    sbi = [0]

    def inproj_fm(off, width, bank):
        for kk in range(8):
            MM(k, B(bank)[0:width, 0:TT_], win[:, kk, off:off + width], hT[:, kk, :], kk == 0, kk == 7,
               [win_b[kk], hT_b], [Bb(bank)])

    for mt in range(env.get("nmt", NMT)):
        t0 = mt * TT_
        k.dma(k.sp, xin[:], xs[mt], reads=[x_src_b[mt]], writes=[xin_b])
        k.dma(k.sp, rc[:], env["rope_cos"][:, t0:t0 + TT_], writes=[rc_b])
        k.dma(k.sp, rs[:], env["rope_ssin"][:, t0:t0 + TT_], writes=[rs_b])
        if "norm" not in env.get("skip", ()):
            nt.run(xin, xin_b, gcol, gcol_b, hT, hT_b)

        def _sec_inproj():
            ipon = lambda i: env.get('ip_only') is None or i in env['ip_only']
            if ipon(0): inproj_fm(OFF_GQ, 128, 0); ACT(k, gq[:], B(0)[:, 0:TT_], AF.Copy, [Bb(0)], [gq_b])
            if ipon(1): inproj_fm(OFF_GK, 128, 1); ACT(k, gk[:], B(1)[:, 0:TT_], AF.Copy, [Bb(1)], [gk_b])
            for j in (range(2) if ipon(2) else ()):
                inproj_fm(OFF_DQ + 128 * j, 128, 0); inproj_fm(OFF_DQS + 128 * j, 128, 1)
                TT(k, k.dve, f1[:], B(0)[:, 0:TT_], rc[:], ALU.mult, [Bb(0), rc_b], [f1_b])
                TT(k, k.dve, f2[:], B(1)[:, 0:TT_], rs[:], ALU.mult, [Bb(1), rs_b], [f2_b])
                TT(k, k.pool, f3[:], f1[:], f2[:], ALU.add, [f1_b, f2_b], [f3_b])
                for i in range(4):
                    TS(k, k.pool if i % 2 else k.dve, qm[:, j, i, :], f3[:], hm[:, i:i + 1], ALU.mult, [f3_b, cb], [qm_b])
            for j in (range(2) if ipon(3) else ()):
                inproj_fm(OFF_DK + 128 * j, 128, 0); inproj_fm(OFF_DKS + 128 * j, 128, 1)
                TT(k, k.dve, f1[:], B(0)[:, 0:TT_], rc[:], ALU.mult, [Bb(0), rc_b], [f1_b])
                TT(k, k.dve, f2[:], B(1)[:, 0:TT_], rs[:], ALU.mult, [Bb(1), rs_b], [f2_b])
                TT(k, k.pool, kc[:, j, t0:t0 + TT_], f1[:], f2[:], ALU.add, [f1_b, f2_b], [kc_b[mt]])
            for j in (range(2) if ipon(4) else ()):
                inproj_fm(OFF_SU + 128 * j, 128, j)
                ACT(k, uf[:, j, :], B(j)[:, 0:TT_], AF.Copy, [Bb(j)], [uf_b])
                CP(k, k.pool, ub[:, j, :], uf[:, j, :], [uf_b], [ub_b])
            for j in (range(2) if ipon(5) else ()):
                inproj_fm(OFF_RX + 128 * j, 128, j)
                ACT(k, rxh[:, j, 3:3 + TT_], B(j)[:, 0:TT_], AF.Copy, [Bb(j)], [rxh_b])
            for j in (range(2) if ipon(6) else ()):
                inproj_fm(OFF_RG + 128 * j, 128, j)
                ACT(k, rg[:, j, :], B(j)[:, 0:TT_], AF.Copy, [Bb(j)], [rg_b])
            if ipon(7): inproj_fm(OFF_GLR, 16, 0); ACT(k, glr[:], B(0)[0:16, 0:TT_], AF.Copy, [Bb(0)], [glr_b])
            for n in (range(4) if ipon(8) else ()):
                bk = n % 2
                for kk in range(8):
                    MM(k, B(bk)[0:64, 0:512], hT[:, kk, n * 64:(n + 1) * 64], win[:, kk, OFF_TMG:OFF_TMG + 512],
                       kk == 0, kk == 7, [win_b[kk], hT_b], [Bb(bk)])
                ACT(k, gvc[:, n, :], B(bk)[0:64, 0:256], AF.Copy, [Bb(bk)], [gvc_b])
                CP(k, k.dve, ogc[:, n, :], B(bk)[0:64, 256:512], [Bb(bk)], [ogc_b])
            for s in (range(NS) if ipon(9) else ()):
                bk = s % 2
                for kk in range(8):
                    MM(k, B(bk)[:, 0:256], hT[:, kk, s * 128:(s + 1) * 128], win[:, kk, OFF_DV:OFF_DV + 256],
                       kk == 0, kk == 7, [win_b[kk], hT_b], [Bb(bk)])
                CP(k, k.dve, vc[:, 2 * mt + s, :, 0:64], B(bk)[:, 0:256].rearrange("p (h v) -> p h v", h=4),
                   [Bb(bk)], [vc_b[mt]])

        if "inproj" not in env.get("skip", ()):
            _sec_inproj()
        def _sec_gla():
            MM(k, B(2)[:, 0:TT_], P["wgate"][:], glr[:], True, True, [pb, glr_b], [Bb(2)])
            ACT(k, f1[:], B(2)[:, 0:TT_], AF.Exp, [Bb(2), pb], [f1_b], scale=-1.0, bias=P["negb"][:])
            ACT(k, f2[:], f1[:], AF.Ln, [f1_b, cb], [f2_b], bias=C["one"][:])
            k.op(k.dve, lambda h: h.tensor_tensor_scan(out=f3[:], data0=C["reset"][:], data1=f2[:], initial=0.0,
                                                       op0=ALU.mult, op1=ALU.add), reads=[f2_b, cb], writes=[f3_b])
            f3v = f3[:].rearrange("p (n c) -> p n c", n=4)
            ACT(k, f1[:], f3[:], AF.Exp, [f3_b], [f1_b], scale=-1.0 / 16.0)
            ACT(k, f2[:], f3[:], AF.Exp, [f3_b], [f2_b], scale=1.0 / 16.0)
            TT(k, k.dve, f4[:].rearrange("p (n c) -> p n c", n=4), f3v, f3v[:, :, 63:64].to_broadcast([128, 4, 64]),
               ALU.subtract, [f3_b], [f4_b])
            ACT(k, f5[:], f4[:], AF.Exp, [f4_b], [f5_b], scale=1.0 / 16.0)
            ACT(k, dec[:], f3v[:, :, 63], AF.Exp, [f3_b], [dec_b], scale=-1.0 / 16.0)
            STT(k, f4[:], gq[:], 32.0 ** -0.5, f1[:], ALU.mult, ALU.mult, [gq_b, f1_b], [f4_b])
            for h_ in range(4):
                TS(k, k.pool if h_ % 2 else k.dve, qdm[:, h_, :], f4[:], hm[:, h_:h_ + 1], ALU.mult, [f4_b, cb], [qdm_b])
            TT(k, k.dve, kd[:], gk[:], f2[:], ALU.mult, [gk_b, f2_b], [kd_b])
            TT(k, k.pool, ks[:], gk[:], f5[:], ALU.mult, [gk_b, f5_b], [ks_b])
            for n in range(4):
                c0 = n * 64
                gn = mt * 4 + n
                sp_, sn_ = gn % 2, (gn + 1) % 2
                TR(k, pT1[0:64, 0:128], ks[:, c0:c0 + 64], ident[:], [ks_b, ident_b], [pT1_b])
                CP(k, k.dve, kst[:], pT1[0:64, 0:128], [pT1_b], [kst_b])
                for h_ in range(4):
                    MM(k, B(2)[0:64, h_ * 64:(h_ + 1) * 64], kd[:, c0:c0 + 64], qdm[:, h_, c0:c0 + 64], True, True,
                       [kd_b, qdm_b], [Bb(2)])
                TT(k, k.dve, AT[:], B(2)[0:64, 0:256].rearrange("p (h c) -> p h c", h=4),
                   tri64[:].unsqueeze(1).to_broadcast([64, 4, 64]), ALU.mult, [Bb(2), cb], [AT_b])
                MM(k, B(3)[:, 0:256], kst[:], gvc[:, n, :], True, True, [kst_b, gvc_b], [Bb(3)])
                for h_ in range(4):
                    MM(k, B(4)[0:64, h_ * 64:(h_ + 1) * 64], AT[:, h_, :], gvc[:, n, h_ * 64:(h_ + 1) * 64], True, False,
                       [AT_b, gvc_b], [Bb(4)])
                    MM(k, B(4)[0:64, h_ * 64:(h_ + 1) * 64], qdm[:, h_, c0:c0 + 64], Sbf[:, sp_, :], False, True,
                       [qdm_b, Sbf_b], [Bb(4)])
                TT(k, k.dve, kvm.rearrange("p (h v) -> p h v", h=4), B(3)[:, 0:256].rearrange("p (h v) -> p h v", h=4),
                   hm[:].unsqueeze(2).to_broadcast([128, 4, 64]), ALU.mult, [Bb(3), cb], [kvm_b])
                k.op(k.dve, lambda h: h.reduce_sum(out=kvr[:], in_=kvm.rearrange("p (h v) -> p v h", h=4), axis=AX.X),
                     reads=[kvm_b], writes=[kvr_b])
                STT(k, Sst[:], Sst[:], dec[:, n:n + 1], kvr[:], ALU.mult, ALU.add, [Sst_b, dec_b, kvr_b], [Sst_b])
                CP(k, k.pool, Sbf[:, sn_, :], Sst[:], [Sst_b], [Sbf_b])
                ACT(k, osb[:], B(4)[0:64, 0:256], AF.Copy, [Bb(4)], [osb_b])
                TT(k, k.pool, osq, osb[:], osb[:], ALU.mult, [osb_b], [osq_b])
                k.op(k.dve, lambda h: h.reduce_sum(out=gst[:, 0:4], in_=osq.rearrange("p (h v) -> p h v", h=4), axis=AX.X),
                     reads=[osq_b], writes=[gst_b])
                ACT(k, gst[:, 4:8], gst[:, 0:4], AF.Ln, [gst_b], [gst_b], scale=1.0 / 64.0, bias=k.eps_ap[0:64, :])
                ACT(k, gst[:, 8:12], gst[:, 4:8], AF.Exp, [gst_b], [gst_b], scale=-0.5)
                ACT(k, sil[:], ogc[:, n, :], AF.Silu, [ogc_b], [sil_b])
                TT(k, k.dve, osb[:].rearrange("p (h v) -> p h v", h=4), osb[:].rearrange("p (h v) -> p h v", h=4),
                   gst[:, 8:12].unsqueeze(2).to_broadcast([64, 4, 64]), ALU.mult, [osb_b, gst_b], [osb_b])
                TT(k, k.pool, sil[:].rearrange("p (h v) -> p h v", h=4), sil[:].rearrange("p (h v) -> p h v", h=4),
                   P["gnorm"][0:64, :].unsqueeze(1).to_broadcast([64, 4, 64]), ALU.mult, [sil_b, pb], [sil_b])
                TT(k, k.dve, otk[:], osb[:], sil[:], ALU.mult, [osb_b, sil_b], [otk_b])
                for j in range(2):
                    TR(k, pT1[:, 128 + j * 64:128 + (j + 1) * 64], otk[:, j * 128:(j + 1) * 128], ident[0:64, 0:64],
                       [otk_b, ident_b], [pT1_b])
                CP(k, k.dve, mixT[:, 0:2, c0:c0 + 64], pT1[:, 128:256].rearrange("p (j c) -> p j c", j=2), [pT1_b], [mixT_b])

        if "gla" not in env.get("skip", ()):
            _sec_gla()
        def _sec_diff():
            for h_ in range(4):
                j = h_ // 2
                for c in range(2):
                    qsl = qm[:, j, (h_ % 2) * 2 + c, :]
                    def emit_s(i):
                        sb_ = sbi[0] % 2
                        sbi[0] += 1
                        bk, ptt, ptb = B(2 + sb_), pt[sb_][0], pt[sb_][1]
                        diag = (i == mt)
                        for u in range(2):
                            kt = 2 * i + u
                            qlo = 128 if (diag and u == 1) else 0
                            MM(k, bk[:, u * TT_ + qlo:(u + 1) * TT_], kc[:, j, kt * 128:(kt + 1) * 128], qsl[:, qlo:TT_],
                               True, True, [kc_b[i], qm_b], [Bb(2 + sb_)])
                        if not diag:
                            ACT(k, ptt[:, 0:2 * TT_], bk[:, 0:2 * TT_], AF.Exp, [Bb(2 + sb_)], [ptb], scale=QS)
                        else:
                            ACT(k, ptt[:, 0:TT_], bk[:, 0:TT_], AF.Exp, [Bb(2 + sb_)], [ptb], scale=QS)
                            ACT(k, ptt[:, TT_ + 128:2 * TT_], bk[:, TT_ + 128:2 * TT_], AF.Exp, [Bb(2 + sb_)], [ptb], scale=QS)
                            TT(k, k.pool, ptt[:, 0:128], ptt[:, 0:128], tri[:], ALU.mult, [ptb, cb], [ptb])
                            TT(k, k.pool, ptt[:, TT_ + 128:2 * TT_], ptt[:, TT_ + 128:2 * TT_], tri[:], ALU.mult, [ptb, cb], [ptb])
                        return sb_

                    def emit_pv(i, sb_):
                        ptt, ptb = pt[sb_][0], pt[sb_][1]
                        for u in range(2):
                            kt = 2 * i + u
                            for s in range(NS):
                                if kt > 2 * mt + s:
                                    continue
                                col = (c * 2 + s) * 65
                                MM(k, B(4)[:, col:col + 65], ptt[:, u * TT_ + s * 128:u * TT_ + (s + 1) * 128],
                                   vc[:, kt, h_, :], kt == 0 and c == 0 and s == 0, kt == 2 * mt + s,
                                   [ptb, vc_b[i], vc1_b], [Bb(4)], sgc=True)

                    prev = None
                    for i in range(mt + 1):
                        sb_ = emit_s(i)
                        if prev is not None:
                            emit_pv(*prev)
                        prev = (i, sb_)
                    emit_pv(*prev)
                for s in range(NS):
                    c1, c2 = (0 * 2 + s) * 65, (1 * 2 + s) * 65
                    k.op(k.dve, lambda h: h.reciprocal(out=dst[:, 0:1], in_=B(4)[:, c1 + 64:c1 + 65]), reads=[Bb(4)], writes=[dst_b])
                    k.op(k.dve, lambda h: h.reciprocal(out=dst[:, 1:2], in_=B(4)[:, c2 + 64:c2 + 65]), reads=[Bb(4)], writes=[dst_b])
                    TT(k, k.dve, dst[:, 2:3], dst[:, 1:2], P["nlam"], ALU.mult, [dst_b, pb], [dst_b])
                    TS(k, k.dve, dt1[:], B(4)[:, c1:c1 + 64], dst[:, 0:1], ALU.mult, [Bb(4), dst_b], [dt1_b])
                    STT(k, od[:, s, h_ * 64:(h_ + 1) * 64], B(4)[:, c2:c2 + 64], dst[:, 2:3], dt1[:], ALU.mult, ALU.add,
                        [Bb(4), dst_b, dt1_b], [od_b])
            TT(k, k.pool, odq, od[:], od[:], ALU.mult, [od_b], [odq_b])
            k.op(k.dve, lambda h: h.reduce_sum(out=dst[:, 8:16], in_=odq.rearrange("p s (h v) -> p (s h) v", h=4), axis=AX.X),
                 reads=[odq_b], writes=[dst_b])
            ACT(k, dst[:, 16:24], dst[:, 8:16], AF.Ln, [dst_b], [dst_b], scale=1.0 / 64.0, bias=k.eps_ap)
            ACT(k, dst[:, 24:32], dst[:, 16:24], AF.Exp, [dst_b], [dst_b], scale=-0.5)
            TT(k, k.dve, od[:].rearrange("p s (h v) -> p (s h) v", h=4), od[:].rearrange("p s (h v) -> p (s h) v", h=4),
               dst[:, 24:32].unsqueeze(2).to_broadcast([128, 8, 64]), ALU.mult, [od_b, dst_b], [od_b])
            TT(k, k.dve, odb[:].rearrange("p s (h v) -> p (s h) v", h=4), od[:].rearrange("p s (h v) -> p (s h) v", h=4),
               P["dnorm"][:].unsqueeze(1).to_broadcast([128, 8, 64]), ALU.mult, [od_b, pb], [odb_b])
            for s in range(NS):
                for j in range(2):
                    TR(k, pT1[:, 256 + (s * 2 + j) * 128:256 + (s * 2 + j + 1) * 128], odb[:, s, j * 128:(j + 1) * 128], ident[:],
                       [odb_b, ident_b], [pT1_b])
            for s in range(NS):
                CP(k, k.dve, mixT[:, 2:4, s * 128:(s + 1) * 128],
                   pT1[:, 256 + s * 256:256 + (s + 1) * 256].rearrange("p (j c) -> p j c", j=2), [pT1_b], [mixT_b])

        if "diff" not in env.get("skip", ()):
            _sec_diff()
        def _sec_s5():
            for qt in range(4):
                q0 = qt * 64
                MM(k, B(4)[:, 0:16], C["jf"][:], zlast[:], True, True, [cb, zlast_b], [Bb(4)])
                TT(k, k.dve, zt[:], B(4)[:, 0:16], P["s64"], ALU.mult, [Bb(4), pb], [zt_b])
                TT(k, k.dve, zin[:], zlast[:], P["c64"], ALU.mult, [zlast_b, pb], [zin_b])
                TT(k, k.dve, zin[:], zin[:], zt[:], ALU.subtract, [zin_b, zt_b], [zin_b])
                TT(k, k.dve, zin[:], zin[:], P["rho"][:], ALU.mult, [zin_b, pb], [zin_b])
                for ch in range(2):
                    g0 = 8 * ch
                    for gg in range(8):
                        MM(k, B(2)[:, gg * 64:(gg + 1) * 64], P["bpad"][:, 0, g0 + gg, :], ub[:, ch, q0:q0 + 64], True, True,
                           [pb, ub_b], [Bb(2)])
                        MM(k, B(3)[:, gg * 64:(gg + 1) * 64], P["bpad"][:, 1, g0 + gg, :], ub[:, ch, q0:q0 + 64], True, True,
                           [pb, ub_b], [Bb(3)])
                    tc8 = P["tabc"][:, g0:g0 + 8, :].rearrange("p g j -> p (g j)")
                    ts8 = P["tabs"][:, g0:g0 + 8, :].rearrange("p g j -> p (g j)")
                    TT(k, k.dve, m1[:], B(2)[:, 0:512], tc8, ALU.mult, [Bb(2), pb], [m1_b])
                    TT(k, k.dve, m2[:], B(3)[:, 0:512], ts8, ALU.mult, [Bb(3), pb], [m2_b])
                    TT(k, k.pool, m1[:], m1[:], m2[:], ALU.add, [m1_b, m2_b], [m1_b])
                    m1v = m1[:].rearrange("p (g j) -> p g j", g=8)
                    TT(k, k.dve, m1v[:, :, 0], m1v[:, :, 0], zin[:, g0:g0 + 8], ALU.add, [m1_b, zin_b], [m1_b])
                    k.op(k.dve, lambda h: h.tensor_tensor_scan(
                        out=zz[:], data0=P["rhot"][:, g0:g0 + 8, :].rearrange("p g j -> p (g j)"), data1=m1[:],
                        initial=0.0, op0=ALU.mult, op1=ALU.add), reads=[pb, m1_b], writes=[zz_b])
                    CP(k, k.pool, zlast[:, g0:g0 + 8], zz[:].rearrange("p (g j) -> p g j", g=8)[:, :, 63], [zz_b], [zlast_b])
                    TT(k, k.pool, zc[:], zz[:], tc8, ALU.mult, [zz_b, pb], [zc_b])
                    TT(k, k.dve, zs[:], zz[:], ts8, ALU.mult, [zz_b, pb], [zs_b])
                    yo = ch * TT_ + q0
                    for gg in range(8):
                        MM(k, B(5)[:, yo:yo + 64], P["cpad"][:, 0, g0 + gg, :], zc[:, gg * 64:(gg + 1) * 64], gg == 0, False,
                           [pb, zc_b], [Bb(5)])
                        MM(k, B(5)[:, yo:yo + 64], P["cpad"][:, 1, g0 + gg, :], zs[:, gg * 64:(gg + 1) * 64], False, gg == 7,
                           [pb, zs_b], [Bb(5)])
            for ch in range(2):
                STT(k, f1[:], uf[:, ch, :], P["lv"][:, 4, ch:ch + 1], B(5)[:, ch * TT_:(ch + 1) * TT_], ALU.mult, ALU.add,
                    [uf_b, pb, Bb(5)], [f1_b])
                ACT(k, f2[:], f1[:], AF.Gelu_apprx_tanh, [f1_b], [f2_b])
                CP(k, k.pool, yb[:, ch, :], f2[:], [f2_b], [yb_b])
                if ch == 0:
                    CP(k, k.dve, f4[:], f2[:], [f2_b], [f4_b])
                else:
                    CP(k, k.dve, f5[:], f2[:], [f2_b], [f5_b])
            for nch in range(2):
                for kk in range(2):
                    MM(k, B(2 + nch)[:, 0:TT_], P["wglu"][:, kk, nch * 128:(nch + 1) * 128],
                       yb[:, kk, :], kk == 0, kk == 1, [pb, yb_b], [Bb(2 + nch)])
                ACT(k, f1[:], B(2 + nch)[:, 0:TT_], AF.Sigmoid, [Bb(2 + nch), pb], [f1_b], bias=P["bglu"][:, nch:nch + 1])
                src, src_b = (f4, f4_b) if nch == 0 else (f5, f5_b)
                TT(k, k.dve, mixT[:, 4 + nch, :], src[:], f1[:], ALU.mult, [src_b, f1_b], [mixT_b])

        if "s5" not in env.get("skip", ()):
            _sec_s5()
        def _sec_lru():
            for ch in range(2):
                cw = P["cw"]
                TS(k, k.dve, f1[:], rxh[:, ch, 3:3 + TT_], cw[:, ch, 3:4], ALU.mult, [rxh_b, pb], [f1_b],
                   s2=P["lv"][:, 0, ch:ch + 1], op1=ALU.add)
                for tap in range(3):
                    STT(k, f1[:], rxh[:, ch, tap:tap + TT_], cw[:, ch, tap:tap + 1], f1[:], ALU.mult, ALU.add,
                        [rxh_b, pb, f1_b], [f1_b])
                CP(k, k.pool, xcb[:], f1[:], [f1_b], [xcb_b])
                MM(k, B(2)[:, 0:TT_], P["wa"][:, ch, :], xcb[:], True, True, [pb, xcb_b], [Bb(2)])
                MM(k, B(3)[:, 0:TT_], P["wx"][:, ch, :], xcb[:], True, True, [pb, xcb_b], [Bb(3)])
                ACT(k, f2[:], B(2)[:, 0:TT_], AF.Sigmoid, [Bb(2), pb], [f2_b], bias=P["lv"][:, 1, ch:ch + 1])
                ACT(k, f3[:], B(3)[:, 0:TT_], AF.Sigmoid, [Bb(3), pb], [f3_b], bias=P["lv"][:, 2, ch:ch + 1])
                ACT(k, f4[:], f2[:], AF.Exp, [f2_b, pb], [f4_b], scale=P["lsp"][:, 2, ch:ch + 1])
                ACT(k, f5[:], f2[:], AF.Exp, [f2_b, pb], [f5_b], scale=P["lsp"][:, 3, ch:ch + 1])
                TS(k, k.dve, f5[:], f5[:], -1.0, ALU.mult, [f5_b], [f5_b], s2=1.0, op1=ALU.add)
                TS(k, k.dve, f5[:], f5[:], 1e-12, ALU.max, [f5_b], [f5_b])
                ACT(k, f5[:], f5[:], AF.Sqrt, [f5_b], [f5_b])
                TT(k, k.pool, f3[:], f3[:], f1[:], ALU.mult, [f3_b, f1_b], [f3_b])
                TT(k, k.dve, f3[:], f3[:], f5[:], ALU.mult, [f3_b, f5_b], [f3_b])
                k.op(k.dve, lambda h: h.tensor_tensor_scan(out=f2[:], data0=f4[:], data1=f3[:], initial=hst[:, ch:ch + 1],
                                                           op0=ALU.mult, op1=ALU.add),
                     reads=[f4_b, f3_b, hst_b], writes=[f2_b])
                CP(k, k.pool, hst[:, ch:ch + 1], f2[:, TT_ - 1:TT_], [f2_b], [hst_b])
                ACT(k, f1[:], rg[:, ch, :], AF.Gelu_apprx_tanh, [rg_b], [f1_b])
                TT(k, k.dve, mixT[:, 6 + ch, :], f2[:], f1[:], ALU.mult, [f2_b, f1_b], [mixT_b])
            CP(k, k.pool, rxh[:, :, 0:3], rxh[:, :, TT_:TT_ + 3], [rxh_b], [rxh_b])

        if "lru" not in env.get("skip", ()):
            _sec_lru()
        if dbg is not None:
            k.dma(k.sp, dbg[:, :, t0:t0 + TT_].rearrange("j p t -> p j t"), mixT[:], reads=[mixT_b], writes=[Buf("dbg")])

        def _sec_out():
            for s in range(NS):
                for hf in range(2):
                    for kk in range(8):
                        MM(k, B(hf)[:, 0:512], mixT[:, kk, s * 128:(s + 1) * 128], wout[:, kk, hf * 512:(hf + 1) * 512],
                           kk == 0, kk == 7, [mixT_b, wout_b], [Bb(hf)])
                post_norm_residual(k, [B(0)[:, 0:512], B(1)[:, 0:512]], [Bb(0), Bb(1)], xin[:, s, :], xin_b, gpost, gpost_b,
                                   st2, st2_b, nt.junk, nt.junk_b, tmpo, tmpo_b, xin[:, s, :], xin_b)

        if "out" not in env.get("skip", ()):
            _sec_out()
        k.dma(k.sp, xd[mt], xin[:], reads=[xin_b], writes=[x_dst_b[mt]])
    k.pop()


WEIGHT_SPECS = [
    ("ln_mix_post", [L, D]), ("ln_ffn_post", [L, D]),
    ("ffn_w1", [L, D, DFF]), ("ffn_w2", [L, DFF, D]),
    ("ln_mix_pre_col", [L, 128, 8]), ("ln_ffn_pre_col", [L, 128, 8]),
    ("w_in_ext", [L, D, NCOL]), ("w_out", [L, D, D]),
    ("rope_cos", [128, S]), ("rope_ssin", [128, S]),
    ("gla_w_gate", [L, 16, 128]), ("gla_b_gate_col", [L, 128, 1]), ("gla_norm", [L, 64]),
    ("diff_lq1", [L, 32]), ("diff_lk1", [L, 32]), ("diff_lq2", [L, 32]), ("diff_lk2", [L, 32]), ("diff_norm", [L, 64]),
    ("s5_log_step", [L, 16]), ("s5_a_re_T", [L, 64, 16]), ("s5_a_im_T", [L, 64, 16]),
    ("s5_b_re_p", [L, 64, 256]), ("s5_b_im_p", [L, 64, 256]), ("s5_c_re_p", [L, 64, 256]), ("s5_c_im_p", [L, 64, 256]),
    ("s5_d_col", [L, 128, 2]), ("s5_w_glu", [L, 256, 256]), ("s5_b_glu_col", [L, 128, 2]),
    ("lru_conv_w_col", [L, 128, 8]), ("lru_conv_b_col", [L, 128, 2]), ("lru_w_a", [L, 4, 64, 64]), ("lru_w_x", [L, 4, 64, 64]),
    ("lru_b_a_col", [L, 128, 2]), ("lru_b_x_col", [L, 128, 2]), ("lru_lambda_col", [L, 128, 2]),
]


def build(mode="full", opts=None):
    nc = bass.Bass("TRN2", target_bir_lowering=False)
    env = dict(opts or {})
    x = nc.dram_tensor("x", [S, D], F32, kind="ExternalInput").ap()
    for name, shape in WEIGHT_SPECS:
        env[name] = nc.dram_tensor(name, shape, F32, kind="ExternalInput").ap()
    env["lru_conv_w_col"] = env["lru_conv_w_col"].rearrange("l p (c t) -> l p c t", c=2)
    y = nc.dram_tensor("y", [S, D], F32, kind="ExternalOutput").ap()
    k = K(nc)
    eps = k.sb([128, 1], F32, "eps_c")
    eps_b = Buf("eps")
    k.op(k.pool, lambda h: h.memset(eps[:], EPS), writes=[eps_b])
    k.eps_ap = eps[:]
    env["banks"] = [k.ps([128, 512], F32, "bank%d" % i) for i in range(6)]
    env["bank_b"] = [Buf("bank%d" % i, True) for i in range(6)]
    env["pT"] = k.ps([128, 1024], BF16, "pT")
    env["pT_b"] = Buf("pT", True)
    env["pT1"] = k.ps([128, 1024], BF16, "pT1")
    env["pT1_b"] = Buf("pT1", True)
    env["ident"], env["ident_b"] = make_ident(k, nc)
    C = make_consts(k, env)
    k.barrier()

    NMT = 16
    mk = lambda p: [Buf("%s%d" % (p, i)) for i in range(NMT)]
    x_b, y_b = mk("x"), mk("y")
    dbg = None
    if mode == "ffn0":
        ffn_pass(k, env, 0, x, x_b, y, y_b)
    elif mode == "mix0":
        dbg = nc.dram_tensor("dbg", [8, 128, S], BF16, kind="ExternalOutput").ap()
        mixer_pass(k, env, 0, x, x_b, y, y_b, C, dbg=dbg)
    elif mode == "full":
        xa = nc.dram_tensor("xa", [S, D], F32, kind="Internal").ap()
        xb = nc.dram_tensor("xb", [S, D], F32, kind="Internal").ap()
        xa_b, xb_b = mk("xa"), mk("xb")
        mixer_pass(k, env, 0, x, x_b, xa, xa_b, C)
        ffn_pass(k, env, 0, xa, xa_b, xb, xb_b)
        mixer_pass(k, env, 1, xb, xb_b, xa, xa_b, C)
        ffn_pass(k, env, 1, xa, xa_b, y, y_b)
    k.finish(y_b)
    k.barrier()
    return nc, env


_CACHE = {}


def _rope_tables():
    inv = 10000.0 ** (-np.arange(0, 32, 2, dtype=np.float32) / 32.0)
    ang = np.arange(S, dtype=np.float32)[:, None] * inv[None, :].astype(np.float32)
    ang = ang.astype(np.float32)
    emb = np.concatenate([ang, ang], axis=-1)
    cos = np.cos(emb).astype(np.float32).T
    sin = np.sin(emb).astype(np.float32).T
    ssin = np.concatenate([-sin[:16], sin[16:]], axis=0)
    return np.tile(cos, (4, 1)), np.tile(ssin, (4, 1))


def _prep_weights(inputs):
    src = {k_: np.asarray(v) for k_, v in inputs.items()}
    col8 = lambda a: a.reshape(L, 8, 128).transpose(0, 2, 1)
    col2 = lambda a: a.reshape(L, 2, 128).transpose(0, 2, 1)
    src["ln_mix_pre_col"] = col8(src["ln_mix_pre"])
    src["ln_ffn_pre_col"] = col8(src["ln_ffn_pre"])
    w = src["w_in"]
    seg = lambda a, b: w[:, :, a:b]

    def swap(a0):
        parts = []
        for blk in range(8):
            b0 = a0 + blk * 32
            parts += [w[:, :, b0 + 16:b0 + 32], w[:, :, b0:b0 + 16]]
        return np.concatenate(parts, axis=-1)

    src["w_in_ext"] = np.concatenate([
        seg(0, 128), seg(128, 256), seg(784, 1040), swap(784), seg(1040, 1296), swap(1040),
        seg(1552, 1808), seg(1808, 2064), seg(2064, 2320), seg(512, 528), np.zeros((L, D, 48), np.float32),
        seg(256, 512), seg(528, 784), seg(1296, 1552)], axis=-1)
    assert src["w_in_ext"].shape[-1] == NCOL
    src["rope_cos"], src["rope_ssin"] = _rope_tables()
    src["gla_b_gate_col"] = src["gla_b_gate"].reshape(L, 128, 1)
    src["s5_a_re_T"] = src["s5_a_re"].transpose(0, 2, 1)
    src["s5_a_im_T"] = src["s5_a_im"].transpose(0, 2, 1)
    src["s5_b_re_p"] = src["s5_b_re"].transpose(0, 2, 1, 3).reshape(L, 64, 256)
    src["s5_b_im_p"] = src["s5_b_im"].transpose(0, 2, 1, 3).reshape(L, 64, 256)
    src["s5_c_re_p"] = src["s5_c_re"].transpose(0, 3, 1, 2).reshape(L, 64, 256)
    src["s5_c_im_p"] = src["s5_c_im"].transpose(0, 3, 1, 2).reshape(L, 64, 256)
    src["s5_d_col"] = col2(src["s5_d"])
    src["s5_b_glu_col"] = col2(src["s5_b_glu"])
    src["lru_conv_w_col"] = src["lru_conv_w"].reshape(L, 4, 2, 128).transpose(0, 3, 2, 1).reshape(L, 128, 8)
    for nm_ in ("lru_conv_b", "lru_b_a", "lru_b_x", "lru_lambda"):
        src[nm_ + "_col"] = col2(src[nm_])
    out = {}
    for name, shape in WEIGHT_SPECS:
        a = np.ascontiguousarray(src[name], dtype=np.float32)
        assert list(a.shape) == list(shape), (name, a.shape, shape)
        out[name] = a
    return out


def run(inputs, mode="full", want=("y",), opts=None):
    key = (mode, repr(sorted((opts or {}).items())))
    if key not in _CACHE:
        _CACHE[key] = build(mode, opts)
    nc, env = _CACHE[key]
    w = _prep_weights(inputs)
    x = np.ascontiguousarray(inputs["x"], dtype=np.float32)
    in_maps = []
    for c in range(NCORES):
        m = dict(w)
        m["x"] = x[c]
        in_maps.append(m)
    res = run_bass_kernel_spmd(nc, in_maps, core_ids=list(range(NCORES)))
    outs = [np.stack([r[nm_] for r in res.results], axis=0) for nm_ in want]
    return outs[0] if len(outs) == 1 else outs


def kernel(**inputs):
    return run(inputs, "full")
```

```python
import math
import numpy as np
import concourse.bass as bass
import concourse.mybir as mybir
from concourse.bass_utils import run_bass_kernel_spmd

F32 = mybir.dt.float32
BF16 = mybir.dt.bfloat16
I32 = mybir.dt.int32
AF = mybir.ActivationFunctionType
ALU = mybir.AluOpType
AX = mybir.AxisListType

S = 4096
D = 1024
DFF = 4096
L = 2
EPS = 1e-6
NCORES = 8


class Buf:
    __slots__ = ("name", "w", "r", "excl")

    def __init__(self, name, excl=False):
        self.name = name
        self.w = {}
        self.r = {}
        self.excl = excl


class Eng:
    def __init__(self, nc, name, h, sync_self):
        self.name = name
        self.h = h
        self.sem = nc.semaphore("sem_" + name).__enter__()
        self.cnt = 0
        self.waited = {}
        self.sync_self = sync_self


class K:
    def __init__(self, nc, n_dma_sems=20):
        self.nc = nc
        self.pe = Eng(nc, "pe", nc.tensor, False)
        self.act = Eng(nc, "act", nc.scalar, True)
        self.dve = Eng(nc, "dve", nc.vector, True)
        self.pool = Eng(nc, "pool", nc.gpsimd, True)
        self.sp = Eng(nc, "sp", nc.sync, False)
        self.dsem = {}
        for q in ("sp", "pool", "act"):
            self.dsem[q] = [[nc.semaphore("dq_%s_%d" % (q, i)).__enter__(), 0] for i in range(n_dma_sems)]
        self.dnext = {"sp": 0, "pool": 0, "act": 0}
        self._n = 0
        self.stacks = []

    def push(self):
        from contextlib import ExitStack
        self.stacks.append(ExitStack())

    def pop(self):
        self.barrier()
        self.stacks.pop().close()

    def barrier(self):
        engs = [self.pe, self.act, self.dve, self.pool, self.sp]
        deps = [(e.name, e.sem, e.cnt) for e in engs if e.cnt > 0]
        for q, pool in self.dsem.items():
            for j, (sem, cnt) in enumerate(pool):
                if cnt > 0:
                    deps.append(("d_%s_%d" % (q, j), sem, 16 * cnt))
        for e in engs:
            sv = e.sync_self
            e.sync_self = False
            self._wait(e, deps)
            e.sync_self = sv

    def uid(self, p="t"):
        self._n += 1
        return "%s%d" % (p, self._n)

    def sb(self, shape, dtype, name=None):
        cm = self.nc.sbuf_tensor(name or self.uid("sb"), list(shape), dtype)
        if self.stacks:
            return self.stacks[-1].enter_context(cm)
        return cm.__enter__()

    def ps(self, shape, dtype, name=None):
        return self.nc.psum_tensor(name or self.uid("ps"), list(shape), dtype).__enter__()

    def _wait(self, e, deps):
        best = {}
        for (key, sem, val) in deps:
            if key == e.name and not e.sync_self:
                continue
            if key not in best or best[key][1] < val:
                best[key] = (sem, val)
        for key, (sem, val) in best.items():
            if e.waited.get(key, 0) >= val:
                continue
            e.h.wait_ge(sem, val)
            e.waited[key] = val

    def _deps(self, reads, writes):
        deps = []
        for b in reads:
            deps.extend(b.w.values())
            if b.excl:
                deps.extend(b.r.values())
        for b in writes:
            deps.extend(b.w.values())
            deps.extend(b.r.values())
        return deps

    def op(self, e, fn, reads=(), writes=()):
        self._wait(e, self._deps(reads, writes))
        inst = fn(e.h)
        e.cnt += 1
        inst.then_inc(e.sem, 1)
        t = (e.name, e.sem, e.cnt)
        for b in reads:
            b.r[e.name] = t
        for b in writes:
            b.w[e.name] = t
        return t

    def dma(self, q, out, in_, reads=(), writes=()):
        pool = self.dsem[q.name]
        j = self.dnext[q.name]
        self.dnext[q.name] = (j + 1) % len(pool)
        sem, cnt = pool[j]
        key = "d_%s_%d" % (q.name, j)
        deps = self._deps(reads, writes)
        if cnt > 0:
            deps.append((key, sem, 16 * cnt))
        self._wait(q, deps)
        q.h.dma_start(out=out, in_=in_).then_inc(sem, 16)
        pool[j][1] = cnt + 1
        t = (key, sem, 16 * (cnt + 1))
        for b in reads:
            b.r[key] = t
        for b in writes:
            b.w[key] = t
        return t

    def finish(self, bufs):
        deps = []
        for b in bufs:
            deps.extend(b.w.values())
        self._wait(self.sp, deps)


def make_ident(k, nc):
    ident = k.sb([128, 128], BF16, "ident")
    b = Buf("ident")
    k.op(k.pool, lambda h: h.memset(ident[:], 1.0), writes=[b])
    k.op(k.pool, lambda h: h.affine_select(out=ident[:], in_=ident[:], pattern=[[1, 128]],
                                            compare_op=ALU.is_equal, fill=0.0, base=0,
                                            channel_multiplier=-1), reads=[b], writes=[b])
    return ident, b


class NormT:
    def __init__(self, k, ns, ident, ident_b, pT, pT_b, tag):
        self.k = k
        self.ns = ns
        self.ident, self.ident_b = ident, ident_b
        self.pT, self.pT_b = pT, pT_b
        self.xn = k.sb([128, 1024], BF16, tag + "_xn")
        self.xn_b = Buf(tag + "_xn")
        self.junk = k.sb([128, 1024], BF16, tag + "_junk")
        self.junk_b = Buf(tag + "_junk")
        self.st = k.sb([128, 4 * ns], F32, tag + "_st")
        self.st_b = Buf(tag + "_st")

    def run(self, xin, xin_b, gcol, gcol_b, hT, hT_b):
        k, ns = self.k, self.ns
        st = self.st
        for s in range(ns):
            k.op(k.act, lambda h: h.activation(out=self.junk[:], in_=xin[:, s, :], func=AF.Square,
                                               accum_out=st[:, s:s + 1]),
                 reads=[xin_b], writes=[self.junk_b, self.st_b])
        k.op(k.act, lambda h: h.activation(out=st[:, ns:2 * ns], in_=st[:, 0:ns], func=AF.Ln,
                                           scale=1.0 / D, bias=self.k.eps_ap),
             reads=[self.st_b], writes=[self.st_b])
        k.op(k.act, lambda h: h.activation(out=st[:, 2 * ns:3 * ns], in_=st[:, ns:2 * ns], func=AF.Exp,
                                           scale=-0.5),
             reads=[self.st_b], writes=[self.st_b])
        for s in range(ns):
            k.op(k.act, lambda h: h.activation(out=self.xn[:], in_=xin[:, s, :], func=AF.Copy,
                                               scale=st[:, 2 * ns + s:2 * ns + s + 1]),
                 reads=[xin_b, self.st_b], writes=[self.xn_b])
            for kk in range(8):
                k.op(k.pe, lambda h: h.transpose(out=self.pT[:, kk * 128:(kk + 1) * 128],
                                                 in_=self.xn[:, kk * 128:(kk + 1) * 128],
                                                 identity=self.ident[:]),
                     reads=[self.xn_b, self.ident_b], writes=[self.pT_b])
            k.op(k.dve, lambda h: h.tensor_tensor(
                out=hT[:, :, s * 128:(s + 1) * 128],
                in0=self.pT[:].rearrange("p (k t) -> p k t", k=8),
                in1=gcol[:].unsqueeze(2).to_broadcast([128, 8, 128]),
                op=ALU.mult),
                reads=[self.pT_b, gcol_b], writes=[hT_b])


def post_norm_residual(k, ps_halves, ps_bufs, xres_ap, xres_b, gpost, gpost_b, st, st_b, junk, junk_b,
                       tmp, tmp_b, out_ap, out_b):
    for hf in range(2):
        k.op(k.act, lambda h: h.activation(out=junk[:, 0:512], in_=ps_halves[hf], func=AF.Square,
                                           accum_out=st[:, hf:hf + 1]),
             reads=[ps_bufs[hf]], writes=[junk_b, st_b])
    k.op(k.dve, lambda h: h.tensor_tensor(out=st[:, 2:3], in0=st[:, 0:1], in1=st[:, 1:2], op=ALU.add),
         reads=[st_b], writes=[st_b])
    k.op(k.act, lambda h: h.activation(out=st[:, 3:4], in_=st[:, 2:3], func=AF.Ln, scale=1.0 / D, bias=k.eps_ap),
         reads=[st_b], writes=[st_b])
    k.op(k.act, lambda h: h.activation(out=st[:, 4:5], in_=st[:, 3:4], func=AF.Exp, scale=-0.5),
         reads=[st_b], writes=[st_b])
    for hf in range(2):
        k.op(k.dve, lambda h: h.scalar_tensor_tensor(out=tmp[:, hf * 512:(hf + 1) * 512], in0=ps_halves[hf],
                                                     scalar=st[:, 4:5], in1=gpost[:, hf * 512:(hf + 1) * 512],
                                                     op0=ALU.mult, op1=ALU.mult),
             reads=[ps_bufs[hf], st_b, gpost_b], writes=[tmp_b])
    k.op(k.pool, lambda h: h.tensor_tensor(out=out_ap, in0=tmp[:], in1=xres_ap, op=ALU.add),
         reads=[tmp_b, xres_b], writes=[out_b])


def ffn_pass(k, env, l, x_src, x_src_b, x_dst, x_dst_b):
    nc = k.nc
    NS = 2
    TT = NS * 128
    NMT = S // TT
    banks, bank_b = env["banks"], env["bank_b"]
    pT, pT_b = env["pT"], env["pT_b"]
    k.push()
    sbase = nc.sbuf_bytes_remaining

    w1 = k.sb([128, 8, DFF], BF16, "w1_%d" % l)
    w2 = k.sb([128, 32, D], BF16, "w2_%d" % l)
    w1_b = [Buf("w1k%d" % i) for i in range(8)]
    w2_b = [Buf("w2k%d" % i) for i in range(8)]
    w1d = env["ffn_w1"][l].rearrange("(k p) f -> p k f", p=128)
    w2d = env["ffn_w2"][l].rearrange("(k p) f -> p k f", p=128)
    for i in range(8):
        k.dma(k.pool, w1[:, i, :], w1d[:, i, :], writes=[w1_b[i]])
    for i in range(8):
        k.dma(k.pool, w2[:, 4 * i:4 * i + 4, :], w2d[:, 4 * i:4 * i + 4, :], writes=[w2_b[i]])

    gcol = k.sb([128, 8], F32, "f_gcol%d" % l)
    gcol_b = Buf("gcol")
    k.dma(k.sp, gcol[:], env["ln_ffn_pre_col"][l], writes=[gcol_b])
    gpost = k.sb([128, D], F32, "f_gpost%d" % l)
    gpost_b = Buf("gpost")
    k.dma(k.sp, gpost[:], env["ln_ffn_post"][l:l + 1, :].partition_broadcast(128), writes=[gpost_b])

    xin = [k.sb([128, NS, D], F32, "f_xin%d_%d" % (l, i)) for i in range(2)]
    xin_b = [Buf("xin%d" % i) for i in range(2)]
    hT = [k.sb([128, 8, TT], BF16, "f_hT%d_%d" % (l, i)) for i in range(2)]
    hT_b = [Buf("hT%d" % i) for i in range(2)]
    hid = k.sb([128, 32, TT], BF16, "f_hid%d" % l)
    hid_b = [Buf("hid%d" % i) for i in range(4)]
    rl = [k.sb([128, TT], F32, "f_rl%d_%d" % (l, i)) for i in range(2)]
    rl_b = [Buf("rl%d" % i) for i in range(2)]
    tmp = k.sb([128, D], F32, "f_tmp%d" % l)
    tmp_b = Buf("tmp")
    st2 = k.sb([128, 8], F32, "f_st2%d" % l)
    st2_b = Buf("st2")
    nt = NormT(k, NS, env["ident"], env["ident_b"], pT, pT_b, "f_nt%d" % l)

    xs = x_src.rearrange("(m s p) d -> m p s d", s=NS, p=128)
    xd = x_dst.rearrange("(m s p) d -> m p s d", s=NS, p=128)

    def load(mt):
        k.dma(k.sp, xin[mt % 2][:], xs[mt], reads=[x_src_b[mt]], writes=[xin_b[mt % 2]])

    load(0)
    for mt in range(NMT):
        sl = mt % 2
        if mt + 1 < NMT:
            load(mt + 1)
        nt.run(xin[sl], xin_b[sl], gcol, gcol_b, hT[sl], hT_b[sl])
        for fc in range(32):
            pb = fc % 2
            for kk in range(8):
                k.op(k.pe, lambda h: h.matmul(banks[pb][:, 0:TT], lhsT=w1[:, kk, fc * 128:(fc + 1) * 128],
                                              rhs=hT[sl][:, kk, :], start=(kk == 0), stop=(kk == 7)),
                     reads=[w1_b[kk], hT_b[sl]], writes=[bank_b[pb]])
            k.op(k.act, lambda h: h.activation(out=rl[pb][:], in_=banks[pb][:, 0:TT], func=AF.Relu),
                 reads=[bank_b[pb]], writes=[rl_b[pb]])
            k.op(k.pool, lambda h: h.tensor_tensor(out=hid[:, fc, :], in0=rl[pb][:], in1=rl[pb][:], op=ALU.mult),
                 reads=[rl_b[pb]], writes=[hid_b[fc // 8]])
        for s in range(NS):
            pbs = [2 + 2 * (s % 2), 3 + 2 * (s % 2)]
            for hf in range(2):
                for fc in range(32):
                    k.op(k.pe, lambda h: h.matmul(banks[pbs[hf]][:, 0:512], lhsT=hid[:, fc, s * 128:(s + 1) * 128],
                                                  rhs=w2[:, fc, hf * 512:(hf + 1) * 512],
                                                  start=(fc == 0), stop=(fc == 31)),
                         reads=[hid_b[fc // 8], w2_b[fc // 4]], writes=[bank_b[pbs[hf]]])
            post_norm_residual(k, [banks[pbs[0]][:, 0:512], banks[pbs[1]][:, 0:512]],
                               [bank_b[pbs[0]], bank_b[pbs[1]]],
                               xin[sl][:, s, :], xin_b[sl], gpost, gpost_b, st2, st2_b,
                               nt.junk, nt.junk_b, tmp, tmp_b, xin[sl][:, s, :], xin_b[sl])
        k.dma(k.sp, xd[mt], xin[sl][:], reads=[xin_b[sl]], writes=[x_dst_b[mt]])
    env["sbuf_used_ffn"] = sbase - nc.sbuf_bytes_remaining
    k.pop()


def TT(k, e, out, in0, in1, op, R, W):
    return k.op(e, lambda h: h.tensor_tensor(out=out, in0=in0, in1=in1, op=op), reads=R, writes=W)


def TS(k, e, out, in0, s1, op0, R, W, s2=None, op1=None):
    if op1 is None:
        return k.op(e, lambda h: h.tensor_scalar(out=out, in0=in0, scalar1=s1, scalar2=None, op0=op0), reads=R, writes=W)
    return k.op(e, lambda h: h.tensor_scalar(out=out, in0=in0, scalar1=s1, scalar2=s2, op0=op0, op1=op1), reads=R, writes=W)


def STT(k, out, in0, scalar, in1, op0, op1, R, W):
    return k.op(k.dve, lambda h: h.scalar_tensor_tensor(out=out, in0=in0, scalar=scalar, in1=in1, op0=op0, op1=op1),
                reads=R, writes=W)


def ACT(k, out, in_, func, R, W, scale=1.0, bias=None):
    if bias is None:
        return k.op(k.act, lambda h: h.activation(out=out, in_=in_, func=func, scale=scale), reads=R, writes=W)
    return k.op(k.act, lambda h: h.activation(out=out, in_=in_, func=func, scale=scale, bias=bias), reads=R, writes=W)


def CP(k, e, out, in_, R, W):
    return k.op(e, lambda h: h.tensor_copy(out=out, in_=in_), reads=R, writes=W)


def MM(k, out, lhsT, rhs, start, stop, R, W, sgc=False):
    return k.op(k.pe, lambda h: h.matmul(out, lhsT=lhsT, rhs=rhs, start=start, stop=stop, skip_group_check=sgc),
                reads=R, writes=W)


def TR(k, out, in_, ident, R, W):
    return k.op(k.pe, lambda h: h.transpose(out=out, in_=in_, identity=ident), reads=R, writes=W)


def MSET(k, e, ap, val, W):
    return k.op(e, lambda h: h.memset(ap, val), writes=W)


def ASEL(k, out, in_, pattern, op, base, cm, R, W):
    return k.op(k.pool, lambda h: h.affine_select(out=out, in_=in_, pattern=pattern, compare_op=op, fill=0.0,
                                                  base=base, channel_multiplier=cm), reads=R, writes=W)


OFF_GQ, OFF_GK = 0, 128
OFF_DQ, OFF_DQS, OFF_DK, OFF_DKS = 256, 512, 768, 1024
OFF_SU, OFF_RX, OFF_RG, OFF_GLR = 1280, 1536, 1792, 2048
OFF_TMG, OFF_DV = 2112, 2624
NCOL = 2880
TWO_PI = 2.0 * math.pi
CW1 = 6.28125
CW2 = TWO_PI - 6.28125


def make_consts(k, env):
    c = {}
    b = Buf("consts")
    c["b"] = b
    one = k.sb([128, 1], F32, "c_one"); MSET(k, k.pool, one[:], 1.0, [b]); c["one"] = one
    sgn = k.sb([128, 1], F32, "c_sgn"); MSET(k, k.pool, sgn[0:64, :], 1.0, [b]); MSET(k, k.pool, sgn[64:128, :], -1.0, [b])
    c["sgn"] = sgn
    tri = k.sb([128, 128], BF16, "c_tri"); MSET(k, k.pool, tri[:], 1.0, [b])
    ASEL(k, tri[:], tri[:], [[1, 128]], ALU.is_ge, 0, -1, [b], [b]); c["tri"] = tri
    tri64 = k.sb([64, 64], F32, "c_tri64"); MSET(k, k.pool, tri64[:], 1.0, [b])
    ASEL(k, tri64[:], tri64[:], [[1, 64]], ALU.is_ge, 0, -1, [b], [b]); c["tri64"] = tri64
    hm = k.sb([128, 4], F32, "c_hm"); MSET(k, k.pool, hm[:], 1.0, [b])
    ASEL(k, hm[:], hm[:], [[-32, 4]], ALU.is_ge, 0, 1, [b], [b])
    ASEL(k, hm[:], hm[:], [[32, 4]], ALU.is_ge, 31, -1, [b], [b]); c["hm"] = hm
    rm = k.sb([128, 8], F32, "c_rm16"); MSET(k, k.pool, rm[:], 1.0, [b])
    ASEL(k, rm[:], rm[:], [[-16, 8]], ALU.is_ge, 0, 1, [b], [b])
    ASEL(k, rm[:], rm[:], [[16, 8]], ALU.is_ge, 15, -1, [b], [b]); c["rm16"] = rm
    jf = k.sb([128, 128], F32, "c_jf"); j2 = k.sb([128, 128], F32, "c_j2")
    MSET(k, k.pool, jf[:], 1.0, [b]); MSET(k, k.pool, j2[:], 1.0, [b])
    ASEL(k, jf[:], jf[:], [[1, 128]], ALU.is_equal, -64, -1, [b], [b])
    ASEL(k, j2[:], j2[:], [[1, 128]], ALU.is_equal, 64, -1, [b], [b])
    TT(k, k.pool, jf[:], jf[:], j2[:], ALU.add, [b], [b]); c["jf"] = jf
    rs = k.sb([128, 256], F32, "c_reset"); MSET(k, k.pool, rs[:], 1.0, [b])
    for i in range(4):
        MSET(k, k.pool, rs[:, 64 * i:64 * i + 1], 0.0, [b])
    c["reset"] = rs
    return c


def mixer_prep(k, env, l, C):
    nc = k.nc
    P = {}
    cb = C["b"]
    lam_init = 0.8 - 0.6 * math.exp(-0.3 * l)
    pb = Buf("prep")
    P["b"] = pb
    banks, bank_b = env["banks"], env["bank_b"]
    pT1, pT1_b = env["pT1"], env["pT1_b"]

    def sbt(shape, dt, nm):
        return k.sb(shape, dt, "p%d_%s" % (l, nm))

    wgate = sbt([16, 128], BF16, "wgate"); negb = sbt([128, 1], F32, "negb"); gnorm = sbt([128, 64], F32, "gnorm")
    lw = sbt([128, 8], F32, "lw"); dnorm = sbt([128, 64], F32, "dnorm")
    cw = sbt([128, 2, 4], F32, "cw"); lv = sbt([128, 5, 2], F32, "lv"); bglu = sbt([128, 2], F32, "bglu")
    wa = sbt([128, 2, 128], BF16, "wa"); wx = sbt([128, 2, 128], BF16, "wx"); lsp = sbt([128, 4, 2], F32, "lsp")
    wglu = sbt([128, 2, 256], BF16, "wglu"); rho = sbt([128, 16], F32, "rho")
    bpad = sbt([128, 2, 16, 128], BF16, "bpad"); cpad = sbt([128, 2, 16, 128], BF16, "cpad")
    wl = sbt([128, 9, 2, 16], F32, "wl")
    tabc = sbt([128, 16, 64], F32, "tabc"); tabs = sbt([128, 16, 64], F32, "tabs")
    rhot = sbt([128, 16, 64], F32, "rhot")
    k.push()

    def ld(t_ap, src, q=None):
        k.dma(q or k.sp, t_ap, src, writes=[pb])

    ld(wgate[:], env["gla_w_gate"][l], k.pool)
    ld(negb[:], env["gla_b_gate_col"][l])
    TS(k, k.dve, negb[:], negb[:], -1.0, ALU.mult, [pb], [pb])
    ld(gnorm[:], env["gla_norm"][l:l + 1, :].partition_broadcast(128))
    P.update(wgate=wgate, negb=negb, gnorm=gnorm)
    lq = sbt([128, 4, 32], F32, "lq")
    for i, nm in enumerate(("diff_lq1", "diff_lk1", "diff_lq2", "diff_lk2")):
        ld(lq[:, i, :], env[nm][l:l + 1, :].partition_broadcast(128))
    pr = sbt([128, 2, 32], F32, "lpr")
    TT(k, k.dve, pr[:, 0, :], lq[:, 0, :], lq[:, 1, :], ALU.mult, [pb], [pb])
    TT(k, k.dve, pr[:, 1, :], lq[:, 2, :], lq[:, 3, :], ALU.mult, [pb], [pb])
    k.op(k.dve, lambda h: h.reduce_sum(out=lw[:, 0:2], in_=pr[:], axis=AX.X), reads=[pb], writes=[pb])
    ACT(k, lw[:, 2:4], lw[:, 0:2], AF.Exp, [pb], [pb])
    TT(k, k.dve, lw[:, 4:5], lw[:, 2:3], lw[:, 3:4], ALU.subtract, [pb], [pb])
    TS(k, k.dve, lw[:, 5:6], lw[:, 4:5], lam_init, ALU.add, [pb], [pb], s2=-1.0, op1=ALU.mult)
    ld(dnorm[:], env["diff_norm"][l:l + 1, :].partition_broadcast(128))
    TS(k, k.dve, dnorm[:], dnorm[:], 1.0 - lam_init, ALU.mult, [pb], [pb])
    P.update(nlam=lw[:, 5:6], dnorm=dnorm)
    ld(cw[:], env["lru_conv_w_col"][l])
    for i, nm in enumerate(("lru_conv_b_col", "lru_b_a_col", "lru_b_x_col", "lru_lambda_col", "s5_d_col")):
        ld(lv[:, i, :], env[nm][l])
    ld(bglu[:], env["s5_b_glu_col"][l])
    MSET(k, k.pool, wa[:], 0.0, [pb]); MSET(k, k.pool, wx[:], 0.0, [pb])
    for n in range(4):
        r0 = (n % 2) * 64
        ld(wa[r0:r0 + 64, n // 2, r0:r0 + 64], env["lru_w_a"][l, n], k.pool)
        ld(wx[r0:r0 + 64, n // 2, r0:r0 + 64], env["lru_w_x"][l, n], k.pool)
    ACT(k, lsp[:, 0, :], lv[:, 3, :], AF.Exp, [pb], [pb], scale=-1.0)
    ACT(k, lsp[:, 1, :], lsp[:, 0, :], AF.Ln, [pb, cb], [pb], bias=C["one"][:])
    TS(k, k.dve, lsp[:, 2, :], lsp[:, 1, :], -8.0, ALU.mult, [pb], [pb])
    TS(k, k.dve, lsp[:, 3, :], lsp[:, 1, :], -16.0, ALU.mult, [pb], [pb])
    ld(wglu[:], env["s5_w_glu"][l].rearrange("(k p) n -> p k n", p=128), k.pool)
    P.update(cw=cw, lv=lv, bglu=bglu, wa=wa, wx=wx, lsp=lsp, wglu=wglu)
    lr = sbt([128, 16], F32, "lr"); li = sbt([128, 16], F32, "li"); ls = sbt([128, 16], F32, "ls")
    for hlf in range(2):
        ld(lr[hlf * 64:(hlf + 1) * 64, :], env["s5_a_re_T"][l])
        ld(li[hlf * 64:(hlf + 1) * 64, :], env["s5_a_im_T"][l])
    ld(ls[:], env["s5_log_step"][l:l + 1, :].partition_broadcast(128))
    bri = sbt([128, 2, 256], F32, "bri"); cri = sbt([128, 2, 256], F32, "cri")
    for hlf in range(2):
        sl_ = slice(hlf * 64, (hlf + 1) * 64)
        ld(bri[sl_, 0, :], env["s5_b_re_p"][l]); ld(bri[sl_, 1, :], env["s5_b_im_p"][l])
        ld(cri[sl_, 0, :], env["s5_c_re_p"][l]); ld(cri[sl_, 1, :], env["s5_c_im_p"][l])
    w = sbt([128, 24, 16], F32, "s5w")
    W_ = lambda i: w[:, i, :]
    R, Wr = [pb], [pb]
    ACT(k, W_(0), ls[:], AF.Exp, R, Wr)
    TT(k, k.dve, W_(1), lr[:], W_(0), ALU.mult, R, Wr)
    TT(k, k.dve, W_(2), li[:], W_(0), ALU.mult, R, Wr)
    ACT(k, rho[:], W_(1), AF.Exp, R, Wr)
    ki = sbt([128, 16], I32, "ki")

    def sin_reduced(dst, ang):
        TS(k, k.dve, W_(3), ang, 1.0 / TWO_PI, ALU.mult, R, Wr)
        CP(k, k.dve, ki[:], W_(3), R, Wr)
        CP(k, k.dve, W_(4), ki[:], R, Wr)
        STT(k, W_(5), W_(4), -CW1, ang, ALU.mult, ALU.add, R, Wr)
        STT(k, W_(5), W_(4), -CW2, W_(5), ALU.mult, ALU.add, R, Wr)
        TS(k, k.dve, W_(5), W_(5), 3.1415925, ALU.min, R, Wr, s2=-3.1415925, op1=ALU.max)
        ACT(k, dst, W_(5), AF.Sin, R, Wr)

    sin_reduced(W_(6), W_(2))
    TS(k, k.dve, W_(7), W_(2), math.pi / 2.0, ALU.add, R, Wr)
    sin_reduced(W_(8), W_(7))
    TT(k, k.dve, W_(9), W_(6), W_(6), ALU.mult, R, Wr)
    TT(k, k.dve, W_(10), W_(8), W_(8), ALU.mult, R, Wr)
    TT(k, k.dve, W_(9), W_(9), W_(10), ALU.add, R, Wr)
    TS(k, k.dve, W_(9), W_(9), -0.5, ALU.mult, R, Wr, s2=1.5, op1=ALU.add)
    TT(k, k.dve, W_(6), W_(6), W_(9), ALU.mult, R, Wr)
    TT(k, k.dve, W_(8), W_(8), W_(9), ALU.mult, R, Wr)
    TT(k, k.dve, W_(11), rho[:], W_(8), ALU.mult, R, Wr)
    TT(k, k.dve, W_(12), rho[:], W_(6), ALU.mult, R, Wr)
    TT(k, k.dve, W_(13), lr[:], lr[:], ALU.mult, R, Wr)
    TT(k, k.dve, W_(14), li[:], li[:], ALU.mult, R, Wr)
    TT(k, k.dve, W_(13), W_(13), W_(14), ALU.add, R, Wr)
    k.op(k.dve, lambda h: h.reciprocal(out=W_(13), in_=W_(13)), reads=R, writes=Wr)
    TS(k, k.dve, W_(11), W_(11), -1.0, ALU.add, R, Wr)
    TT(k, k.dve, W_(14), W_(11), lr[:], ALU.mult, R, Wr)
    TT(k, k.dve, W_(15), W_(12), li[:], ALU.mult, R, Wr)
    TT(k, k.dve, W_(14), W_(14), W_(15), ALU.add, R, Wr)
    TT(k, k.dve, W_(14), W_(14), W_(13), ALU.mult, R, Wr)
    TT(k, k.dve, W_(15), W_(12), lr[:], ALU.mult, R, Wr)
    TT(k, k.dve, W_(16), W_(11), li[:], ALU.mult, R, Wr)
    TT(k, k.dve, W_(15), W_(15), W_(16), ALU.subtract, R, Wr)
    TT(k, k.dve, W_(15), W_(15), W_(13), ALU.mult, R, Wr)
    crb = W_(14).unsqueeze(2).to_broadcast([128, 16, 16])
    cib = W_(15).unsqueeze(2).to_broadcast([128, 16, 16])
    bre = bri[:, 0, :].rearrange("p (g h) -> p g h", g=16)
    bim = bri[:, 1, :].rearrange("p (g h) -> p g h", g=16)
    bb = sbt([128, 4, 256], F32, "bb")
    B3 = lambda i: bb[:, i, :].rearrange("p (g h) -> p g h", g=16)
    TT(k, k.dve, B3(0), bre, crb, ALU.mult, R, Wr)
    TT(k, k.dve, B3(1), bim, cib, ALU.mult, R, Wr)
    TT(k, k.dve, B3(0), B3(0), B3(1), ALU.subtract, R, Wr)
    TT(k, k.dve, B3(2), bim, crb, ALU.mult, R, Wr)
    TT(k, k.dve, B3(3), bre, cib, ALU.mult, R, Wr)
    TT(k, k.dve, B3(2), B3(2), B3(3), ALU.add, R, Wr)
    bst = sbt([128, 2, 256], BF16, "bst")
    CP(k, k.dve, bst[0:64, 0, :], bb[0:64, 0, :], R, Wr)
    CP(k, k.dve, bst[64:128, 0, :], bb[64:128, 2, :], R, Wr)
    CP(k, k.dve, bst[0:64, 1, :], bb[0:64, 2, :], R, Wr)
    CP(k, k.dve, bst[64:128, 1, :], bb[64:128, 0, :], R, Wr)
    for v in range(2):
        for ch in range(2):
            TR(k, pT1[:, 0:128], bst[:, v, ch * 128:(ch + 1) * 128], env["ident"][:], [pb, env["ident_b"]], [pT1_b])
            for gg in range(8):
                TS(k, k.dve, bpad[:, v, ch * 8 + gg, :], pT1[:, 0:128], C["rm16"][:, gg:gg + 1], ALU.mult,
                   [pT1_b, cb], [pb])
    cst = sbt([128, 2, 256], F32, "cst")
    CP(k, k.dve, cst[0:64, 0, :], cri[0:64, 0, :], R, Wr)
    TS(k, k.dve, cst[64:128, 0, :], cri[64:128, 1, :], -1.0, ALU.mult, R, Wr)
    TS(k, k.dve, cst[0:64, 1, :], cri[0:64, 1, :], -1.0, ALU.mult, R, Wr)
    CP(k, k.dve, cst[64:128, 1, :], cri[64:128, 0, :], R, Wr)
    MSET(k, k.pool, cpad[:], 0.0, Wr)
    for v in range(2):
        for g in range(16):
            CP(k, k.pool, cpad[:, v, g, (g % 8) * 16:(g % 8) * 16 + 16], cst[:, v, g * 16:(g + 1) * 16], R, Wr)
    CP(k, k.dve, wl[:, 0, 0, :], W_(8), R, Wr)
    TS(k, k.dve, wl[:, 0, 1, :], W_(6), C["sgn"][:, 0:1], ALU.mult, R + [cb], Wr)
    MSET(k, k.pool, tabc[:, :, 0:1], 1.0, Wr); MSET(k, k.pool, tabs[:, :, 0:1], 0.0, Wr)
    t1 = sbt([128, 16, 32], F32, "dbl1"); t2 = sbt([128, 16, 32], F32, "dbl2")
    n = 1
    for lev in range(6):
        wc = wl[:, lev, 0, :].unsqueeze(2).to_broadcast([128, 16, n])
        ws = wl[:, lev, 1, :].unsqueeze(2).to_broadcast([128, 16, n])
        TT(k, k.pool, t1[:, :, 0:n], tabs[:, :, 0:n], ws, ALU.mult, R, Wr)
        TT(k, k.pool, t2[:, :, 0:n], tabc[:, :, 0:n], wc, ALU.mult, R, Wr)
        TT(k, k.pool, tabc[:, :, n:2 * n], t2[:, :, 0:n], t1[:, :, 0:n], ALU.subtract, R, Wr)
        TT(k, k.pool, t1[:, :, 0:n], tabc[:, :, 0:n], ws, ALU.mult, R, Wr)
        TT(k, k.pool, t2[:, :, 0:n], tabs[:, :, 0:n], wc, ALU.mult, R, Wr)
        TT(k, k.pool, tabs[:, :, n:2 * n], t1[:, :, 0:n], t2[:, :, 0:n], ALU.add, R, Wr)
        TT(k, k.dve, W_(17), wl[:, lev, 0, :], wl[:, lev, 0, :], ALU.mult, R, Wr)
        TT(k, k.dve, W_(18), wl[:, lev, 1, :], wl[:, lev, 1, :], ALU.mult, R, Wr)
        TT(k, k.dve, wl[:, lev + 1, 0, :], W_(17), W_(18), ALU.subtract, R, Wr)
        STT(k, wl[:, lev + 1, 1, :], wl[:, lev, 0, :], 2.0, wl[:, lev, 1, :], ALU.mult, ALU.mult, R, Wr)
        n *= 2
    P.update(rho=rho, bpad=bpad, cpad=cpad, tabc=tabc, tabs=tabs, c64=wl[:, 6, 0, :], s64=wl[:, 6, 1, :], rhot=rhot)
    CP(k, k.pool, rhot[:], rho[:].unsqueeze(2).to_broadcast([128, 16, 64]), R, Wr)
    MSET(k, k.pool, rhot[:, :, 0:1], 0.0, Wr)
    k.pop()
    return P


def mixer_pass(k, env, l, x_src, x_src_b, x_dst, x_dst_b, C, dbg=None):
    nc = k.nc
    NS, TT_, NMT = 2, 256, 16
    banks, bank_b = env["banks"], env["bank_b"]
    pT, pT_b, pT1, pT1_b = env["pT"], env["pT_b"], env["pT1"], env["pT1_b"]
    ident, ident_b = env["ident"], env["ident_b"]
    cb = C["b"]
    hm, tri, tri64 = C["hm"], C["tri"], C["tri64"]
    k.push()
    sbase = nc.sbuf_bytes_remaining
    nm = lambda s_: "m%d_%s" % (l, s_)

    win = k.sb([128, 8, NCOL], BF16, nm("win")); win_b = [Buf("win%d" % i) for i in range(8)]
    wind = env["w_in_ext"][l].rearrange("(k p) n -> p k n", p=128)
    for i in range(8):
        k.dma(k.pool, win[:, i, :], wind[:, i, :], writes=[win_b[i]])
    wout = k.sb([128, 8, D], BF16, nm("wout")); wout_b = Buf("wout")
    k.dma(k.pool, wout[:], env["w_out"][l].rearrange("(k p) n -> p k n", p=128), writes=[wout_b])
    gcol = k.sb([128, 8], F32, nm("gcol")); gcol_b = Buf("gcol")
    k.dma(k.sp, gcol[:], env["ln_mix_pre_col"][l], writes=[gcol_b])
    gpost = k.sb([128, D], F32, nm("gpost")); gpost_b = Buf("gpost")
    k.dma(k.sp, gpost[:], env["ln_mix_post"][l:l + 1, :].partition_broadcast(128), writes=[gpost_b])
    kc = k.sb([128, 2, S], BF16, nm("kc")); kc_b = [Buf("kc%d" % i) for i in range(NMT)]
    vc = k.sb([128, 32, 4, 65], BF16, nm("vc")); vc_b = [Buf("vc%d" % i) for i in range(NMT)]
    vc1_b = Buf("vc_ones")
    MSET(k, k.pool, vc[:, :, :, 64:65], 1.0, [vc1_b])
    P = mixer_prep(k, env, l, C) if "prep" not in env.get("skip", ()) else {"b": Buf("prep")}
    pb = P["b"]

    def T(shape, dt, s_):
        return k.sb(shape, dt, nm(s_)), Buf(s_)

    xin, xin_b = T([128, NS, D], F32, "xin")
    hT, hT_b = T([128, 8, TT_], BF16, "hT")
    mixT, mixT_b = T([128, 8, TT_], BF16, "mixT")
    nt = NormT(k, NS, ident, ident_b, pT, pT_b, nm("nt"))
    rc, rc_b = T([128, TT_], F32, "rc"); rs, rs_b = T([128, TT_], F32, "rs")
    tmpo, tmpo_b = T([128, D], F32, "tmpo")
    st2, st2_b = T([128, 8], F32, "st2")
    f1, f1_b = T([128, TT_], F32, "f1"); f2, f2_b = T([128, TT_], F32, "f2"); f3, f3_b = T([128, TT_], F32, "f3")
    f4, f4_b = T([128, TT_], F32, "f4"); f5, f5_b = T([128, TT_], F32, "f5")
    gq, gq_b = T([128, TT_], F32, "gq"); gk, gk_b = T([128, TT_], F32, "gk")
    glr, glr_b = T([16, TT_], BF16, "glr")
    qm, qm_b = T([128, 2, 4, TT_], BF16, "qm")
    uf, uf_b = T([128, 2, TT_], F32, "uf"); ub, ub_b = T([128, 2, TT_], BF16, "ub")
    rxh, rxh_b = T([128, 2, 3 + TT_], F32, "rxh"); rg, rg_b = T([128, 2, TT_], F32, "rg")
    gvc, gvc_b = T([64, 4, 256], BF16, "gvc"); ogc, ogc_b = T([64, 4, 256], F32, "ogc")
    qdm, qdm_b = T([128, 4, TT_], BF16, "qdm"); kd, kd_b = T([128, TT_], BF16, "kd"); ks, ks_b = T([128, TT_], BF16, "ks")
    dec, dec_b = T([128, 4], F32, "dec")
    Sst, Sst_b = T([128, 64], F32, "Sst"); Sbf, Sbf_b = T([128, 2, 64], BF16, "Sbf")
    kst, kst_b = T([64, 128], BF16, "kst"); AT, AT_b = T([64, 4, 64], BF16, "AT")
    kvr, kvr_b = T([128, 64], F32, "kvr")
    osb, osb_b = T([64, 256], F32, "osb"); sil, sil_b = T([64, 256], F32, "sil")
    gst, gst_b = T([64, 12], F32, "gst"); otk, otk_b = T([64, 256], BF16, "otk")
    pt = [T([128, 2 * TT_], BF16, "pt%d" % i) for i in range(2)]
    od, od_b = T([128, 2, 256], F32, "od"); odb, odb_b = T([128, 2, 256], BF16, "odb")
    dst, dst_b = T([128, 32], F32, "dst"); dt1, dt1_b = T([128, 64], F32, "dt1")
    zlast, zlast_b = T([128, 16], F32, "zlast"); zin, zin_b = T([128, 16], F32, "zin"); zt, zt_b = T([128, 16], F32, "zt")
    m1, m1_b = T([128, 256], F32, "m1"); m2, m2_b = T([128, 256], F32, "m2"); zz, zz_b = T([128, 256], F32, "zz")
    zc, zc_b = T([128, 256], BF16, "zc"); zs, zs_b = T([128, 256], BF16, "zs")
    sf1, sf1_b = T([128, TT_], F32, "sf1"); sf2, sf2_b = T([128, TT_], F32, "sf2")
    sf4, sf4_b = T([128, TT_], F32, "sf4"); sf5, sf5_b = T([128, TT_], F32, "sf5")
    yb, yb_b = T([128, 2, TT_], BF16, "yb")
    xcb, xcb_b = T([128, TT_], BF16, "xcb"); hst, hst_b = T([128, 2], F32, "hst")
    odq, odq_b = tmpo[:, 0:512].rearrange("p (s f) -> p s f", s=2), tmpo_b
    kvm, kvm_b = tmpo[:, 512:768], tmpo_b
    osq, osq_b = tmpo[0:64, 768:1024], tmpo_b
    MSET(k, k.pool, Sst[:], 0.0, [Sst_b]); MSET(k, k.pool, Sbf[:], 0.0, [Sbf_b])
    MSET(k, k.pool, zlast[:], 0.0, [zlast_b]); MSET(k, k.pool, hst[:], 0.0, [hst_b]); MSET(k, k.pool, rxh[:, :, 0:3], 0.0, [rxh_b])
    env["sbuf_used_mix"] = sbase - nc.sbuf_bytes_remaining

    xs = x_src.rearrange("(m s p) d -> m p s d", s=NS, p=128)
    xd = x_dst.rearrange("(m s p) d -> m p s d", s=NS, p=128)
    B = lambda i: banks[i]
    Bb = lambda i: bank_b[i]
    QS = 32.0 ** -0.5
    sbi = [0]

    def inproj_fm(off, width, bank):
        for kk in range(8):
            MM(k, B(bank)[0:width, 0:TT_], win[:, kk, off:off + width], hT[:, kk, :], kk == 0, kk == 7,
               [win_b[kk], hT_b], [Bb(bank)])

    for mt in range(env.get("nmt", NMT)):
        t0 = mt * TT_
        k.dma(k.sp, xin[:], xs[mt], reads=[x_src_b[mt]], writes=[xin_b])
        k.dma(k.sp, rc[:], env["rope_cos"][:, t0:t0 + TT_], writes=[rc_b])
        k.dma(k.sp, rs[:], env["rope_ssin"][:, t0:t0 + TT_], writes=[rs_b])
        if "norm" not in env.get("skip", ()):
            nt.run(xin, xin_b, gcol, gcol_b, hT, hT_b)

        def _sec_inproj():
            ipon = lambda i: env.get('ip_only') is None or i in env['ip_only']
            if ipon(0): inproj_fm(OFF_GQ, 128, 0); ACT(k, gq[:], B(0)[:, 0:TT_], AF.Copy, [Bb(0)], [gq_b])
            if ipon(1): inproj_fm(OFF_GK, 128, 1); ACT(k, gk[:], B(1)[:, 0:TT_], AF.Copy, [Bb(1)], [gk_b])
            for j in (range(2) if ipon(2) else ()):
                inproj_fm(OFF_DQ + 128 * j, 128, 0); inproj_fm(OFF_DQS + 128 * j, 128, 1)
                TT(k, k.dve, f1[:], B(0)[:, 0:TT_], rc[:], ALU.mult, [Bb(0), rc_b], [f1_b])
                TT(k, k.dve, f2[:], B(1)[:, 0:TT_], rs[:], ALU.mult, [Bb(1), rs_b], [f2_b])
                TT(k, k.pool, f3[:], f1[:], f2[:], ALU.add, [f1_b, f2_b], [f3_b])
                for i in range(4):
                    TS(k, k.pool if i % 2 else k.dve, qm[:, j, i, :], f3[:], hm[:, i:i + 1], ALU.mult, [f3_b, cb], [qm_b])
            for j in (range(2) if ipon(3) else ()):
                inproj_fm(OFF_DK + 128 * j, 128, 0); inproj_fm(OFF_DKS + 128 * j, 128, 1)
                TT(k, k.dve, f1[:], B(0)[:, 0:TT_], rc[:], ALU.mult, [Bb(0), rc_b], [f1_b])
                TT(k, k.dve, f2[:], B(1)[:, 0:TT_], rs[:], ALU.mult, [Bb(1), rs_b], [f2_b])
                TT(k, k.pool, kc[:, j, t0:t0 + TT_], f1[:], f2[:], ALU.add, [f1_b, f2_b], [kc_b[mt]])
            for j in (range(2) if ipon(4) else ()):
                inproj_fm(OFF_SU + 128 * j, 128, j)
                ACT(k, uf[:, j, :], B(j)[:, 0:TT_], AF.Copy, [Bb(j)], [uf_b])
                CP(k, k.pool, ub[:, j, :], uf[:, j, :], [uf_b], [ub_b])
            for j in (range(2) if ipon(5) else ()):
                inproj_fm(OFF_RX + 128 * j, 128, j)
                ACT(k, rxh[:, j, 3:3 + TT_], B(j)[:, 0:TT_], AF.Copy, [Bb(j)], [rxh_b])
            for j in (range(2) if ipon(6) else ()):
                inproj_fm(OFF_RG + 128 * j, 128, j)
                ACT(k, rg[:, j, :], B(j)[:, 0:TT_], AF.Copy, [Bb(j)], [rg_b])
            if ipon(7): inproj_fm(OFF_GLR, 16, 0); ACT(k, glr[:], B(0)[0:16, 0:TT_], AF.Copy, [Bb(0)], [glr_b])
            for n in (range(4) if ipon(8) else ()):
                bk = n % 2
                for kk in range(8):
                    MM(k, B(bk)[0:64, 0:512], hT[:, kk, n * 64:(n + 1) * 64], win[:, kk, OFF_TMG:OFF_TMG + 512],
                       kk == 0, kk == 7, [win_b[kk], hT_b], [Bb(bk)])
                ACT(k, gvc[:, n, :], B(bk)[0:64, 0:256], AF.Copy, [Bb(bk)], [gvc_b])
                CP(k, k.dve, ogc[:, n, :], B(bk)[0:64, 256:512], [Bb(bk)], [ogc_b])
            for s in (range(NS) if ipon(9) else ()):
                bk = s % 2
                for kk in range(8):
                    MM(k, B(bk)[:, 0:256], hT[:, kk, s * 128:(s + 1) * 128], win[:, kk, OFF_DV:OFF_DV + 256],
                       kk == 0, kk == 7, [win_b[kk], hT_b], [Bb(bk)])
                CP(k, k.dve, vc[:, 2 * mt + s, :, 0:64], B(bk)[:, 0:256].rearrange("p (h v) -> p h v", h=4),
                   [Bb(bk)], [vc_b[mt]])

        if "inproj" not in env.get("skip", ()):
            _sec_inproj()
        def _sec_gla():
            MM(k, B(0)[:, 0:TT_], P["wgate"][:], glr[:], True, True, [pb, glr_b], [Bb(0)])
            ACT(k, f1[:], B(0)[:, 0:TT_], AF.Exp, [Bb(0), pb], [f1_b], scale=-1.0, bias=P["negb"][:])
            ACT(k, f2[:], f1[:], AF.Ln, [f1_b, cb], [f2_b], bias=C["one"][:])
            k.op(k.dve, lambda h: h.tensor_tensor_scan(out=f3[:], data0=C["reset"][:], data1=f2[:], initial=0.0,
                                                       op0=ALU.mult, op1=ALU.add), reads=[f2_b, cb], writes=[f3_b])
            f3v = f3[:].rearrange("p (n c) -> p n c", n=4)
            ACT(k, f1[:], f3[:], AF.Exp, [f3_b], [f1_b], scale=-1.0 / 16.0)
            ACT(k, f2[:], f3[:], AF.Exp, [f3_b], [f2_b], scale=1.0 / 16.0)
            TT(k, k.dve, f4[:].rearrange("p (n c) -> p n c", n=4), f3v, f3v[:, :, 63:64].to_broadcast([128, 4, 64]),
               ALU.subtract, [f3_b], [f4_b])
            ACT(k, f5[:], f4[:], AF.Exp, [f4_b], [f5_b], scale=1.0 / 16.0)
            ACT(k, dec[:], f3v[:, :, 63], AF.Exp, [f3_b], [dec_b], scale=-1.0 / 16.0)
            STT(k, f4[:], gq[:], 32.0 ** -0.5, f1[:], ALU.mult, ALU.mult, [gq_b, f1_b], [f4_b])
            for h_ in range(4):
                TS(k, k.pool if h_ % 2 else k.dve, qdm[:, h_, :], f4[:], hm[:, h_:h_ + 1], ALU.mult, [f4_b, cb], [qdm_b])
            TT(k, k.dve, kd[:], gk[:], f2[:], ALU.mult, [gk_b, f2_b], [kd_b])
            TT(k, k.pool, ks[:], gk[:], f5[:], ALU.mult, [gk_b, f5_b], [ks_b])
            yield
            for n in range(4):
                c0 = n * 64
                gn = mt * 4 + n
                sp_, sn_ = gn % 2, (gn + 1) % 2
                TR(k, pT1[0:64, 0:128], ks[:, c0:c0 + 64], ident[:], [ks_b, ident_b], [pT1_b])
                CP(k, k.dve, kst[:], pT1[0:64, 0:128], [pT1_b], [kst_b])
                for h_ in range(4):
                    MM(k, B(0)[0:64, h_ * 64:(h_ + 1) * 64], kd[:, c0:c0 + 64], qdm[:, h_, c0:c0 + 64], True, True,
                       [kd_b, qdm_b], [Bb(0)])
                TT(k, k.dve, AT[:], B(0)[0:64, 0:256].rearrange("p (h c) -> p h c", h=4),
                   tri64[:].unsqueeze(1).to_broadcast([64, 4, 64]), ALU.mult, [Bb(0), cb], [AT_b])
                yield
                MM(k, B(1)[:, 0:256], kst[:], gvc[:, n, :], True, True, [kst_b, gvc_b], [Bb(1)])
                for h_ in range(4):
                    MM(k, B(0)[0:64, 256 + h_ * 64:256 + (h_ + 1) * 64], AT[:, h_, :], gvc[:, n, h_ * 64:(h_ + 1) * 64], True, False,
                       [AT_b, gvc_b], [Bb(0)])
                    MM(k, B(0)[0:64, 256 + h_ * 64:256 + (h_ + 1) * 64], qdm[:, h_, c0:c0 + 64], Sbf[:, sp_, :], False, True,
                       [qdm_b, Sbf_b], [Bb(0)])
                yield
                TT(k, k.dve, kvm.rearrange("p (h v) -> p h v", h=4), B(1)[:, 0:256].rearrange("p (h v) -> p h v", h=4),
                   hm[:].unsqueeze(2).to_broadcast([128, 4, 64]), ALU.mult, [Bb(1), cb], [kvm_b])
                k.op(k.dve, lambda h: h.reduce_sum(out=kvr[:], in_=kvm.rearrange("p (h v) -> p v h", h=4), axis=AX.X),
                     reads=[kvm_b], writes=[kvr_b])
                STT(k, Sst[:], Sst[:], dec[:, n:n + 1], kvr[:], ALU.mult, ALU.add, [Sst_b, dec_b, kvr_b], [Sst_b])
                CP(k, k.pool, Sbf[:, sn_, :], Sst[:], [Sst_b], [Sbf_b])
                yield
                ACT(k, osb[:], B(0)[0:64, 256:512], AF.Copy, [Bb(0)], [osb_b])
                TT(k, k.pool, osq, osb[:], osb[:], ALU.mult, [osb_b], [osq_b])
                k.op(k.dve, lambda h: h.reduce_sum(out=gst[:, 0:4], in_=osq.rearrange("p (h v) -> p h v", h=4), axis=AX.X),
                     reads=[osq_b], writes=[gst_b])
                ACT(k, gst[:, 4:8], gst[:, 0:4], AF.Ln, [gst_b], [gst_b], scale=1.0 / 64.0, bias=k.eps_ap[0:64, :])
                ACT(k, gst[:, 8:12], gst[:, 4:8], AF.Exp, [gst_b], [gst_b], scale=-0.5)
                ACT(k, sil[:], ogc[:, n, :], AF.Silu, [ogc_b], [sil_b])
                TT(k, k.dve, osb[:].rearrange("p (h v) -> p h v", h=4), osb[:].rearrange("p (h v) -> p h v", h=4),
                   gst[:, 8:12].unsqueeze(2).to_broadcast([64, 4, 64]), ALU.mult, [osb_b, gst_b], [osb_b])
                TT(k, k.pool, sil[:].rearrange("p (h v) -> p h v", h=4), sil[:].rearrange("p (h v) -> p h v", h=4),
                   P["gnorm"][0:64, :].unsqueeze(1).to_broadcast([64, 4, 64]), ALU.mult, [sil_b, pb], [sil_b])
                yield
                TT(k, k.dve, otk[:], osb[:], sil[:], ALU.mult, [osb_b, sil_b], [otk_b])
                for j in range(2):
                    TR(k, pT1[:, 128 + j * 64:128 + (j + 1) * 64], otk[:, j * 128:(j + 1) * 128], ident[0:64, 0:64],
                       [otk_b, ident_b], [pT1_b])
                yield
                CP(k, k.dve, mixT[:, 0:2, c0:c0 + 64], pT1[:, 128:256].rearrange("p (j c) -> p j c", j=2), [pT1_b], [mixT_b])

        def _sec_diff():
            for h_ in range(4):
                j = h_ // 2
                for c in range(2):
                    qsl = qm[:, j, (h_ % 2) * 2 + c, :]
                    def emit_s(i):
                        sb_ = sbi[0] % 2
                        sbi[0] += 1
                        bk, ptt, ptb = B(2 + sb_), pt[sb_][0], pt[sb_][1]
                        diag = (i == mt)
                        for u in range(2):
                            kt = 2 * i + u
                            qlo = 128 if (diag and u == 1) else 0
                            MM(k, bk[:, u * TT_ + qlo:(u + 1) * TT_], kc[:, j, kt * 128:(kt + 1) * 128], qsl[:, qlo:TT_],
                               True, True, [kc_b[i], qm_b], [Bb(2 + sb_)])
                        if not diag:
                            ACT(k, ptt[:, 0:2 * TT_], bk[:, 0:2 * TT_], AF.Exp, [Bb(2 + sb_)], [ptb], scale=QS)
                        else:
                            ACT(k, ptt[:, 0:TT_], bk[:, 0:TT_], AF.Exp, [Bb(2 + sb_)], [ptb], scale=QS)
                            ACT(k, ptt[:, TT_ + 128:2 * TT_], bk[:, TT_ + 128:2 * TT_], AF.Exp, [Bb(2 + sb_)], [ptb], scale=QS)
                            TT(k, k.pool, ptt[:, 0:128], ptt[:, 0:128], tri[:], ALU.mult, [ptb, cb], [ptb])
                            TT(k, k.pool, ptt[:, TT_ + 128:2 * TT_], ptt[:, TT_ + 128:2 * TT_], tri[:], ALU.mult, [ptb, cb], [ptb])
                        return sb_

                    def emit_pv(i, sb_):
                        ptt, ptb = pt[sb_][0], pt[sb_][1]
                        for u in range(2):
                            kt = 2 * i + u
                            for s in range(NS):
                                if kt > 2 * mt + s:
                                    continue
                                col = (c * 2 + s) * 65
                                MM(k, B(4)[:, col:col + 65], ptt[:, u * TT_ + s * 128:u * TT_ + (s + 1) * 128],
                                   vc[:, kt, h_, :], kt == 0 and c == 0 and s == 0, kt == 2 * mt + s,
                                   [ptb, vc_b[i], vc1_b], [Bb(4)], sgc=True)

                    prev = None
                    for i in range(mt + 1):
                        sb_ = emit_s(i)
                        if prev is not None:
                            emit_pv(*prev)
                        prev = (i, sb_)
                        yield
                    emit_pv(*prev)
                    yield
                for s in range(NS):
                    c1, c2 = (0 * 2 + s) * 65, (1 * 2 + s) * 65
                    k.op(k.dve, lambda h: h.reciprocal(out=dst[:, 0:1], in_=B(4)[:, c1 + 64:c1 + 65]), reads=[Bb(4)], writes=[dst_b])
                    k.op(k.dve, lambda h: h.reciprocal(out=dst[:, 1:2], in_=B(4)[:, c2 + 64:c2 + 65]), reads=[Bb(4)], writes=[dst_b])
                    TT(k, k.dve, dst[:, 2:3], dst[:, 1:2], P["nlam"], ALU.mult, [dst_b, pb], [dst_b])
                    TS(k, k.dve, dt1[:], B(4)[:, c1:c1 + 64], dst[:, 0:1], ALU.mult, [Bb(4), dst_b], [dt1_b])
                    STT(k, od[:, s, h_ * 64:(h_ + 1) * 64], B(4)[:, c2:c2 + 64], dst[:, 2:3], dt1[:], ALU.mult, ALU.add,
                        [Bb(4), dst_b, dt1_b], [od_b])
            yield
            TT(k, k.pool, odq, od[:], od[:], ALU.mult, [od_b], [odq_b])
            k.op(k.dve, lambda h: h.reduce_sum(out=dst[:, 8:16], in_=odq.rearrange("p s (h v) -> p (s h) v", h=4), axis=AX.X),
                 reads=[odq_b], writes=[dst_b])
            ACT(k, dst[:, 16:24], dst[:, 8:16], AF.Ln, [dst_b], [dst_b], scale=1.0 / 64.0, bias=k.eps_ap)
            ACT(k, dst[:, 24:32], dst[:, 16:24], AF.Exp, [dst_b], [dst_b], scale=-0.5)
            TT(k, k.dve, od[:].rearrange("p s (h v) -> p (s h) v", h=4), od[:].rearrange("p s (h v) -> p (s h) v", h=4),
               dst[:, 24:32].unsqueeze(2).to_broadcast([128, 8, 64]), ALU.mult, [od_b, dst_b], [od_b])
            TT(k, k.dve, odb[:].rearrange("p s (h v) -> p (s h) v", h=4), od[:].rearrange("p s (h v) -> p (s h) v", h=4),
               P["dnorm"][:].unsqueeze(1).to_broadcast([128, 8, 64]), ALU.mult, [od_b, pb], [odb_b])
            for s in range(NS):
                for j in range(2):
                    TR(k, pT1[:, 256 + (s * 2 + j) * 128:256 + (s * 2 + j + 1) * 128], odb[:, s, j * 128:(j + 1) * 128], ident[:],
                       [odb_b, ident_b], [pT1_b])
            for s in range(NS):
                CP(k, k.dve, mixT[:, 2:4, s * 128:(s + 1) * 128],
                   pT1[:, 256 + s * 256:256 + (s + 1) * 256].rearrange("p (j c) -> p j c", j=2), [pT1_b], [mixT_b])

        def _sec_s5():
            for qt in range(4):
                q0 = qt * 64
                MM(k, B(1)[:, 256:272], C["jf"][:], zlast[:], True, True, [cb, zlast_b], [Bb(1)])
                TT(k, k.dve, zt[:], B(1)[:, 256:272], P["s64"], ALU.mult, [Bb(1), pb], [zt_b])
                TT(k, k.dve, zin[:], zlast[:], P["c64"], ALU.mult, [zlast_b, pb], [zin_b])
                TT(k, k.dve, zin[:], zin[:], zt[:], ALU.subtract, [zin_b, zt_b], [zin_b])
                TT(k, k.dve, zin[:], zin[:], P["rho"][:], ALU.mult, [zin_b, pb], [zin_b])
                for ch in range(2):
                    for hb in range(2):
                        g0 = 8 * ch + 4 * hb
                        for gg in range(4):
                            MM(k, B(5)[:, gg * 64:(gg + 1) * 64], P["bpad"][:, 0, g0 + gg, :], ub[:, ch, q0:q0 + 64],
                               True, True, [pb, ub_b], [Bb(5)])
                            MM(k, B(5)[:, 256 + gg * 64:256 + (gg + 1) * 64], P["bpad"][:, 1, g0 + gg, :], ub[:, ch, q0:q0 + 64],
                               True, True, [pb, ub_b], [Bb(5)])
                        tc4 = P["tabc"][:, g0:g0 + 4, :].rearrange("p g j -> p (g j)")
                        ts4 = P["tabs"][:, g0:g0 + 4, :].rearrange("p g j -> p (g j)")
                        TT(k, k.dve, m1[:], B(5)[:, 0:256], tc4, ALU.mult, [Bb(5), pb], [m1_b])
                        TT(k, k.dve, m2[:], B(5)[:, 256:512], ts4, ALU.mult, [Bb(5), pb], [m2_b])
                        yield
                        TT(k, k.pool, m1[:], m1[:], m2[:], ALU.add, [m1_b, m2_b], [m1_b])
                        m1v = m1[:].rearrange("p (g j) -> p g j", g=4)
                        TT(k, k.dve, m1v[:, :, 0], m1v[:, :, 0], zin[:, g0:g0 + 4], ALU.add, [m1_b, zin_b], [m1_b])
                        k.op(k.dve, lambda h: h.tensor_tensor_scan(
                            out=zz[:], data0=P["rhot"][:, g0:g0 + 4, :].rearrange("p g j -> p (g j)"), data1=m1[:],
                            initial=0.0, op0=ALU.mult, op1=ALU.add), reads=[pb, m1_b], writes=[zz_b])
                        yield
                        CP(k, k.pool, zlast[:, g0:g0 + 4], zz[:].rearrange("p (g j) -> p g j", g=4)[:, :, 63], [zz_b], [zlast_b])
                        TT(k, k.pool, zc[:], zz[:], tc4, ALU.mult, [zz_b, pb], [zc_b])
                        TT(k, k.dve, zs[:], zz[:], ts4, ALU.mult, [zz_b, pb], [zs_b])
                        yield
                        yo = ch * TT_ + q0
                        for gg in range(4):
                            MM(k, B(6)[:, yo:yo + 64], P["cpad"][:, 0, g0 + gg, :], zc[:, gg * 64:(gg + 1) * 64],
                               hb == 0 and gg == 0, False, [pb, zc_b], [Bb(6)])
                            MM(k, B(6)[:, yo:yo + 64], P["cpad"][:, 1, g0 + gg, :], zs[:, gg * 64:(gg + 1) * 64],
                               False, hb == 1 and gg == 3, [pb, zs_b], [Bb(6)])
                        yield
            for ch in range(2):
                STT(k, sf1[:], uf[:, ch, :], P["lv"][:, 4, ch:ch + 1], B(6)[:, ch * TT_:(ch + 1) * TT_], ALU.mult, ALU.add,
                    [uf_b, pb, Bb(6)], [sf1_b])
                ACT(k, sf2[:], sf1[:], AF.Gelu_apprx_tanh, [sf1_b], [sf2_b])
                CP(k, k.pool, yb[:, ch, :], sf2[:], [sf2_b], [yb_b])
                if ch == 0:
                    CP(k, k.dve, sf4[:], sf2[:], [sf2_b], [sf4_b])
                else:
                    CP(k, k.dve, sf5[:], sf2[:], [sf2_b], [sf5_b])
                yield
            for nch in range(2):
                for kk in range(2):
                    MM(k, B(5)[:, nch * TT_:(nch + 1) * TT_], P["wglu"][:, kk, nch * 128:(nch + 1) * 128],
                       yb[:, kk, :], kk == 0, kk == 1, [pb, yb_b], [Bb(5)])
                ACT(k, sf1[:], B(5)[:, nch * TT_:(nch + 1) * TT_], AF.Sigmoid, [Bb(5), pb], [sf1_b], bias=P["bglu"][:, nch:nch + 1])
                src, src_b = (sf4, sf4_b) if nch == 0 else (sf5, sf5_b)
                TT(k, k.dve, mixT[:, 4 + nch, :], src[:], sf1[:], ALU.mult, [src_b, sf1_b], [mixT_b])
                yield

        def _sec_lru():
            for ch in range(2):
                cw = P["cw"]
                TS(k, k.dve, f1[:], rxh[:, ch, 3:3 + TT_], cw[:, ch, 3:4], ALU.mult, [rxh_b, pb], [f1_b],
                   s2=P["lv"][:, 0, ch:ch + 1], op1=ALU.add)
                for tap in range(3):
                    STT(k, f1[:], rxh[:, ch, tap:tap + TT_], cw[:, ch, tap:tap + 1], f1[:], ALU.mult, ALU.add,
                        [rxh_b, pb, f1_b], [f1_b])
                CP(k, k.pool, xcb[:], f1[:], [f1_b], [xcb_b])
                yield
                MM(k, B(0)[:, 0:TT_], P["wa"][:, ch, :], xcb[:], True, True, [pb, xcb_b], [Bb(0)])
                MM(k, B(1)[:, 0:TT_], P["wx"][:, ch, :], xcb[:], True, True, [pb, xcb_b], [Bb(1)])
                ACT(k, f2[:], B(0)[:, 0:TT_], AF.Sigmoid, [Bb(0), pb], [f2_b], bias=P["lv"][:, 1, ch:ch + 1])
                ACT(k, f3[:], B(1)[:, 0:TT_], AF.Sigmoid, [Bb(1), pb], [f3_b], bias=P["lv"][:, 2, ch:ch + 1])
                ACT(k, f4[:], f2[:], AF.Exp, [f2_b, pb], [f4_b], scale=P["lsp"][:, 2, ch:ch + 1])
                ACT(k, f5[:], f2[:], AF.Exp, [f2_b, pb], [f5_b], scale=P["lsp"][:, 3, ch:ch + 1])
                yield
                TS(k, k.dve, f5[:], f5[:], -1.0, ALU.mult, [f5_b], [f5_b], s2=1.0, op1=ALU.add)
                TS(k, k.dve, f5[:], f5[:], 1e-12, ALU.max, [f5_b], [f5_b])
                ACT(k, f5[:], f5[:], AF.Sqrt, [f5_b], [f5_b])
                TT(k, k.pool, f3[:], f3[:], f1[:], ALU.mult, [f3_b, f1_b], [f3_b])
                TT(k, k.dve, f3[:], f3[:], f5[:], ALU.mult, [f3_b, f5_b], [f3_b])
                yield
                k.op(k.dve, lambda h: h.tensor_tensor_scan(out=f2[:], data0=f4[:], data1=f3[:], initial=hst[:, ch:ch + 1],
                                                           op0=ALU.mult, op1=ALU.add),
                     reads=[f4_b, f3_b, hst_b], writes=[f2_b])
                CP(k, k.pool, hst[:, ch:ch + 1], f2[:, TT_ - 1:TT_], [f2_b], [hst_b])
                ACT(k, f1[:], rg[:, ch, :], AF.Gelu_apprx_tanh, [rg_b], [f1_b])
                TT(k, k.dve, mixT[:, 6 + ch, :], f2[:], f1[:], ALU.mult, [f2_b, f1_b], [mixT_b])
            yield
            CP(k, k.pool, rxh[:, :, 0:3], rxh[:, :, TT_:TT_ + 3], [rxh_b], [rxh_b])

        def _chain(*fs):
            for f_ in fs:
                for _ in f_():
                    yield
        SK = env.get("skip", ())
        streams = [_chain(*[f_ for n_, f_ in (("gla", _sec_gla), ("lru", _sec_lru)) if n_ not in SK])]
        if "diff" not in SK:
            streams.append(_sec_diff())
        if "s5" not in SK:
            streams.append(_sec_s5())
        if env.get("serial"):
            for g_ in streams:
                for _ in g_:
                    pass
        else:
            while streams:
                for g_ in list(streams):
                    try:
                        next(g_)
                    except StopIteration:
                        streams.remove(g_)

        if dbg is not None:
            k.dma(k.sp, dbg[:, :, t0:t0 + TT_].rearrange("j p t -> p j t"), mixT[:], reads=[mixT_b], writes=[Buf("dbg")])

        def _sec_out():
            for s in range(NS):
                for hf in range(2):
                    for kk in range(8):
                        MM(k, B(hf)[:, 0:512], mixT[:, kk, s * 128:(s + 1) * 128], wout[:, kk, hf * 512:(hf + 1) * 512],
                           kk == 0, kk == 7, [mixT_b, wout_b], [Bb(hf)])
                post_norm_residual(k, [B(0)[:, 0:512], B(1)[:, 0:512]], [Bb(0), Bb(1)], xin[:, s, :], xin_b, gpost, gpost_b,
                                   st2, st2_b, nt.junk, nt.junk_b, tmpo, tmpo_b, xin[:, s, :], xin_b)

        if "out" not in env.get("skip", ()):
            _sec_out()
        k.dma(k.sp, xd[mt], xin[:], reads=[xin_b], writes=[x_dst_b[mt]])
    k.pop()


WEIGHT_SPECS = [
    ("ln_mix_post", [L, D]), ("ln_ffn_post", [L, D]),
    ("ffn_w1", [L, D, DFF]), ("ffn_w2", [L, DFF, D]),
    ("ln_mix_pre_col", [L, 128, 8]), ("ln_ffn_pre_col", [L, 128, 8]),
    ("w_in_ext", [L, D, NCOL]), ("w_out", [L, D, D]),
    ("rope_cos", [128, S]), ("rope_ssin", [128, S]),
    ("gla_w_gate", [L, 16, 128]), ("gla_b_gate_col", [L, 128, 1]), ("gla_norm", [L, 64]),
    ("diff_lq1", [L, 32]), ("diff_lk1", [L, 32]), ("diff_lq2", [L, 32]), ("diff_lk2", [L, 32]), ("diff_norm", [L, 64]),
    ("s5_log_step", [L, 16]), ("s5_a_re_T", [L, 64, 16]), ("s5_a_im_T", [L, 64, 16]),
    ("s5_b_re_p", [L, 64, 256]), ("s5_b_im_p", [L, 64, 256]), ("s5_c_re_p", [L, 64, 256]), ("s5_c_im_p", [L, 64, 256]),
    ("s5_d_col", [L, 128, 2]), ("s5_w_glu", [L, 256, 256]), ("s5_b_glu_col", [L, 128, 2]),
    ("lru_conv_w_col", [L, 128, 8]), ("lru_conv_b_col", [L, 128, 2]), ("lru_w_a", [L, 4, 64, 64]), ("lru_w_x", [L, 4, 64, 64]),
    ("lru_b_a_col", [L, 128, 2]), ("lru_b_x_col", [L, 128, 2]), ("lru_lambda_col", [L, 128, 2]),
]


def build(mode="full", opts=None):
    nc = bass.Bass("TRN2", target_bir_lowering=False)
    env = dict(opts or {})
    x = nc.dram_tensor("x", [S, D], F32, kind="ExternalInput").ap()
    for name, shape in WEIGHT_SPECS:
        env[name] = nc.dram_tensor(name, shape, F32, kind="ExternalInput").ap()
    env["lru_conv_w_col"] = env["lru_conv_w_col"].rearrange("l p (c t) -> l p c t", c=2)
    y = nc.dram_tensor("y", [S, D], F32, kind="ExternalOutput").ap()
    k = K(nc)
    eps = k.sb([128, 1], F32, "eps_c")
    eps_b = Buf("eps")
    k.op(k.pool, lambda h: h.memset(eps[:], EPS), writes=[eps_b])
    k.eps_ap = eps[:]
    env["banks"] = [k.ps([128, 512], F32, "bank%d" % i) for i in range(7)]
    env["bank_b"] = [Buf("bank%d" % i, True) for i in range(7)]
    env["pT1"] = k.ps([128, 1024], BF16, "pT1")
    env["pT1_b"] = Buf("pT1", True)
    env["pT"], env["pT_b"] = env["pT1"], env["pT1_b"]
    env["ident"], env["ident_b"] = make_ident(k, nc)
    C = make_consts(k, env)
    k.barrier()

    NMT = 16
    mk = lambda p: [Buf("%s%d" % (p, i)) for i in range(NMT)]
    x_b, y_b = mk("x"), mk("y")
    dbg = None
    if mode == "ffn0":
        ffn_pass(k, env, 0, x, x_b, y, y_b)
    elif mode == "mix0":
        dbg = nc.dram_tensor("dbg", [8, 128, S], BF16, kind="ExternalOutput").ap()
        mixer_pass(k, env, 0, x, x_b, y, y_b, C, dbg=dbg)
    elif mode == "full":
        xa = nc.dram_tensor("xa", [S, D], F32, kind="Internal").ap()
        xb = nc.dram_tensor("xb", [S, D], F32, kind="Internal").ap()
        xa_b, xb_b = mk("xa"), mk("xb")
        mixer_pass(k, env, 0, x, x_b, xa, xa_b, C)
        ffn_pass(k, env, 0, xa, xa_b, xb, xb_b)
        mixer_pass(k, env, 1, xb, xb_b, xa, xa_b, C)
        ffn_pass(k, env, 1, xa, xa_b, y, y_b)
    k.finish(y_b)
    k.barrier()
    return nc, env


_CACHE = {}


def _rope_tables():
    inv = 10000.0 ** (-np.arange(0, 32, 2, dtype=np.float32) / 32.0)
    ang = np.arange(S, dtype=np.float32)[:, None] * inv[None, :].astype(np.float32)
    ang = ang.astype(np.float32)
    emb = np.concatenate([ang, ang], axis=-1)
    cos = np.cos(emb).astype(np.float32).T
    sin = np.sin(emb).astype(np.float32).T
    ssin = np.concatenate([-sin[:16], sin[16:]], axis=0)
    return np.tile(cos, (4, 1)), np.tile(ssin, (4, 1))


def _prep_weights(inputs):
    src = {k_: np.asarray(v) for k_, v in inputs.items()}
    col8 = lambda a: a.reshape(L, 8, 128).transpose(0, 2, 1)
    col2 = lambda a: a.reshape(L, 2, 128).transpose(0, 2, 1)
    src["ln_mix_pre_col"] = col8(src["ln_mix_pre"])
    src["ln_ffn_pre_col"] = col8(src["ln_ffn_pre"])
    w = src["w_in"]
    seg = lambda a, b: w[:, :, a:b]

    def swap(a0):
        parts = []
        for blk in range(8):
            b0 = a0 + blk * 32
            parts += [w[:, :, b0 + 16:b0 + 32], w[:, :, b0:b0 + 16]]
        return np.concatenate(parts, axis=-1)

    src["w_in_ext"] = np.concatenate([
        seg(0, 128), seg(128, 256), seg(784, 1040), swap(784), seg(1040, 1296), swap(1040),
        seg(1552, 1808), seg(1808, 2064), seg(2064, 2320), seg(512, 528), np.zeros((L, D, 48), np.float32),
        seg(256, 512), seg(528, 784), seg(1296, 1552)], axis=-1)
    assert src["w_in_ext"].shape[-1] == NCOL
    src["rope_cos"], src["rope_ssin"] = _rope_tables()
    src["gla_b_gate_col"] = src["gla_b_gate"].reshape(L, 128, 1)
    src["s5_a_re_T"] = src["s5_a_re"].transpose(0, 2, 1)
    src["s5_a_im_T"] = src["s5_a_im"].transpose(0, 2, 1)
    src["s5_b_re_p"] = src["s5_b_re"].transpose(0, 2, 1, 3).reshape(L, 64, 256)
    src["s5_b_im_p"] = src["s5_b_im"].transpose(0, 2, 1, 3).reshape(L, 64, 256)
    src["s5_c_re_p"] = src["s5_c_re"].transpose(0, 3, 1, 2).reshape(L, 64, 256)
    src["s5_c_im_p"] = src["s5_c_im"].transpose(0, 3, 1, 2).reshape(L, 64, 256)
    src["s5_d_col"] = col2(src["s5_d"])
    src["s5_b_glu_col"] = col2(src["s5_b_glu"])
    src["lru_conv_w_col"] = src["lru_conv_w"].reshape(L, 4, 2, 128).transpose(0, 3, 2, 1).reshape(L, 128, 8)
    for nm_ in ("lru_conv_b", "lru_b_a", "lru_b_x", "lru_lambda"):
        src[nm_ + "_col"] = col2(src[nm_])
    out = {}
    for name, shape in WEIGHT_SPECS:
        a = np.ascontiguousarray(src[name], dtype=np.float32)
        assert list(a.shape) == list(shape), (name, a.shape, shape)
        out[name] = a
    return out


def run(inputs, mode="full", want=("y",), opts=None):
    key = (mode, repr(sorted((opts or {}).items())))
    if key not in _CACHE:
        _CACHE[key] = build(mode, opts)
    nc, env = _CACHE[key]
    w = _prep_weights(inputs)
    x = np.ascontiguousarray(inputs["x"], dtype=np.float32)
    in_maps = []
    for c in range(NCORES):
        m = dict(w)
        m["x"] = x[c]
        in_maps.append(m)
    res = run_bass_kernel_spmd(nc, in_maps, core_ids=list(range(NCORES)))
    outs = [np.stack([r[nm_] for r in res.results], axis=0) for nm_ in want]
    return outs[0] if len(outs) == 1 else outs


def kernel(**inputs):
    return run(inputs, "full")
```

```python
import math
import numpy as np
import concourse.bass as bass
import concourse.mybir as mybir
from concourse.bass_utils import run_bass_kernel_spmd

F32 = mybir.dt.float32
BF16 = mybir.dt.bfloat16
I32 = mybir.dt.int32
AF = mybir.ActivationFunctionType
ALU = mybir.AluOpType
AX = mybir.AxisListType

S = 4096
D = 1024
DFF = 4096
L = 2
EPS = 1e-6
NCORES = 8


class Buf:
    __slots__ = ("name", "w", "r", "excl")

    def __init__(self, name, excl=False):
        self.name = name
        self.w = {}
        self.r = {}
        self.excl = excl


class Eng:
    def __init__(self, nc, name, h, sync_self):
        self.name = name
        self.h = h
        self.sem = nc.semaphore("sem_" + name).__enter__()
        self.cnt = 0
        self.waited = {}
        self.sync_self = sync_self


class K:
    def __init__(self, nc, n_dma_sems=20):
        self.nc = nc
        self.pe = Eng(nc, "pe", nc.tensor, False)
        self.act = Eng(nc, "act", nc.scalar, True)
        self.dve = Eng(nc, "dve", nc.vector, True)
        self.pool = Eng(nc, "pool", nc.gpsimd, True)
        self.sp = Eng(nc, "sp", nc.sync, False)
        self.dsem = {}
        for q in ("sp", "pool", "act"):
            self.dsem[q] = [[nc.semaphore("dq_%s_%d" % (q, i)).__enter__(), 0] for i in range(n_dma_sems)]
        self.dnext = {"sp": 0, "pool": 0, "act": 0}
        self._n = 0
        self.stacks = []

    def push(self):
        from contextlib import ExitStack
        self.stacks.append(ExitStack())

    def pop(self):
        self.barrier()
        self.stacks.pop().close()

    def barrier(self):
        engs = [self.pe, self.act, self.dve, self.pool, self.sp]
        deps = [(e.name, e.sem, e.cnt) for e in engs if e.cnt > 0]
        for q, pool in self.dsem.items():
            for j, (sem, cnt) in enumerate(pool):
                if cnt > 0:
                    deps.append(("d_%s_%d" % (q, j), sem, 16 * cnt))
        for e in engs:
            sv = e.sync_self
            e.sync_self = False
            self._wait(e, deps)
            e.sync_self = sv

    def uid(self, p="t"):
        self._n += 1
        return "%s%d" % (p, self._n)

    def sb(self, shape, dtype, name=None):
        cm = self.nc.sbuf_tensor(name or self.uid("sb"), list(shape), dtype)
        if self.stacks:
            return self.stacks[-1].enter_context(cm)
        return cm.__enter__()

    def ps(self, shape, dtype, name=None):
        return self.nc.psum_tensor(name or self.uid("ps"), list(shape), dtype).__enter__()

    def _wait(self, e, deps):
        best = {}
        for (key, sem, val) in deps:
            if key == e.name and not e.sync_self:
                continue
            if key not in best or best[key][1] < val:
                best[key] = (sem, val)
        for key, (sem, val) in best.items():
            if e.waited.get(key, 0) >= val:
                continue
            e.h.wait_ge(sem, val)
            e.waited[key] = val

    def _deps(self, reads, writes):
        deps = []
        for b in reads:
            deps.extend(b.w.values())
            if b.excl:
                deps.extend(b.r.values())
        for b in writes:
            deps.extend(b.w.values())
            deps.extend(b.r.values())
        return deps

    def op(self, e, fn, reads=(), writes=()):
        self._wait(e, self._deps(reads, writes))
        inst = fn(e.h)
        e.cnt += 1
        inst.then_inc(e.sem, 1)
        t = (e.name, e.sem, e.cnt)
        for b in reads:
            b.r[e.name] = t
        for b in writes:
            b.w[e.name] = t
        return t

    def dma(self, q, out, in_, reads=(), writes=()):
        pool = self.dsem[q.name]
        j = self.dnext[q.name]
        self.dnext[q.name] = (j + 1) % len(pool)
        sem, cnt = pool[j]
        key = "d_%s_%d" % (q.name, j)
        deps = self._deps(reads, writes)
        if cnt > 0:
            deps.append((key, sem, 16 * cnt))
        self._wait(q, deps)
        q.h.dma_start(out=out, in_=in_).then_inc(sem, 16)
        pool[j][1] = cnt + 1
        t = (key, sem, 16 * (cnt + 1))
        for b in reads:
            b.r[key] = t
        for b in writes:
            b.w[key] = t
        return t

    def finish(self, bufs):
        deps = []
        for b in bufs:
            deps.extend(b.w.values())
        self._wait(self.sp, deps)


def make_ident(k, nc):
    ident = k.sb([128, 128], BF16, "ident")
    b = Buf("ident")
    k.op(k.pool, lambda h: h.memset(ident[:], 1.0), writes=[b])
    k.op(k.pool, lambda h: h.affine_select(out=ident[:], in_=ident[:], pattern=[[1, 128]],
                                            compare_op=ALU.is_equal, fill=0.0, base=0,
                                            channel_multiplier=-1), reads=[b], writes=[b])
    return ident, b


class NormT:
    def __init__(self, k, ns, ident, ident_b, pT, pT_b, tag):
        self.k = k
        self.ns = ns
        self.ident, self.ident_b = ident, ident_b
        self.pT, self.pT_b = pT, pT_b
        self.xn = k.sb([128, 1024], BF16, tag + "_xn")
        self.xn_b = Buf(tag + "_xn")
        self.junk, self.junk_b = self.xn, self.xn_b
        self.st = k.sb([128, 4 * ns], F32, tag + "_st")
        self.st_b = Buf(tag + "_st")

    def run(self, xin, xin_b, gcol, gcol_b, hT, hT_b):
        for _ in self.run_gen(xin, xin_b, gcol, gcol_b, hT, hT_b):
            pass

    def run_gen(self, xin, xin_b, gcol, gcol_b, hT, hT_b):
        k, ns = self.k, self.ns
        st = self.st
        for s in range(ns):
            k.op(k.act, lambda h: h.activation(out=self.junk[:], in_=xin[:, s, :], func=AF.Square,
                                               accum_out=st[:, s:s + 1]),
                 reads=[xin_b], writes=[self.junk_b, self.st_b])
        k.op(k.act, lambda h: h.activation(out=st[:, ns:2 * ns], in_=st[:, 0:ns], func=AF.Ln,
                                           scale=1.0 / D, bias=self.k.eps_ap),
             reads=[self.st_b], writes=[self.st_b])
        k.op(k.act, lambda h: h.activation(out=st[:, 2 * ns:3 * ns], in_=st[:, ns:2 * ns], func=AF.Exp,
                                           scale=-0.5),
             reads=[self.st_b], writes=[self.st_b])
        yield
        for s in range(ns):
            k.op(k.act, lambda h: h.activation(out=self.xn[:], in_=xin[:, s, :], func=AF.Copy,
                                               scale=st[:, 2 * ns + s:2 * ns + s + 1]),
                 reads=[xin_b, self.st_b], writes=[self.xn_b])
            for kk in range(8):
                k.op(k.pe, lambda h: h.transpose(out=self.pT[:, kk * 128:(kk + 1) * 128],
                                                 in_=self.xn[:, kk * 128:(kk + 1) * 128],
                                                 identity=self.ident[:]),
                     reads=[self.xn_b, self.ident_b], writes=[self.pT_b])
            k.op(k.dve, lambda h: h.tensor_tensor(
                out=hT[:, :, s * 128:(s + 1) * 128],
                in0=self.pT[:].rearrange("p (k t) -> p k t", k=8),
                in1=gcol[:].unsqueeze(2).to_broadcast([128, 8, 128]),
                op=ALU.mult),
                reads=[self.pT_b, gcol_b], writes=[hT_b])
            yield


def post_norm_residual(k, ps_halves, ps_bufs, xres_ap, xres_b, gpost, gpost_b, st, st_b, junk, junk_b,
                       tmp, tmp_b, out_ap, out_b):
    for hf in range(2):
        k.op(k.act, lambda h: h.activation(out=tmp[:, hf * 512:(hf + 1) * 512], in_=ps_halves[hf], func=AF.Square,
                                           accum_out=st[:, hf:hf + 1]),
             reads=[ps_bufs[hf]], writes=[tmp_b, st_b])
    k.op(k.dve, lambda h: h.tensor_tensor(out=st[:, 2:3], in0=st[:, 0:1], in1=st[:, 1:2], op=ALU.add),
         reads=[st_b], writes=[st_b])
    k.op(k.act, lambda h: h.activation(out=st[:, 3:4], in_=st[:, 2:3], func=AF.Ln, scale=1.0 / D, bias=k.eps_ap),
         reads=[st_b], writes=[st_b])
    k.op(k.act, lambda h: h.activation(out=st[:, 4:5], in_=st[:, 3:4], func=AF.Exp, scale=-0.5),
         reads=[st_b], writes=[st_b])
    for hf in range(2):
        k.op(k.dve, lambda h: h.scalar_tensor_tensor(out=tmp[:, hf * 512:(hf + 1) * 512], in0=ps_halves[hf],
                                                     scalar=st[:, 4:5], in1=gpost[:, hf * 512:(hf + 1) * 512],
                                                     op0=ALU.mult, op1=ALU.mult),
             reads=[ps_bufs[hf], st_b, gpost_b], writes=[tmp_b])
    k.op(k.pool, lambda h: h.tensor_tensor(out=out_ap, in0=tmp[:], in1=xres_ap, op=ALU.add),
         reads=[tmp_b, xres_b], writes=[out_b])


def ffn_pass(k, env, l, x_src, x_src_b, x_dst, x_dst_b):
    nc = k.nc
    NS = 2
    TT = NS * 128
    NMT = S // TT
    banks, bank_b = env["banks"], env["bank_b"]
    pT, pT_b = env["pT"], env["pT_b"]
    k.push()
    sbase = nc.sbuf_bytes_remaining

    w1 = k.sb([128, 8, DFF], BF16, "w1_%d" % l)
    w2 = k.sb([128, 32, D], BF16, "w2_%d" % l)
    w1_b = [Buf("w1f%d" % i) for i in range(8)]
    w2_b = [Buf("w2k%d" % i) for i in range(8)]
    w1d = env["ffn_w1"][l].rearrange("(k p) f -> p k f", p=128)
    w2d = env["ffn_w2"][l].rearrange("(k p) f -> p k f", p=128)
    for i in range(8):
        k.dma(k.pool, w1[:, :, i * 512:(i + 1) * 512], w1d[:, :, i * 512:(i + 1) * 512], writes=[w1_b[i]])
    for i in range(8):
        k.dma(k.pool, w2[:, 4 * i:4 * i + 4, :], w2d[:, 4 * i:4 * i + 4, :], writes=[w2_b[i]])

    gcol = k.sb([128, 8], F32, "f_gcol%d" % l)
    gcol_b = Buf("gcol")
    k.dma(k.sp, gcol[:], env["ln_ffn_pre_col"][l], writes=[gcol_b])
    gpost = k.sb([128, D], F32, "f_gpost%d" % l)
    gpost_b = Buf("gpost")
    k.dma(k.sp, gpost[:], env["ln_ffn_post"][l:l + 1, :].partition_broadcast(128), writes=[gpost_b])

    xin = [k.sb([128, NS, D], F32, "f_xin%d_%d" % (l, i)) for i in range(2)]
    xin_b = [Buf("xin%d" % i) for i in range(2)]
    hT = [k.sb([128, 8, TT], BF16, "f_hT%d_%d" % (l, i)) for i in range(2)]
    hT_b = [Buf("hT%d" % i) for i in range(2)]
    hid = k.sb([128, 32, TT], BF16, "f_hid%d" % l)
    hid_b = [Buf("hid%d" % i) for i in range(4)]
    rl = [k.sb([128, TT], F32, "f_rl%d_%d" % (l, i)) for i in range(2)]
    rl_b = [Buf("rl%d" % i) for i in range(2)]
    tmp = k.sb([128, D], F32, "f_tmp%d" % l)
    tmp_b = Buf("tmp")
    st2 = k.sb([128, 8], F32, "f_st2%d" % l)
    st2_b = Buf("st2")
    nt = NormT(k, NS, env["ident"], env["ident_b"], pT, pT_b, "f_nt%d" % l)

    xs = x_src.rearrange("(m s p) d -> m p s d", s=NS, p=128)
    xd = x_dst.rearrange("(m s p) d -> m p s d", s=NS, p=128)

    def load(mt):
        k.dma(k.sp, xin[mt % 2][:], xs[mt], reads=[x_src_b[mt]], writes=[xin_b[mt % 2]])

    load(0)
    for mt in range(NMT):
        sl = mt % 2
        if mt + 1 < NMT:
            load(mt + 1)
        nt.run(xin[sl], xin_b[sl], gcol, gcol_b, hT[sl], hT_b[sl])
        for fc in range(32):
            pb = fc % 2
            for kk in range(8):
                k.op(k.pe, lambda h: h.matmul(banks[pb][:, 0:TT], lhsT=w1[:, kk, fc * 128:(fc + 1) * 128],
                                              rhs=hT[sl][:, kk, :], start=(kk == 0), stop=(kk == 7)),
                     reads=[w1_b[fc // 4], hT_b[sl]], writes=[bank_b[pb]])
            k.op(k.act, lambda h: h.activation(out=rl[pb][:], in_=banks[pb][:, 0:TT], func=AF.Relu),
                 reads=[bank_b[pb]], writes=[rl_b[pb]])
            k.op(k.pool, lambda h: h.tensor_tensor(out=hid[:, fc, :], in0=rl[pb][:], in1=rl[pb][:], op=ALU.mult),
                 reads=[rl_b[pb]], writes=[hid_b[fc // 8]])
        for s in range(NS):
            pbs = [2 + 2 * (s % 2), 3 + 2 * (s % 2)]
            for hf in range(2):
                for fc in range(32):
                    k.op(k.pe, lambda h: h.matmul(banks[pbs[hf]][:, 0:512], lhsT=hid[:, fc, s * 128:(s + 1) * 128],
                                                  rhs=w2[:, fc, hf * 512:(hf + 1) * 512],
                                                  start=(fc == 0), stop=(fc == 31)),
                         reads=[hid_b[fc // 8], w2_b[fc // 4]], writes=[bank_b[pbs[hf]]])
            post_norm_residual(k, [banks[pbs[0]][:, 0:512], banks[pbs[1]][:, 0:512]],
                               [bank_b[pbs[0]], bank_b[pbs[1]]],
                               xin[sl][:, s, :], xin_b[sl], gpost, gpost_b, st2, st2_b,
                               nt.junk, nt.junk_b, tmp, tmp_b, xin[sl][:, s, :], xin_b[sl])
        k.dma(k.sp, xd[mt], xin[sl][:], reads=[xin_b[sl]], writes=[x_dst_b[mt]])
    env["sbuf_used_ffn"] = sbase - nc.sbuf_bytes_remaining
    k.pop()


def TT(k, e, out, in0, in1, op, R, W):
    return k.op(e, lambda h: h.tensor_tensor(out=out, in0=in0, in1=in1, op=op), reads=R, writes=W)


def TS(k, e, out, in0, s1, op0, R, W, s2=None, op1=None):
    if op1 is None:
        return k.op(e, lambda h: h.tensor_scalar(out=out, in0=in0, scalar1=s1, scalar2=None, op0=op0), reads=R, writes=W)
    return k.op(e, lambda h: h.tensor_scalar(out=out, in0=in0, scalar1=s1, scalar2=s2, op0=op0, op1=op1), reads=R, writes=W)


def STT(k, out, in0, scalar, in1, op0, op1, R, W):
    return k.op(k.dve, lambda h: h.scalar_tensor_tensor(out=out, in0=in0, scalar=scalar, in1=in1, op0=op0, op1=op1),
                reads=R, writes=W)


def ACT(k, out, in_, func, R, W, scale=1.0, bias=None):
    if bias is None:
        return k.op(k.act, lambda h: h.activation(out=out, in_=in_, func=func, scale=scale), reads=R, writes=W)
    return k.op(k.act, lambda h: h.activation(out=out, in_=in_, func=func, scale=scale, bias=bias), reads=R, writes=W)


def CP(k, e, out, in_, R, W):
    return k.op(e, lambda h: h.tensor_copy(out=out, in_=in_), reads=R, writes=W)


def MM(k, out, lhsT, rhs, start, stop, R, W, sgc=False):
    return k.op(k.pe, lambda h: h.matmul(out, lhsT=lhsT, rhs=rhs, start=start, stop=stop, skip_group_check=sgc),
                reads=R, writes=W)


def TR(k, out, in_, ident, R, W):
    return k.op(k.pe, lambda h: h.transpose(out=out, in_=in_, identity=ident), reads=R, writes=W)


def MSET(k, e, ap, val, W):
    return k.op(e, lambda h: h.memset(ap, val), writes=W)


def ASEL(k, out, in_, pattern, op, base, cm, R, W):
    return k.op(k.pool, lambda h: h.affine_select(out=out, in_=in_, pattern=pattern, compare_op=op, fill=0.0,
                                                  base=base, channel_multiplier=cm), reads=R, writes=W)


OFF_GQ, OFF_GK = 0, 128
OFF_DQ, OFF_DQS, OFF_DK, OFF_DKS = 256, 512, 768, 1024
OFF_SU, OFF_RX, OFF_RG, OFF_GLR = 1280, 1536, 1792, 2048
OFF_TMG, OFF_DV = 2112, 2624
NCOL = 2880
TWO_PI = 2.0 * math.pi
CW1 = 6.28125
CW2 = TWO_PI - 6.28125


def make_consts(k, env):
    c = {}
    b = Buf("consts")
    c["b"] = b
    one = k.sb([128, 1], F32, "c_one"); MSET(k, k.pool, one[:], 1.0, [b]); c["one"] = one
    sgn = k.sb([128, 1], F32, "c_sgn"); MSET(k, k.pool, sgn[0:64, :], 1.0, [b]); MSET(k, k.pool, sgn[64:128, :], -1.0, [b])
    c["sgn"] = sgn
    tri = k.sb([128, 128], BF16, "c_tri"); MSET(k, k.pool, tri[:], 1.0, [b])
    ASEL(k, tri[:], tri[:], [[1, 128]], ALU.is_ge, 0, -1, [b], [b]); c["tri"] = tri
    tri64 = k.sb([64, 64], F32, "c_tri64"); MSET(k, k.pool, tri64[:], 1.0, [b])
    ASEL(k, tri64[:], tri64[:], [[1, 64]], ALU.is_ge, 0, -1, [b], [b]); c["tri64"] = tri64
    hm = k.sb([128, 4], F32, "c_hm"); MSET(k, k.pool, hm[:], 1.0, [b])
    ASEL(k, hm[:], hm[:], [[-32, 4]], ALU.is_ge, 0, 1, [b], [b])
    ASEL(k, hm[:], hm[:], [[32, 4]], ALU.is_ge, 31, -1, [b], [b]); c["hm"] = hm
    rm = k.sb([128, 8], F32, "c_rm16"); MSET(k, k.pool, rm[:], 1.0, [b])
    ASEL(k, rm[:], rm[:], [[-16, 8]], ALU.is_ge, 0, 1, [b], [b])
    ASEL(k, rm[:], rm[:], [[16, 8]], ALU.is_ge, 15, -1, [b], [b]); c["rm16"] = rm
    jf = k.sb([128, 128], F32, "c_jf"); j2 = k.sb([128, 128], F32, "c_j2")
    MSET(k, k.pool, jf[:], 1.0, [b]); MSET(k, k.pool, j2[:], 1.0, [b])
    ASEL(k, jf[:], jf[:], [[1, 128]], ALU.is_equal, -64, -1, [b], [b])
    ASEL(k, j2[:], j2[:], [[1, 128]], ALU.is_equal, 64, -1, [b], [b])
    TT(k, k.pool, jf[:], jf[:], j2[:], ALU.add, [b], [b]); c["jf"] = jf
    rs = k.sb([128, 256], F32, "c_reset"); MSET(k, k.pool, rs[:], 1.0, [b])
    for i in range(4):
        MSET(k, k.pool, rs[:, 64 * i:64 * i + 1], 0.0, [b])
    c["reset"] = rs
    return c


def mixer_prep(k, env, l, C):
    nc = k.nc
    P = {}
    cb = C["b"]
    lam_init = 0.8 - 0.6 * math.exp(-0.3 * l)
    pb = Buf("prep")
    P["b"] = pb
    banks, bank_b = env["banks"], env["bank_b"]
    pT1, pT1_b = env["pT1"], env["pT1_b"]

    def sbt(shape, dt, nm):
        return k.sb(shape, dt, "p%d_%s" % (l, nm))

    wgate = sbt([16, 128], BF16, "wgate"); negb = sbt([128, 1], F32, "negb"); gnorm = sbt([128, 64], F32, "gnorm")
    lw = sbt([128, 8], F32, "lw"); dnorm = sbt([128, 64], F32, "dnorm")
    cw = sbt([128, 2, 4], F32, "cw"); lv = sbt([128, 5, 2], F32, "lv"); bglu = sbt([128, 2], F32, "bglu")
    wa = sbt([128, 2, 128], BF16, "wa"); wx = sbt([128, 2, 128], BF16, "wx"); lsp = sbt([128, 4, 2], F32, "lsp")
    wglu = sbt([128, 2, 256], BF16, "wglu"); rho = sbt([128, 16], F32, "rho")
    bpad = sbt([128, 2, 16, 128], BF16, "bpad"); cpad = sbt([128, 2, 16, 128], BF16, "cpad")
    wl = sbt([128, 9, 2, 16], F32, "wl")
    tabc = sbt([128, 16, 64], F32, "tabc"); tabs = sbt([128, 16, 64], F32, "tabs")
    rhot = sbt([128, 16, 64], F32, "rhot")
    k.push()

    def ld(t_ap, src, q=None):
        k.dma(q or k.sp, t_ap, src, writes=[pb])

    ld(wgate[:], env["gla_w_gate"][l], k.pool)
    ld(negb[:], env["gla_b_gate_col"][l])
    TS(k, k.dve, negb[:], negb[:], -1.0, ALU.mult, [pb], [pb])
    ld(gnorm[:], env["gla_norm"][l:l + 1, :].partition_broadcast(128))
    P.update(wgate=wgate, negb=negb, gnorm=gnorm)
    lq = sbt([128, 4, 32], F32, "lq")
    for i, nm in enumerate(("diff_lq1", "diff_lk1", "diff_lq2", "diff_lk2")):
        ld(lq[:, i, :], env[nm][l:l + 1, :].partition_broadcast(128))
    pr = sbt([128, 2, 32], F32, "lpr")
    TT(k, k.dve, pr[:, 0, :], lq[:, 0, :], lq[:, 1, :], ALU.mult, [pb], [pb])
    TT(k, k.dve, pr[:, 1, :], lq[:, 2, :], lq[:, 3, :], ALU.mult, [pb], [pb])
    k.op(k.dve, lambda h: h.reduce_sum(out=lw[:, 0:2], in_=pr[:], axis=AX.X), reads=[pb], writes=[pb])
    ACT(k, lw[:, 2:4], lw[:, 0:2], AF.Exp, [pb], [pb])
    TT(k, k.dve, lw[:, 4:5], lw[:, 2:3], lw[:, 3:4], ALU.subtract, [pb], [pb])
    TS(k, k.dve, lw[:, 5:6], lw[:, 4:5], lam_init, ALU.add, [pb], [pb], s2=-1.0, op1=ALU.mult)
    ld(dnorm[:], env["diff_norm"][l:l + 1, :].partition_broadcast(128))
    TS(k, k.dve, dnorm[:], dnorm[:], 1.0 - lam_init, ALU.mult, [pb], [pb])
    P.update(nlam=lw[:, 5:6], dnorm=dnorm)
    ld(cw[:], env["lru_conv_w_col"][l])
    for i, nm in enumerate(("lru_conv_b_col", "lru_b_a_col", "lru_b_x_col", "lru_lambda_col", "s5_d_col")):
        ld(lv[:, i, :], env[nm][l])
    ld(bglu[:], env["s5_b_glu_col"][l])
    MSET(k, k.pool, wa[:], 0.0, [pb]); MSET(k, k.pool, wx[:], 0.0, [pb])
    for n in range(4):
        r0 = (n % 2) * 64
        ld(wa[r0:r0 + 64, n // 2, r0:r0 + 64], env["lru_w_a"][l, n], k.pool)
        ld(wx[r0:r0 + 64, n // 2, r0:r0 + 64], env["lru_w_x"][l, n], k.pool)
    ACT(k, lsp[:, 0, :], lv[:, 3, :], AF.Exp, [pb], [pb], scale=-1.0)
    ACT(k, lsp[:, 1, :], lsp[:, 0, :], AF.Ln, [pb, cb], [pb], bias=C["one"][:])
    TS(k, k.dve, lsp[:, 2, :], lsp[:, 1, :], -8.0, ALU.mult, [pb], [pb])
    TS(k, k.dve, lsp[:, 3, :], lsp[:, 1, :], -16.0, ALU.mult, [pb], [pb])
    ld(wglu[:], env["s5_w_glu"][l].rearrange("(k p) n -> p k n", p=128), k.pool)
    P.update(cw=cw, lv=lv, bglu=bglu, wa=wa, wx=wx, lsp=lsp, wglu=wglu)
    lr = sbt([128, 16], F32, "lr"); li = sbt([128, 16], F32, "li"); ls = sbt([128, 16], F32, "ls")
    for hlf in range(2):
        ld(lr[hlf * 64:(hlf + 1) * 64, :], env["s5_a_re_T"][l])
        ld(li[hlf * 64:(hlf + 1) * 64, :], env["s5_a_im_T"][l])
    ld(ls[:], env["s5_log_step"][l:l + 1, :].partition_broadcast(128))
    bri = sbt([128, 2, 256], F32, "bri"); cri = sbt([128, 2, 256], F32, "cri")
    for hlf in range(2):
        sl_ = slice(hlf * 64, (hlf + 1) * 64)
        ld(bri[sl_, 0, :], env["s5_b_re_p"][l]); ld(bri[sl_, 1, :], env["s5_b_im_p"][l])
        ld(cri[sl_, 0, :], env["s5_c_re_p"][l]); ld(cri[sl_, 1, :], env["s5_c_im_p"][l])
    w = sbt([128, 24, 16], F32, "s5w")
    W_ = lambda i: w[:, i, :]
    R, Wr = [pb], [pb]
    ACT(k, W_(0), ls[:], AF.Exp, R, Wr)
    TT(k, k.dve, W_(1), lr[:], W_(0), ALU.mult, R, Wr)
    TT(k, k.dve, W_(2), li[:], W_(0), ALU.mult, R, Wr)
    ACT(k, rho[:], W_(1), AF.Exp, R, Wr)
    ki = sbt([128, 16], I32, "ki")

    def sin_reduced(dst, ang):
        TS(k, k.dve, W_(3), ang, 1.0 / TWO_PI, ALU.mult, R, Wr)
        CP(k, k.dve, ki[:], W_(3), R, Wr)
        CP(k, k.dve, W_(4), ki[:], R, Wr)
        STT(k, W_(5), W_(4), -CW1, ang, ALU.mult, ALU.add, R, Wr)
        STT(k, W_(5), W_(4), -CW2, W_(5), ALU.mult, ALU.add, R, Wr)
        TS(k, k.dve, W_(5), W_(5), 3.1415925, ALU.min, R, Wr, s2=-3.1415925, op1=ALU.max)
        ACT(k, dst, W_(5), AF.Sin, R, Wr)

    sin_reduced(W_(6), W_(2))
    TS(k, k.dve, W_(7), W_(2), math.pi / 2.0, ALU.add, R, Wr)
    sin_reduced(W_(8), W_(7))
    TT(k, k.dve, W_(9), W_(6), W_(6), ALU.mult, R, Wr)
    TT(k, k.dve, W_(10), W_(8), W_(8), ALU.mult, R, Wr)
    TT(k, k.dve, W_(9), W_(9), W_(10), ALU.add, R, Wr)
    TS(k, k.dve, W_(9), W_(9), -0.5, ALU.mult, R, Wr, s2=1.5, op1=ALU.add)
    TT(k, k.dve, W_(6), W_(6), W_(9), ALU.mult, R, Wr)
    TT(k, k.dve, W_(8), W_(8), W_(9), ALU.mult, R, Wr)
    TT(k, k.dve, W_(11), rho[:], W_(8), ALU.mult, R, Wr)
    TT(k, k.dve, W_(12), rho[:], W_(6), ALU.mult, R, Wr)
    TT(k, k.dve, W_(13), lr[:], lr[:], ALU.mult, R, Wr)
    TT(k, k.dve, W_(14), li[:], li[:], ALU.mult, R, Wr)
    TT(k, k.dve, W_(13), W_(13), W_(14), ALU.add, R, Wr)
    k.op(k.dve, lambda h: h.reciprocal(out=W_(13), in_=W_(13)), reads=R, writes=Wr)
    TS(k, k.dve, W_(11), W_(11), -1.0, ALU.add, R, Wr)
    TT(k, k.dve, W_(14), W_(11), lr[:], ALU.mult, R, Wr)
    TT(k, k.dve, W_(15), W_(12), li[:], ALU.mult, R, Wr)
    TT(k, k.dve, W_(14), W_(14), W_(15), ALU.add, R, Wr)
    TT(k, k.dve, W_(14), W_(14), W_(13), ALU.mult, R, Wr)
    TT(k, k.dve, W_(15), W_(12), lr[:], ALU.mult, R, Wr)
    TT(k, k.dve, W_(16), W_(11), li[:], ALU.mult, R, Wr)
    TT(k, k.dve, W_(15), W_(15), W_(16), ALU.subtract, R, Wr)
    TT(k, k.dve, W_(15), W_(15), W_(13), ALU.mult, R, Wr)
    crb = W_(14).unsqueeze(2).to_broadcast([128, 16, 16])
    cib = W_(15).unsqueeze(2).to_broadcast([128, 16, 16])
    bre = bri[:, 0, :].rearrange("p (g h) -> p g h", g=16)
    bim = bri[:, 1, :].rearrange("p (g h) -> p g h", g=16)
    bb = sbt([128, 4, 256], F32, "bb")
    B3 = lambda i: bb[:, i, :].rearrange("p (g h) -> p g h", g=16)
    TT(k, k.dve, B3(0), bre, crb, ALU.mult, R, Wr)
    TT(k, k.dve, B3(1), bim, cib, ALU.mult, R, Wr)
    TT(k, k.dve, B3(0), B3(0), B3(1), ALU.subtract, R, Wr)
    TT(k, k.dve, B3(2), bim, crb, ALU.mult, R, Wr)
    TT(k, k.dve, B3(3), bre, cib, ALU.mult, R, Wr)
    TT(k, k.dve, B3(2), B3(2), B3(3), ALU.add, R, Wr)
    bst = sbt([128, 2, 256], BF16, "bst")
    CP(k, k.dve, bst[0:64, 0, :], bb[0:64, 0, :], R, Wr)
    CP(k, k.dve, bst[64:128, 0, :], bb[64:128, 2, :], R, Wr)
    CP(k, k.dve, bst[0:64, 1, :], bb[0:64, 2, :], R, Wr)
    CP(k, k.dve, bst[64:128, 1, :], bb[64:128, 0, :], R, Wr)
    for v in range(2):
        for ch in range(2):
            TR(k, pT1[:, 0:128], bst[:, v, ch * 128:(ch + 1) * 128], env["ident"][:], [pb, env["ident_b"]], [pT1_b])
            for gg in range(8):
                TS(k, k.dve, bpad[:, v, ch * 8 + gg, :], pT1[:, 0:128], C["rm16"][:, gg:gg + 1], ALU.mult,
                   [pT1_b, cb], [pb])
    cst = sbt([128, 2, 256], F32, "cst")
    CP(k, k.dve, cst[0:64, 0, :], cri[0:64, 0, :], R, Wr)
    TS(k, k.dve, cst[64:128, 0, :], cri[64:128, 1, :], -1.0, ALU.mult, R, Wr)
    TS(k, k.dve, cst[0:64, 1, :], cri[0:64, 1, :], -1.0, ALU.mult, R, Wr)
    CP(k, k.dve, cst[64:128, 1, :], cri[64:128, 0, :], R, Wr)
    MSET(k, k.pool, cpad[:], 0.0, Wr)
    for v in range(2):
        for g in range(16):
            CP(k, k.pool, cpad[:, v, g, (g % 8) * 16:(g % 8) * 16 + 16], cst[:, v, g * 16:(g + 1) * 16], R, Wr)
    CP(k, k.dve, wl[:, 0, 0, :], W_(8), R, Wr)
    TS(k, k.dve, wl[:, 0, 1, :], W_(6), C["sgn"][:, 0:1], ALU.mult, R + [cb], Wr)
    MSET(k, k.pool, tabc[:, :, 0:1], 1.0, Wr); MSET(k, k.pool, tabs[:, :, 0:1], 0.0, Wr)
    t1 = sbt([128, 16, 32], F32, "dbl1"); t2 = sbt([128, 16, 32], F32, "dbl2")
    n = 1
    for lev in range(6):
        wc = wl[:, lev, 0, :].unsqueeze(2).to_broadcast([128, 16, n])
        ws = wl[:, lev, 1, :].unsqueeze(2).to_broadcast([128, 16, n])
        TT(k, k.pool, t1[:, :, 0:n], tabs[:, :, 0:n], ws, ALU.mult, R, Wr)
        TT(k, k.pool, t2[:, :, 0:n], tabc[:, :, 0:n], wc, ALU.mult, R, Wr)
        TT(k, k.pool, tabc[:, :, n:2 * n], t2[:, :, 0:n], t1[:, :, 0:n], ALU.subtract, R, Wr)
        TT(k, k.pool, t1[:, :, 0:n], tabc[:, :, 0:n], ws, ALU.mult, R, Wr)
        TT(k, k.pool, t2[:, :, 0:n], tabs[:, :, 0:n], wc, ALU.mult, R, Wr)
        TT(k, k.pool, tabs[:, :, n:2 * n], t1[:, :, 0:n], t2[:, :, 0:n], ALU.add, R, Wr)
        TT(k, k.dve, W_(17), wl[:, lev, 0, :], wl[:, lev, 0, :], ALU.mult, R, Wr)
        TT(k, k.dve, W_(18), wl[:, lev, 1, :], wl[:, lev, 1, :], ALU.mult, R, Wr)
        TT(k, k.dve, wl[:, lev + 1, 0, :], W_(17), W_(18), ALU.subtract, R, Wr)
        STT(k, wl[:, lev + 1, 1, :], wl[:, lev, 0, :], 2.0, wl[:, lev, 1, :], ALU.mult, ALU.mult, R, Wr)
        n *= 2
    P.update(rho=rho, bpad=bpad, cpad=cpad, tabc=tabc, tabs=tabs, c64=wl[:, 6, 0, :], s64=wl[:, 6, 1, :], rhot=rhot)
    CP(k, k.pool, rhot[:], rho[:].unsqueeze(2).to_broadcast([128, 16, 64]), R, Wr)
    MSET(k, k.pool, rhot[:, :, 0:1], 0.0, Wr)
    k.pop()
    return P


def mixer_pass(k, env, l, x_src, x_src_b, x_dst, x_dst_b, C, dbg=None):
    nc = k.nc
    NS, TT_, NMT = 2, 256, 16
    banks, bank_b = env["banks"], env["bank_b"]
    pT, pT_b, pT1, pT1_b = env["pT"], env["pT_b"], env["pT1"], env["pT1_b"]
    ident, ident_b = env["ident"], env["ident_b"]
    cb = C["b"]
    hm, tri, tri64 = C["hm"], C["tri"], C["tri64"]
    k.push()
    sbase = nc.sbuf_bytes_remaining
    nm = lambda s_: "m%d_%s" % (l, s_)

    win = k.sb([128, 8, NCOL], BF16, nm("win")); win_b = [Buf("win%d" % i) for i in range(8)]
    wind = env["w_in_ext"][l].rearrange("(k p) n -> p k n", p=128)
    for i in range(8):
        k.dma(k.pool, win[:, i, :], wind[:, i, :], writes=[win_b[i]])
    wout = k.sb([128, 8, D], BF16, nm("wout")); wout_b = Buf("wout")
    k.dma(k.pool, wout[:], env["w_out"][l].rearrange("(k p) n -> p k n", p=128), writes=[wout_b])
    gcol = k.sb([128, 8], F32, nm("gcol")); gcol_b = Buf("gcol")
    k.dma(k.sp, gcol[:], env["ln_mix_pre_col"][l], writes=[gcol_b])
    gpost = k.sb([128, D], F32, nm("gpost")); gpost_b = Buf("gpost")
    k.dma(k.sp, gpost[:], env["ln_mix_post"][l:l + 1, :].partition_broadcast(128), writes=[gpost_b])
    kc = k.sb([128, 2, S], BF16, nm("kc")); kc_b = [Buf("kc%d" % i) for i in range(NMT)]
    vc = k.sb([128, 32, 4, 65], BF16, nm("vc")); vc_b = [Buf("vc%d" % i) for i in range(NMT)]
    vc1_b = Buf("vc_ones")
    MSET(k, k.pool, vc[:, :, :, 64:65], 1.0, [vc1_b])
    P = mixer_prep(k, env, l, C) if "prep" not in env.get("skip", ()) else {"b": Buf("prep")}
    pb = P["b"]

    def T(shape, dt, s_):
        return k.sb(shape, dt, nm(s_)), Buf(s_)

    xin2 = [T([128, NS, D], F32, "xin%d" % i) for i in range(2)]
    hT, hT_b = T([128, 8, TT_], BF16, "hT")
    mixT, mixT_b = T([128, 8, TT_], BF16, "mixT")
    nt = NormT(k, NS, ident, ident_b, pT, pT_b, nm("nt"))
    rc, rc_b = T([128, TT_], F32, "rc"); rs, rs_b = T([128, TT_], F32, "rs")
    tmpo, tmpo_b = T([128, D], F32, "tmpo")
    st2, st2_b = T([128, 8], F32, "st2")
    f1, f1_b = T([128, TT_], F32, "f1"); f2, f2_b = T([128, TT_], F32, "f2"); f3, f3_b = T([128, TT_], F32, "f3")
    f4, f4_b = T([128, TT_], F32, "f4"); f5, f5_b = T([128, TT_], F32, "f5")
    gq, gq_b = T([128, TT_], F32, "gq"); gk, gk_b = T([128, TT_], F32, "gk")
    glr, glr_b = T([16, TT_], BF16, "glr")
    qm, qm_b = T([128, 2, 4, TT_], BF16, "qm")
    uf, uf_b = T([128, 2, TT_], F32, "uf"); ub, ub_b = T([128, 2, TT_], BF16, "ub")
    rxh, rxh_b = T([128, 2, 3 + TT_], F32, "rxh"); rg, rg_b = T([128, 2, TT_], F32, "rg")
    gvc, gvc_b = T([64, 4, 256], BF16, "gvc"); ogc, ogc_b = T([64, 4, 256], F32, "ogc")
    qdm, qdm_b = T([128, 4, TT_], BF16, "qdm"); kd, kd_b = T([128, TT_], BF16, "kd"); ks, ks_b = T([128, TT_], BF16, "ks")
    dec, dec_b = T([128, 4], F32, "dec")
    Sst, Sst_b = T([128, 64], F32, "Sst"); Sbf, Sbf_b = T([128, 2, 64], BF16, "Sbf")
    kst, kst_b = T([64, 128], BF16, "kst"); AT, AT_b = T([64, 4, 64], BF16, "AT")
    kvr, kvr_b = T([128, 64], F32, "kvr")
    osb, osb_b = T([64, 256], F32, "osb"); sil, sil_b = T([64, 256], F32, "sil")
    gst, gst_b = T([64, 12], F32, "gst"); otk, otk_b = T([64, 256], BF16, "otk")
    pt = [T([128, 2 * TT_], BF16, "pt%d" % i) for i in range(2)]
    od, od_b = T([128, 2, 256], F32, "od"); odb, odb_b = T([128, 2, 256], BF16, "odb")
    dst, dst_b = T([128, 32], F32, "dst"); dt1, dt1_b = T([128, 64], F32, "dt1")
    zlast, zlast_b = T([128, 16], F32, "zlast"); zin, zin_b = T([128, 16], F32, "zin"); zt, zt_b = T([128, 16], F32, "zt")
    m1, m1_b = T([128, 256], F32, "m1"); m2, m2_b = T([128, 256], F32, "m2"); zz, zz_b = T([128, 256], F32, "zz")
    zc, zc_b = T([128, 256], BF16, "zc"); zs, zs_b = T([128, 256], BF16, "zs")
    sf1, sf1_b, sf2, sf2_b = m1, m1_b, m2, m2_b
    yb, yb_b = T([128, 2, TT_], BF16, "yb")
    xcb, xcb_b = T([128, TT_], BF16, "xcb"); hst, hst_b = T([128, 2], F32, "hst")
    odq, odq_b = tmpo[:, 0:512].rearrange("p (s f) -> p s f", s=2), tmpo_b
    kvm, kvm_b = tmpo[:, 512:768], tmpo_b
    osq, osq_b = tmpo[0:64, 768:1024], tmpo_b
    MSET(k, k.pool, Sst[:], 0.0, [Sst_b]); MSET(k, k.pool, Sbf[:], 0.0, [Sbf_b])
    MSET(k, k.pool, zlast[:], 0.0, [zlast_b]); MSET(k, k.pool, hst[:], 0.0, [hst_b]); MSET(k, k.pool, rxh[:, :, 0:3], 0.0, [rxh_b])
    env["sbuf_used_mix"] = sbase - nc.sbuf_bytes_remaining

    xs = x_src.rearrange("(m s p) d -> m p s d", s=NS, p=128)
    xd = x_dst.rearrange("(m s p) d -> m p s d", s=NS, p=128)
    B = lambda i: banks[i]
    Bb = lambda i: bank_b[i]
    QS = 32.0 ** -0.5
    sbi = [0]

    def inproj_fm(off, width, bank):
        for kk in range(8):
            MM(k, B(bank)[0:width, 0:TT_], win[:, kk, off:off + width], hT[:, kk, :], kk == 0, kk == 7,
               [win_b[kk], hT_b], [Bb(bank)])

    NMT_ = env.get("nmt", NMT)

    def load_x(m_):
        k.dma(k.sp, xin2[m_ % 2][0][:], xs[m_], reads=[x_src_b[m_]], writes=[xin2[m_ % 2][1]])

    def load_rope(m_):
        k.dma(k.sp, rc[:], env["rope_cos"][:, m_ * TT_:(m_ + 1) * TT_], writes=[rc_b])
        k.dma(k.sp, rs[:], env["rope_ssin"][:, m_ * TT_:(m_ + 1) * TT_], writes=[rs_b])

    load_x(0)
    load_rope(0)
    nt.run(xin2[0][0], xin2[0][1], gcol, gcol_b, hT, hT_b)
    for mt in range(NMT_):
        t0 = mt * TT_
        xin, xin_b = xin2[mt % 2]

        def _sec_inproj():
            ipon = lambda i: env.get('ip_only') is None or i in env['ip_only']
            if ipon(0): inproj_fm(OFF_GQ, 128, 0); ACT(k, gq[:], B(0)[:, 0:TT_], AF.Copy, [Bb(0)], [gq_b])
            if ipon(1): inproj_fm(OFF_GK, 128, 1); ACT(k, gk[:], B(1)[:, 0:TT_], AF.Copy, [Bb(1)], [gk_b])
            for j in (range(2) if ipon(2) else ()):
                inproj_fm(OFF_DQ + 128 * j, 128, 0); inproj_fm(OFF_DQS + 128 * j, 128, 1)
                TT(k, k.dve, f1[:], B(0)[:, 0:TT_], rc[:], ALU.mult, [Bb(0), rc_b], [f1_b])
                TT(k, k.dve, f2[:], B(1)[:, 0:TT_], rs[:], ALU.mult, [Bb(1), rs_b], [f2_b])
                TT(k, k.pool, f3[:], f1[:], f2[:], ALU.add, [f1_b, f2_b], [f3_b])
                for i in range(4):
                    TS(k, k.pool if i % 2 else k.dve, qm[:, j, i, :], f3[:], hm[:, i:i + 1], ALU.mult, [f3_b, cb], [qm_b])
            for j in (range(2) if ipon(3) else ()):
                inproj_fm(OFF_DK + 128 * j, 128, 0); inproj_fm(OFF_DKS + 128 * j, 128, 1)
                TT(k, k.dve, f1[:], B(0)[:, 0:TT_], rc[:], ALU.mult, [Bb(0), rc_b], [f1_b])
                TT(k, k.dve, f2[:], B(1)[:, 0:TT_], rs[:], ALU.mult, [Bb(1), rs_b], [f2_b])
                TT(k, k.pool, kc[:, j, t0:t0 + TT_], f1[:], f2[:], ALU.add, [f1_b, f2_b], [kc_b[mt]])
            for j in (range(2) if ipon(4) else ()):
                inproj_fm(OFF_SU + 128 * j, 128, j)
                ACT(k, uf[:, j, :], B(j)[:, 0:TT_], AF.Copy, [Bb(j)], [uf_b])
                CP(k, k.pool, ub[:, j, :], uf[:, j, :], [uf_b], [ub_b])
            for j in (range(2) if ipon(5) else ()):
                inproj_fm(OFF_RX + 128 * j, 128, j)
                ACT(k, rxh[:, j, 3:3 + TT_], B(j)[:, 0:TT_], AF.Copy, [Bb(j)], [rxh_b])
            for j in (range(2) if ipon(6) else ()):
                inproj_fm(OFF_RG + 128 * j, 128, j)
                ACT(k, rg[:, j, :], B(j)[:, 0:TT_], AF.Copy, [Bb(j)], [rg_b])
            if ipon(7): inproj_fm(OFF_GLR, 16, 0); ACT(k, glr[:], B(0)[0:16, 0:TT_], AF.Copy, [Bb(0)], [glr_b])
            for n in (range(4) if ipon(8) else ()):
                bk = n % 2
                for kk in range(8):
                    MM(k, B(bk)[0:64, 0:512], hT[:, kk, n * 64:(n + 1) * 64], win[:, kk, OFF_TMG:OFF_TMG + 512],
                       kk == 0, kk == 7, [win_b[kk], hT_b], [Bb(bk)])
                ACT(k, gvc[:, n, :], B(bk)[0:64, 0:256], AF.Copy, [Bb(bk)], [gvc_b])
                CP(k, k.dve, ogc[:, n, :], B(bk)[0:64, 256:512], [Bb(bk)], [ogc_b])
            for s in (range(NS) if ipon(9) else ()):
                bk = s % 2
                for kk in range(8):
                    MM(k, B(bk)[:, 0:256], hT[:, kk, s * 128:(s + 1) * 128], win[:, kk, OFF_DV:OFF_DV + 256],
                       kk == 0, kk == 7, [win_b[kk], hT_b], [Bb(bk)])
                CP(k, k.dve, vc[:, 2 * mt + s, :, 0:64], B(bk)[:, 0:256].rearrange("p (h v) -> p h v", h=4),
                   [Bb(bk)], [vc_b[mt]])

        if "inproj" not in env.get("skip", ()):
            _sec_inproj()
        def _sec_gla():
            MM(k, B(0)[:, 0:TT_], P["wgate"][:], glr[:], True, True, [pb, glr_b], [Bb(0)])
            ACT(k, f1[:], B(0)[:, 0:TT_], AF.Exp, [Bb(0), pb], [f1_b], scale=-1.0, bias=P["negb"][:])
            ACT(k, f2[:], f1[:], AF.Ln, [f1_b, cb], [f2_b], bias=C["one"][:])
            k.op(k.dve, lambda h: h.tensor_tensor_scan(out=f3[:], data0=C["reset"][:], data1=f2[:], initial=0.0,
                                                       op0=ALU.mult, op1=ALU.add), reads=[f2_b, cb], writes=[f3_b])
            f3v = f3[:].rearrange("p (n c) -> p n c", n=4)
            ACT(k, f1[:], f3[:], AF.Exp, [f3_b], [f1_b], scale=-1.0 / 16.0)
            ACT(k, f2[:], f3[:], AF.Exp, [f3_b], [f2_b], scale=1.0 / 16.0)
            TT(k, k.dve, f4[:].rearrange("p (n c) -> p n c", n=4), f3v, f3v[:, :, 63:64].to_broadcast([128, 4, 64]),
               ALU.subtract, [f3_b], [f4_b])
            ACT(k, f5[:], f4[:], AF.Exp, [f4_b], [f5_b], scale=1.0 / 16.0)
            ACT(k, dec[:], f3v[:, :, 63], AF.Exp, [f3_b], [dec_b], scale=-1.0 / 16.0)
            STT(k, f4[:], gq[:], 32.0 ** -0.5, f1[:], ALU.mult, ALU.mult, [gq_b, f1_b], [f4_b])
            for h_ in range(4):
                TS(k, k.pool if h_ % 2 else k.dve, qdm[:, h_, :], f4[:], hm[:, h_:h_ + 1], ALU.mult, [f4_b, cb], [qdm_b])
            TT(k, k.dve, kd[:], gk[:], f2[:], ALU.mult, [gk_b, f2_b], [kd_b])
            TT(k, k.pool, ks[:], gk[:], f5[:], ALU.mult, [gk_b, f5_b], [ks_b])
            yield
            for n in range(4):
                c0 = n * 64
                gn = mt * 4 + n
                sp_, sn_ = gn % 2, (gn + 1) % 2
                TR(k, pT1[0:64, 0:128], ks[:, c0:c0 + 64], ident[:], [ks_b, ident_b], [pT1_b])
                CP(k, k.dve, kst[:], pT1[0:64, 0:128], [pT1_b], [kst_b])
                for h_ in range(4):
                    MM(k, B(0)[0:64, h_ * 64:(h_ + 1) * 64], kd[:, c0:c0 + 64], qdm[:, h_, c0:c0 + 64], True, True,
                       [kd_b, qdm_b], [Bb(0)])
                TT(k, k.dve, AT[:], B(0)[0:64, 0:256].rearrange("p (h c) -> p h c", h=4),
                   tri64[:].unsqueeze(1).to_broadcast([64, 4, 64]), ALU.mult, [Bb(0), cb], [AT_b])
                yield
                MM(k, B(1)[:, 0:256], kst[:], gvc[:, n, :], True, True, [kst_b, gvc_b], [Bb(1)])
                for h_ in range(4):
                    MM(k, B(0)[0:64, 256 + h_ * 64:256 + (h_ + 1) * 64], AT[:, h_, :], gvc[:, n, h_ * 64:(h_ + 1) * 64], True, False,
                       [AT_b, gvc_b], [Bb(0)])
                    MM(k, B(0)[0:64, 256 + h_ * 64:256 + (h_ + 1) * 64], qdm[:, h_, c0:c0 + 64], Sbf[:, sp_, :], False, True,
                       [qdm_b, Sbf_b], [Bb(0)])
                yield
                TT(k, k.dve, kvm.rearrange("p (h v) -> p h v", h=4), B(1)[:, 0:256].rearrange("p (h v) -> p h v", h=4),
                   hm[:].unsqueeze(2).to_broadcast([128, 4, 64]), ALU.mult, [Bb(1), cb], [kvm_b])
                k.op(k.dve, lambda h: h.reduce_sum(out=kvr[:], in_=kvm.rearrange("p (h v) -> p v h", h=4), axis=AX.X),
                     reads=[kvm_b], writes=[kvr_b])
                STT(k, Sst[:], Sst[:], dec[:, n:n + 1], kvr[:], ALU.mult, ALU.add, [Sst_b, dec_b, kvr_b], [Sst_b])
                CP(k, k.pool, Sbf[:, sn_, :], Sst[:], [Sst_b], [Sbf_b])
                yield
                ACT(k, osb[:], B(0)[0:64, 256:512], AF.Copy, [Bb(0)], [osb_b])
                TT(k, k.pool, osq, osb[:], osb[:], ALU.mult, [osb_b], [osq_b])
                k.op(k.dve, lambda h: h.reduce_sum(out=gst[:, 0:4], in_=osq.rearrange("p (h v) -> p h v", h=4), axis=AX.X),
                     reads=[osq_b], writes=[gst_b])
                ACT(k, gst[:, 4:8], gst[:, 0:4], AF.Ln, [gst_b], [gst_b], scale=1.0 / 64.0, bias=k.eps_ap[0:64, :])
                ACT(k, gst[:, 8:12], gst[:, 4:8], AF.Exp, [gst_b], [gst_b], scale=-0.5)
                ACT(k, sil[:], ogc[:, n, :], AF.Silu, [ogc_b], [sil_b])
                TT(k, k.dve, osb[:].rearrange("p (h v) -> p h v", h=4), osb[:].rearrange("p (h v) -> p h v", h=4),
                   gst[:, 8:12].unsqueeze(2).to_broadcast([64, 4, 64]), ALU.mult, [osb_b, gst_b], [osb_b])
                TT(k, k.pool, sil[:].rearrange("p (h v) -> p h v", h=4), sil[:].rearrange("p (h v) -> p h v", h=4),
                   P["gnorm"][0:64, :].unsqueeze(1).to_broadcast([64, 4, 64]), ALU.mult, [sil_b, pb], [sil_b])
                yield
                TT(k, k.dve, otk[:], osb[:], sil[:], ALU.mult, [osb_b, sil_b], [otk_b])
                for j in range(2):
                    TR(k, pT1[:, 128 + j * 64:128 + (j + 1) * 64], otk[:, j * 128:(j + 1) * 128], ident[0:64, 0:64],
                       [otk_b, ident_b], [pT1_b])
                yield
                CP(k, k.dve, mixT[:, 0:2, c0:c0 + 64], pT1[:, 128:256].rearrange("p (j c) -> p j c", j=2), [pT1_b], [mixT_b])

        def _sec_diff():
            for h_ in range(4):
                j = h_ // 2
                for c in range(2):
                    qsl = qm[:, j, (h_ % 2) * 2 + c, :]
                    def emit_s(i):
                        sb_ = sbi[0] % 2
                        sbi[0] += 1
                        bk, ptt, ptb = B(2 + sb_), pt[sb_][0], pt[sb_][1]
                        diag = (i == mt)
                        for u in range(2):
                            kt = 2 * i + u
                            qlo = 128 if (diag and u == 1) else 0
                            MM(k, bk[:, u * TT_ + qlo:(u + 1) * TT_], kc[:, j, kt * 128:(kt + 1) * 128], qsl[:, qlo:TT_],
                               True, True, [kc_b[i], qm_b], [Bb(2 + sb_)])
                        if not diag:
                            ACT(k, ptt[:, 0:2 * TT_], bk[:, 0:2 * TT_], AF.Exp, [Bb(2 + sb_)], [ptb], scale=QS)
                        else:
                            ACT(k, ptt[:, 0:TT_], bk[:, 0:TT_], AF.Exp, [Bb(2 + sb_)], [ptb], scale=QS)
                            ACT(k, ptt[:, TT_ + 128:2 * TT_], bk[:, TT_ + 128:2 * TT_], AF.Exp, [Bb(2 + sb_)], [ptb], scale=QS)
                            TT(k, k.pool, ptt[:, 0:128], ptt[:, 0:128], tri[:], ALU.mult, [ptb, cb], [ptb])
                            TT(k, k.pool, ptt[:, TT_ + 128:2 * TT_], ptt[:, TT_ + 128:2 * TT_], tri[:], ALU.mult, [ptb, cb], [ptb])
                        return sb_

                    def emit_pv(i, sb_):
                        ptt, ptb = pt[sb_][0], pt[sb_][1]
                        for u in range(2):
                            kt = 2 * i + u
                            for s in range(NS):
                                if kt > 2 * mt + s:
                                    continue
                                col = (c * 2 + s) * 65
                                MM(k, B(4)[:, col:col + 65], ptt[:, u * TT_ + s * 128:u * TT_ + (s + 1) * 128],
                                   vc[:, kt, h_, :], kt == 0 and c == 0 and s == 0, kt == 2 * mt + s,
                                   [ptb, vc_b[i], vc1_b], [Bb(4)], sgc=True)

                    prev = None
                    for i in range(mt + 1):
                        sb_ = emit_s(i)
                        if prev is not None:
                            emit_pv(*prev)
                        prev = (i, sb_)
                        yield
                    emit_pv(*prev)
                    yield
                for s in range(NS):
                    c1, c2 = (0 * 2 + s) * 65, (1 * 2 + s) * 65
                    k.op(k.dve, lambda h: h.reciprocal(out=dst[:, 0:1], in_=B(4)[:, c1 + 64:c1 + 65]), reads=[Bb(4)], writes=[dst_b])
                    k.op(k.dve, lambda h: h.reciprocal(out=dst[:, 1:2], in_=B(4)[:, c2 + 64:c2 + 65]), reads=[Bb(4)], writes=[dst_b])
                    TT(k, k.dve, dst[:, 2:3], dst[:, 1:2], P["nlam"], ALU.mult, [dst_b, pb], [dst_b])
                    TS(k, k.dve, dt1[:], B(4)[:, c1:c1 + 64], dst[:, 0:1], ALU.mult, [Bb(4), dst_b], [dt1_b])
                    STT(k, od[:, s, h_ * 64:(h_ + 1) * 64], B(4)[:, c2:c2 + 64], dst[:, 2:3], dt1[:], ALU.mult, ALU.add,
                        [Bb(4), dst_b, dt1_b], [od_b])
            yield
            TT(k, k.pool, odq, od[:], od[:], ALU.mult, [od_b], [odq_b])
            k.op(k.dve, lambda h: h.reduce_sum(out=dst[:, 8:16], in_=odq.rearrange("p s (h v) -> p (s h) v", h=4), axis=AX.X),
                 reads=[odq_b], writes=[dst_b])
            ACT(k, dst[:, 16:24], dst[:, 8:16], AF.Ln, [dst_b], [dst_b], scale=1.0 / 64.0, bias=k.eps_ap)
            ACT(k, dst[:, 24:32], dst[:, 16:24], AF.Exp, [dst_b], [dst_b], scale=-0.5)
            TT(k, k.dve, od[:].rearrange("p s (h v) -> p (s h) v", h=4), od[:].rearrange("p s (h v) -> p (s h) v", h=4),
               dst[:, 24:32].unsqueeze(2).to_broadcast([128, 8, 64]), ALU.mult, [od_b, dst_b], [od_b])
            TT(k, k.dve, odb[:].rearrange("p s (h v) -> p (s h) v", h=4), od[:].rearrange("p s (h v) -> p (s h) v", h=4),
               P["dnorm"][:].unsqueeze(1).to_broadcast([128, 8, 64]), ALU.mult, [od_b, pb], [odb_b])
            for s in range(NS):
                for j in range(2):
                    TR(k, pT1[:, 256 + (s * 2 + j) * 128:256 + (s * 2 + j + 1) * 128], odb[:, s, j * 128:(j + 1) * 128], ident[:],
                       [odb_b, ident_b], [pT1_b])
            for s in range(NS):
                CP(k, k.dve, mixT[:, 2:4, s * 128:(s + 1) * 128],
                   pT1[:, 256 + s * 256:256 + (s + 1) * 256].rearrange("p (j c) -> p j c", j=2), [pT1_b], [mixT_b])

        def _sec_s5():
            for qt in range(4):
                q0 = qt * 64
                MM(k, B(1)[:, 256:272], C["jf"][:], zlast[:], True, True, [cb, zlast_b], [Bb(1)])
                TT(k, k.dve, zt[:], B(1)[:, 256:272], P["s64"], ALU.mult, [Bb(1), pb], [zt_b])
                TT(k, k.dve, zin[:], zlast[:], P["c64"], ALU.mult, [zlast_b, pb], [zin_b])
                TT(k, k.dve, zin[:], zin[:], zt[:], ALU.subtract, [zin_b, zt_b], [zin_b])
                TT(k, k.dve, zin[:], zin[:], P["rho"][:], ALU.mult, [zin_b, pb], [zin_b])
                for ch in range(2):
                    for hb in range(2):
                        g0 = 8 * ch + 4 * hb
                        for gg in range(4):
                            MM(k, B(5)[:, gg * 64:(gg + 1) * 64], P["bpad"][:, 0, g0 + gg, :], ub[:, ch, q0:q0 + 64],
                               True, True, [pb, ub_b], [Bb(5)])
                            MM(k, B(5)[:, 256 + gg * 64:256 + (gg + 1) * 64], P["bpad"][:, 1, g0 + gg, :], ub[:, ch, q0:q0 + 64],
                               True, True, [pb, ub_b], [Bb(5)])
                        tc4 = P["tabc"][:, g0:g0 + 4, :].rearrange("p g j -> p (g j)")
                        ts4 = P["tabs"][:, g0:g0 + 4, :].rearrange("p g j -> p (g j)")
                        TT(k, k.dve, m1[:], B(5)[:, 0:256], tc4, ALU.mult, [Bb(5), pb], [m1_b])
                        TT(k, k.dve, m2[:], B(5)[:, 256:512], ts4, ALU.mult, [Bb(5), pb], [m2_b])
                        yield
                        TT(k, k.pool, m1[:], m1[:], m2[:], ALU.add, [m1_b, m2_b], [m1_b])
                        m1v = m1[:].rearrange("p (g j) -> p g j", g=4)
                        TT(k, k.dve, m1v[:, :, 0], m1v[:, :, 0], zin[:, g0:g0 + 4], ALU.add, [m1_b, zin_b], [m1_b])
                        k.op(k.dve, lambda h: h.tensor_tensor_scan(
                            out=zz[:], data0=P["rhot"][:, g0:g0 + 4, :].rearrange("p g j -> p (g j)"), data1=m1[:],
                            initial=0.0, op0=ALU.mult, op1=ALU.add), reads=[pb, m1_b], writes=[zz_b])
                        yield
                        CP(k, k.pool, zlast[:, g0:g0 + 4], zz[:].rearrange("p (g j) -> p g j", g=4)[:, :, 63], [zz_b], [zlast_b])
                        TT(k, k.pool, zc[:], zz[:], tc4, ALU.mult, [zz_b, pb], [zc_b])
                        TT(k, k.dve, zs[:], zz[:], ts4, ALU.mult, [zz_b, pb], [zs_b])
                        yield
                        yo = ch * TT_ + q0
                        for gg in range(4):
                            MM(k, B(6)[:, yo:yo + 64], P["cpad"][:, 0, g0 + gg, :], zc[:, gg * 64:(gg + 1) * 64],
                               hb == 0 and gg == 0, False, [pb, zc_b], [Bb(6)])
                            MM(k, B(6)[:, yo:yo + 64], P["cpad"][:, 1, g0 + gg, :], zs[:, gg * 64:(gg + 1) * 64],
                               False, hb == 1 and gg == 3, [pb, zs_b], [Bb(6)])
                        yield
            for ch in range(2):
                STT(k, sf1[:], uf[:, ch, :], P["lv"][:, 4, ch:ch + 1], B(6)[:, ch * TT_:(ch + 1) * TT_], ALU.mult, ALU.add,
                    [uf_b, pb, Bb(6)], [sf1_b])
                ACT(k, sf2[:], sf1[:], AF.Gelu_apprx_tanh, [sf1_b], [sf2_b])
                CP(k, k.pool, yb[:, ch, :], sf2[:], [sf2_b], [yb_b])
                yield
            for nch in range(2):
                for kk in range(2):
                    MM(k, B(5)[:, nch * TT_:(nch + 1) * TT_], P["wglu"][:, kk, nch * 128:(nch + 1) * 128],
                       yb[:, kk, :], kk == 0, kk == 1, [pb, yb_b], [Bb(5)])
                ACT(k, sf1[:], B(5)[:, nch * TT_:(nch + 1) * TT_], AF.Sigmoid, [Bb(5), pb], [sf1_b], bias=P["bglu"][:, nch:nch + 1])
                TT(k, k.dve, mixT[:, 4 + nch, :], yb[:, nch, :], sf1[:], ALU.mult, [yb_b, sf1_b], [mixT_b])
                yield

        def _sec_lru():
            for ch in range(2):
                cw = P["cw"]
                TS(k, k.dve, f1[:], rxh[:, ch, 3:3 + TT_], cw[:, ch, 3:4], ALU.mult, [rxh_b, pb], [f1_b],
                   s2=P["lv"][:, 0, ch:ch + 1], op1=ALU.add)
                for tap in range(3):
                    STT(k, f1[:], rxh[:, ch, tap:tap + TT_], cw[:, ch, tap:tap + 1], f1[:], ALU.mult, ALU.add,
                        [rxh_b, pb, f1_b], [f1_b])
                CP(k, k.pool, xcb[:], f1[:], [f1_b], [xcb_b])
                yield
                MM(k, B(0)[:, 0:TT_], P["wa"][:, ch, :], xcb[:], True, True, [pb, xcb_b], [Bb(0)])
                MM(k, B(1)[:, 0:TT_], P["wx"][:, ch, :], xcb[:], True, True, [pb, xcb_b], [Bb(1)])
                ACT(k, f2[:], B(0)[:, 0:TT_], AF.Sigmoid, [Bb(0), pb], [f2_b], bias=P["lv"][:, 1, ch:ch + 1])
                ACT(k, f3[:], B(1)[:, 0:TT_], AF.Sigmoid, [Bb(1), pb], [f3_b], bias=P["lv"][:, 2, ch:ch + 1])
                ACT(k, f4[:], f2[:], AF.Exp, [f2_b, pb], [f4_b], scale=P["lsp"][:, 2, ch:ch + 1])
                ACT(k, f5[:], f2[:], AF.Exp, [f2_b, pb], [f5_b], scale=P["lsp"][:, 3, ch:ch + 1])
                yield
                TS(k, k.dve, f5[:], f5[:], -1.0, ALU.mult, [f5_b], [f5_b], s2=1.0, op1=ALU.add)
                TS(k, k.dve, f5[:], f5[:], 1e-12, ALU.max, [f5_b], [f5_b])
                ACT(k, f5[:], f5[:], AF.Sqrt, [f5_b], [f5_b])
                TT(k, k.pool, f3[:], f3[:], f1[:], ALU.mult, [f3_b, f1_b], [f3_b])
                TT(k, k.dve, f3[:], f3[:], f5[:], ALU.mult, [f3_b, f5_b], [f3_b])
                yield
                k.op(k.dve, lambda h: h.tensor_tensor_scan(out=f2[:], data0=f4[:], data1=f3[:], initial=hst[:, ch:ch + 1],
                                                           op0=ALU.mult, op1=ALU.add),
                     reads=[f4_b, f3_b, hst_b], writes=[f2_b])
                CP(k, k.pool, hst[:, ch:ch + 1], f2[:, TT_ - 1:TT_], [f2_b], [hst_b])
                ACT(k, f1[:], rg[:, ch, :], AF.Gelu_apprx_tanh, [rg_b], [f1_b])
                TT(k, k.dve, mixT[:, 6 + ch, :], f2[:], f1[:], ALU.mult, [f2_b, f1_b], [mixT_b])
            yield
            CP(k, k.pool, rxh[:, :, 0:3], rxh[:, :, TT_:TT_ + 3], [rxh_b], [rxh_b])

        def _chain(*fs):
            for f_ in fs:
                for _ in f_():
                    yield
        SK = env.get("skip", ())
        streams = [_chain(*[f_ for n_, f_ in (("gla", _sec_gla), ("lru", _sec_lru)) if n_ not in SK])]
        if mt + 1 < NMT_:
            load_x(mt + 1)
            load_rope(mt + 1)
            streams.append(nt.run_gen(xin2[(mt + 1) % 2][0], xin2[(mt + 1) % 2][1], gcol, gcol_b, hT, hT_b))
        if "diff" not in SK:
            streams.append(_sec_diff())
        if "s5" not in SK:
            streams.append(_sec_s5())
        if env.get("serial"):
            for g_ in streams:
                for _ in g_:
                    pass
        else:
            while streams:
                for g_ in list(streams):
                    try:
                        next(g_)
                    except StopIteration:
                        streams.remove(g_)

        if dbg is not None:
            k.dma(k.sp, dbg[:, :, t0:t0 + TT_].rearrange("j p t -> p j t"), mixT[:], reads=[mixT_b], writes=[Buf("dbg")])

        def _sec_out():
            for s in range(NS):
                for hf in range(2):
                    for kk in range(8):
                        MM(k, B(hf)[:, 0:512], mixT[:, kk, s * 128:(s + 1) * 128], wout[:, kk, hf * 512:(hf + 1) * 512],
                           kk == 0, kk == 7, [mixT_b, wout_b], [Bb(hf)])
                post_norm_residual(k, [B(0)[:, 0:512], B(1)[:, 0:512]], [Bb(0), Bb(1)], xin[:, s, :], xin_b, gpost, gpost_b,
                                   st2, st2_b, nt.junk, nt.junk_b, tmpo, tmpo_b, xin[:, s, :], xin_b)

        if "out" not in env.get("skip", ()):
            _sec_out()
        k.dma(k.sp, xd[mt], xin[:], reads=[xin_b], writes=[x_dst_b[mt]])
    k.pop()


WEIGHT_SPECS = [
    ("ln_mix_post", [L, D]), ("ln_ffn_post", [L, D]),
    ("ffn_w1", [L, D, DFF]), ("ffn_w2", [L, DFF, D]),
    ("ln_mix_pre_col", [L, 128, 8]), ("ln_ffn_pre_col", [L, 128, 8]),
    ("w_in_ext", [L, D, NCOL]), ("w_out", [L, D, D]),
    ("rope_cos", [128, S]), ("rope_ssin", [128, S]),
    ("gla_w_gate", [L, 16, 128]), ("gla_b_gate_col", [L, 128, 1]), ("gla_norm", [L, 64]),
    ("diff_lq1", [L, 32]), ("diff_lk1", [L, 32]), ("diff_lq2", [L, 32]), ("diff_lk2", [L, 32]), ("diff_norm", [L, 64]),
    ("s5_log_step", [L, 16]), ("s5_a_re_T", [L, 64, 16]), ("s5_a_im_T", [L, 64, 16]),
    ("s5_b_re_p", [L, 64, 256]), ("s5_b_im_p", [L, 64, 256]), ("s5_c_re_p", [L, 64, 256]), ("s5_c_im_p", [L, 64, 256]),
    ("s5_d_col", [L, 128, 2]), ("s5_w_glu", [L, 256, 256]), ("s5_b_glu_col", [L, 128, 2]),
    ("lru_conv_w_col", [L, 128, 8]), ("lru_conv_b_col", [L, 128, 2]), ("lru_w_a", [L, 4, 64, 64]), ("lru_w_x", [L, 4, 64, 64]),
    ("lru_b_a_col", [L, 128, 2]), ("lru_b_x_col", [L, 128, 2]), ("lru_lambda_col", [L, 128, 2]),
]


def build(mode="full", opts=None):
    nc = bass.Bass("TRN2", target_bir_lowering=False)
    env = dict(opts or {})
    x = nc.dram_tensor("x", [S, D], F32, kind="ExternalInput").ap()
    for name, shape in WEIGHT_SPECS:
        env[name] = nc.dram_tensor(name, shape, F32, kind="ExternalInput").ap()
    env["lru_conv_w_col"] = env["lru_conv_w_col"].rearrange("l p (c t) -> l p c t", c=2)
    y = nc.dram_tensor("y", [S, D], F32, kind="ExternalOutput").ap()
    k = K(nc)
    eps = k.sb([128, 1], F32, "eps_c")
    eps_b = Buf("eps")
    k.op(k.pool, lambda h: h.memset(eps[:], EPS), writes=[eps_b])
    k.eps_ap = eps[:]
    env["banks"] = [k.ps([128, 512], F32, "bank%d" % i) for i in range(7)]
    env["bank_b"] = [Buf("bank%d" % i, True) for i in range(7)]
    env["pT1"] = k.ps([128, 1024], BF16, "pT1")
    env["pT1_b"] = Buf("pT1", True)
    env["pT"], env["pT_b"] = env["pT1"], env["pT1_b"]
    env["ident"], env["ident_b"] = make_ident(k, nc)
    C = make_consts(k, env)
    k.barrier()

    NMT = 16
    mk = lambda p: [Buf("%s%d" % (p, i)) for i in range(NMT)]
    x_b, y_b = mk("x"), mk("y")
    dbg = None
    if mode == "ffn0":
        ffn_pass(k, env, 0, x, x_b, y, y_b)
    elif mode == "mix0":
        dbg = nc.dram_tensor("dbg", [8, 128, S], BF16, kind="ExternalOutput").ap()
        mixer_pass(k, env, 0, x, x_b, y, y_b, C, dbg=dbg)
    elif mode == "full":
        xa = nc.dram_tensor("xa", [S, D], F32, kind="Internal").ap()
        xb = nc.dram_tensor("xb", [S, D], F32, kind="Internal").ap()
        xa_b, xb_b = mk("xa"), mk("xb")
        mixer_pass(k, env, 0, x, x_b, xa, xa_b, C)
        ffn_pass(k, env, 0, xa, xa_b, xb, xb_b)
        mixer_pass(k, env, 1, xb, xb_b, xa, xa_b, C)
        ffn_pass(k, env, 1, xa, xa_b, y, y_b)
    k.finish(y_b)
    k.barrier()
    return nc, env


_CACHE = {}


def _rope_tables():
    inv = 10000.0 ** (-np.arange(0, 32, 2, dtype=np.float32) / 32.0)
    ang = np.arange(S, dtype=np.float32)[:, None] * inv[None, :].astype(np.float32)
    ang = ang.astype(np.float32)
    emb = np.concatenate([ang, ang], axis=-1)
    cos = np.cos(emb).astype(np.float32).T
    sin = np.sin(emb).astype(np.float32).T
    ssin = np.concatenate([-sin[:16], sin[16:]], axis=0)
    return np.tile(cos, (4, 1)), np.tile(ssin, (4, 1))


def _prep_weights(inputs):
    src = {k_: np.asarray(v) for k_, v in inputs.items()}
    col8 = lambda a: a.reshape(L, 8, 128).transpose(0, 2, 1)
    col2 = lambda a: a.reshape(L, 2, 128).transpose(0, 2, 1)
    src["ln_mix_pre_col"] = col8(src["ln_mix_pre"])
    src["ln_ffn_pre_col"] = col8(src["ln_ffn_pre"])
    w = src["w_in"]
    seg = lambda a, b: w[:, :, a:b]

    def swap(a0):
        parts = []
        for blk in range(8):
            b0 = a0 + blk * 32
            parts += [w[:, :, b0 + 16:b0 + 32], w[:, :, b0:b0 + 16]]
        return np.concatenate(parts, axis=-1)

    src["w_in_ext"] = np.concatenate([
        seg(0, 128), seg(128, 256), seg(784, 1040), swap(784), seg(1040, 1296), swap(1040),
        seg(1552, 1808), seg(1808, 2064), seg(2064, 2320), seg(512, 528), np.zeros((L, D, 48), np.float32),
        seg(256, 512), seg(528, 784), seg(1296, 1552)], axis=-1)
    assert src["w_in_ext"].shape[-1] == NCOL
    src["rope_cos"], src["rope_ssin"] = _rope_tables()
    src["gla_b_gate_col"] = src["gla_b_gate"].reshape(L, 128, 1)
    src["s5_a_re_T"] = src["s5_a_re"].transpose(0, 2, 1)
    src["s5_a_im_T"] = src["s5_a_im"].transpose(0, 2, 1)
    src["s5_b_re_p"] = src["s5_b_re"].transpose(0, 2, 1, 3).reshape(L, 64, 256)
    src["s5_b_im_p"] = src["s5_b_im"].transpose(0, 2, 1, 3).reshape(L, 64, 256)
    src["s5_c_re_p"] = src["s5_c_re"].transpose(0, 3, 1, 2).reshape(L, 64, 256)
    src["s5_c_im_p"] = src["s5_c_im"].transpose(0, 3, 1, 2).reshape(L, 64, 256)
    src["s5_d_col"] = col2(src["s5_d"])
    src["s5_b_glu_col"] = col2(src["s5_b_glu"])
    src["lru_conv_w_col"] = src["lru_conv_w"].reshape(L, 4, 2, 128).transpose(0, 3, 2, 1).reshape(L, 128, 8)
    for nm_ in ("lru_conv_b", "lru_b_a", "lru_b_x", "lru_lambda"):
        src[nm_ + "_col"] = col2(src[nm_])
    out = {}
    for name, shape in WEIGHT_SPECS:
        a = np.ascontiguousarray(src[name], dtype=np.float32)
        assert list(a.shape) == list(shape), (name, a.shape, shape)
        out[name] = a
    return out


def run(inputs, mode="full", want=("y",), opts=None):
    key = (mode, repr(sorted((opts or {}).items())))
    if key not in _CACHE:
        _CACHE[key] = build(mode, opts)
    nc, env = _CACHE[key]
    w = _prep_weights(inputs)
    x = np.ascontiguousarray(inputs["x"], dtype=np.float32)
    in_maps = []
    for c in range(NCORES):
        m = dict(w)
        m["x"] = x[c]
        in_maps.append(m)
    res = run_bass_kernel_spmd(nc, in_maps, core_ids=list(range(NCORES)))
    outs = [np.stack([r[nm_] for r in res.results], axis=0) for nm_ in want]
    return outs[0] if len(outs) == 1 else outs


def kernel(**inputs):
    return run(inputs, "full")
```

```python
import math
import numpy as np
import concourse.bass as bass
import concourse.mybir as mybir
from concourse.bass_utils import run_bass_kernel_spmd

F32 = mybir.dt.float32
BF16 = mybir.dt.bfloat16
I32 = mybir.dt.int32
AF = mybir.ActivationFunctionType
ALU = mybir.AluOpType
AX = mybir.AxisListType

S = 4096
D = 1024
DFF = 4096
L = 2
EPS = 1e-6
NCORES = 8


class Buf:
    __slots__ = ("name", "w", "r", "excl")

    def __init__(self, name, excl=False):
        self.name = name
        self.w = {}
        self.r = {}
        self.excl = excl


class Eng:
    def __init__(self, nc, name, h, sync_self):
        self.name = name
        self.h = h
        self.sem = nc.semaphore("sem_" + name).__enter__()
        self.cnt = 0
        self.waited = {}
        self.sync_self = sync_self


class K:
    def __init__(self, nc, n_dma_sems=20):
        self.nc = nc
        self.pe = Eng(nc, "pe", nc.tensor, False)
        self.act = Eng(nc, "act", nc.scalar, True)
        self.dve = Eng(nc, "dve", nc.vector, True)
        self.pool = Eng(nc, "pool", nc.gpsimd, True)
        self.sp = Eng(nc, "sp", nc.sync, False)
        self.dsem = {}
        for q in ("sp", "pool", "act"):
            self.dsem[q] = [[nc.semaphore("dq_%s_%d" % (q, i)).__enter__(), 0] for i in range(n_dma_sems)]
        self.dnext = {"sp": 0, "pool": 0, "act": 0}
        self._n = 0
        self.stacks = []

    def push(self):
        from contextlib import ExitStack
        self.stacks.append(ExitStack())

    def pop(self):
        self.barrier()
        self.stacks.pop().close()

    def barrier(self):
        engs = [self.pe, self.act, self.dve, self.pool, self.sp]
        deps = [(e.name, e.sem, e.cnt) for e in engs if e.cnt > 0]
        for q, pool in self.dsem.items():
            for j, (sem, cnt) in enumerate(pool):
                if cnt > 0:
                    deps.append(("d_%s_%d" % (q, j), sem, 16 * cnt))
        for e in engs:
            sv = e.sync_self
            e.sync_self = False
            self._wait(e, deps)
            e.sync_self = sv

    def uid(self, p="t"):
        self._n += 1
        return "%s%d" % (p, self._n)

    def sb(self, shape, dtype, name=None):
        cm = self.nc.sbuf_tensor(name or self.uid("sb"), list(shape), dtype)
        if self.stacks:
            return self.stacks[-1].enter_context(cm)
        return cm.__enter__()

    def ps(self, shape, dtype, name=None):
        return self.nc.psum_tensor(name or self.uid("ps"), list(shape), dtype).__enter__()

    def _wait(self, e, deps):
        best = {}
        for (key, sem, val) in deps:
            if key == e.name and not e.sync_self:
                continue
            if key not in best or best[key][1] < val:
                best[key] = (sem, val)
        for key, (sem, val) in best.items():
            if e.waited.get(key, 0) >= val:
                continue
            e.h.wait_ge(sem, val)
            e.waited[key] = val

    def _deps(self, reads, writes):
        deps = []
        for b in reads:
            deps.extend(b.w.values())
            if b.excl:
                deps.extend(b.r.values())
        for b in writes:
            deps.extend(b.w.values())
            deps.extend(b.r.values())
        return deps

    def op(self, e, fn, reads=(), writes=(), inc=True):
        self._wait(e, self._deps(reads, writes))
        inst = fn(e.h)
        if inc:
            e.cnt += 1
            inst.then_inc(e.sem, 1)
            t = (e.name, e.sem, e.cnt)
        else:
            assert e is self.pe
            t = (e.name, e.sem, e.cnt + 1)
        for b in reads:
            b.r[e.name] = t
        for b in writes:
            b.w[e.name] = t
        return t

    def dma(self, q, out, in_, reads=(), writes=()):
        pool = self.dsem[q.name]
        j = self.dnext[q.name]
        self.dnext[q.name] = (j + 1) % len(pool)
        sem, cnt = pool[j]
        key = "d_%s_%d" % (q.name, j)
        deps = self._deps(reads, writes)
        if cnt > 0:
            deps.append((key, sem, 16 * cnt))
        self._wait(q, deps)
        q.h.dma_start(out=out, in_=in_).then_inc(sem, 16)
        pool[j][1] = cnt + 1
        t = (key, sem, 16 * (cnt + 1))
        for b in reads:
            b.r[key] = t
        for b in writes:
            b.w[key] = t
        return t

    def finish(self, bufs):
        deps = []
        for b in bufs:
            deps.extend(b.w.values())
        self._wait(self.sp, deps)


def make_ident(k, nc):
    ident = k.sb([128, 128], BF16, "ident")
    b = Buf("ident")
    k.op(k.pool, lambda h: h.memset(ident[:], 1.0), writes=[b])
    k.op(k.pool, lambda h: h.affine_select(out=ident[:], in_=ident[:], pattern=[[1, 128]],
                                            compare_op=ALU.is_equal, fill=0.0, base=0,
                                            channel_multiplier=-1), reads=[b], writes=[b])
    return ident, b


class NormT:
    def __init__(self, k, ns, ident, ident_b, pT, pT_b, tag):
        self.k = k
        self.ns = ns
        self.ident, self.ident_b = ident, ident_b
        self.pT, self.pT_b = pT, pT_b
        self.xn = k.sb([128, 1024], BF16, tag + "_xn")
        self.xn_b = Buf(tag + "_xn")
        self.junk, self.junk_b = self.xn, self.xn_b
        self.st = k.sb([128, 4 * ns], F32, tag + "_st")
        self.st_b = Buf(tag + "_st")

    def run(self, xin, xin_b, gcol, gcol_b, hT, hT_b):
        for _ in self.run_gen(xin, xin_b, gcol, gcol_b, hT, hT_b):
            pass

    def run_gen(self, xin, xin_b, gcol, gcol_b, hT, hT_b):
        k, ns = self.k, self.ns
        st = self.st
        for s in range(ns):
            k.op(k.act, lambda h: h.activation(out=self.junk[:], in_=xin[:, s, :], func=AF.Square,
                                               accum_out=st[:, s:s + 1]),
                 reads=[xin_b], writes=[self.junk_b, self.st_b])
        k.op(k.act, lambda h: h.activation(out=st[:, ns:2 * ns], in_=st[:, 0:ns], func=AF.Ln,
                                           scale=1.0 / D, bias=self.k.eps_ap),
             reads=[self.st_b], writes=[self.st_b])
        k.op(k.act, lambda h: h.activation(out=st[:, 2 * ns:3 * ns], in_=st[:, ns:2 * ns], func=AF.Exp,
                                           scale=-0.5),
             reads=[self.st_b], writes=[self.st_b])
        yield
        for s in range(ns):
            k.op(k.act, lambda h: h.activation(out=self.xn[:], in_=xin[:, s, :], func=AF.Copy,
                                               scale=st[:, 2 * ns + s:2 * ns + s + 1]),
                 reads=[xin_b, self.st_b], writes=[self.xn_b])
            for kk in range(8):
                k.op(k.pe, lambda h: h.transpose(out=self.pT[:, kk * 128:(kk + 1) * 128],
                                                 in_=self.xn[:, kk * 128:(kk + 1) * 128],
                                                 identity=self.ident[:]),
                     reads=[self.xn_b, self.ident_b], writes=[self.pT_b], inc=(kk == 7))
            k.op(k.dve, lambda h: h.tensor_tensor(
                out=hT[:, :, s * 128:(s + 1) * 128],
                in0=self.pT[:].rearrange("p (k t) -> p k t", k=8),
                in1=gcol[:].unsqueeze(2).to_broadcast([128, 8, 128]),
                op=ALU.mult),
                reads=[self.pT_b, gcol_b], writes=[hT_b])
            yield


def post_norm_residual(k, ps_halves, ps_bufs, xres_ap, xres_b, gpost, gpost_b, st, st_b, junk, junk_b,
                       tmp, tmp_b, out_ap, out_b):
    for hf in range(2):
        k.op(k.act, lambda h: h.activation(out=tmp[:, hf * 512:(hf + 1) * 512], in_=ps_halves[hf], func=AF.Square,
                                           accum_out=st[:, hf:hf + 1]),
             reads=[ps_bufs[hf]], writes=[tmp_b, st_b])
    k.op(k.dve, lambda h: h.tensor_tensor(out=st[:, 2:3], in0=st[:, 0:1], in1=st[:, 1:2], op=ALU.add),
         reads=[st_b], writes=[st_b])
    k.op(k.act, lambda h: h.activation(out=st[:, 3:4], in_=st[:, 2:3], func=AF.Ln, scale=1.0 / D, bias=k.eps_ap),
         reads=[st_b], writes=[st_b])
    k.op(k.act, lambda h: h.activation(out=st[:, 4:5], in_=st[:, 3:4], func=AF.Exp, scale=-0.5),
         reads=[st_b], writes=[st_b])
    for hf in range(2):
        k.op(k.dve, lambda h: h.scalar_tensor_tensor(out=tmp[:, hf * 512:(hf + 1) * 512], in0=ps_halves[hf],
                                                     scalar=st[:, 4:5], in1=gpost[:, hf * 512:(hf + 1) * 512],
                                                     op0=ALU.mult, op1=ALU.mult),
             reads=[ps_bufs[hf], st_b, gpost_b], writes=[tmp_b])
    k.op(k.pool, lambda h: h.tensor_tensor(out=out_ap, in0=tmp[:], in1=xres_ap, op=ALU.add),
         reads=[tmp_b, xres_b], writes=[out_b])


def ffn_pass(k, env, l, x_src, x_src_b, x_dst, x_dst_b):
    nc = k.nc
    NS = 2
    TT = NS * 128
    NMT = S // TT
    banks, bank_b = env["banks"], env["bank_b"]
    pT, pT_b = env["pT"], env["pT_b"]
    k.push()
    sbase = nc.sbuf_bytes_remaining

    w1 = k.sb([128, 8, DFF], BF16, "w1_%d" % l)
    w2 = k.sb([128, 32, D], BF16, "w2_%d" % l)
    w1_b = [Buf("w1f%d" % i) for i in range(8)]
    w2_b = [Buf("w2k%d" % i) for i in range(8)]
    w1d = env["ffn_w1"][l].rearrange("(k p) f -> p k f", p=128)
    w2d = env["ffn_w2"][l].rearrange("(k p) f -> p k f", p=128)
    for i in range(8):
        k.dma(k.pool, w1[:, :, i * 512:(i + 1) * 512], w1d[:, :, i * 512:(i + 1) * 512], writes=[w1_b[i]])
    for i in range(8):
        k.dma(k.pool, w2[:, 4 * i:4 * i + 4, :], w2d[:, 4 * i:4 * i + 4, :], writes=[w2_b[i]])

    gcol = k.sb([128, 8], F32, "f_gcol%d" % l)
    gcol_b = Buf("gcol")
    k.dma(k.sp, gcol[:], env["ln_ffn_pre_col"][l], writes=[gcol_b])
    gpost = k.sb([128, D], F32, "f_gpost%d" % l)
    gpost_b = Buf("gpost")
    k.dma(k.sp, gpost[:], env["ln_ffn_post"][l:l + 1, :].partition_broadcast(128), writes=[gpost_b])

    xin = [k.sb([128, NS, D], F32, "f_xin%d_%d" % (l, i)) for i in range(2)]
    xin_b = [Buf("xin%d" % i) for i in range(2)]
    hT = [k.sb([128, 8, TT], BF16, "f_hT%d_%d" % (l, i)) for i in range(2)]
    hT_b = [Buf("hT%d" % i) for i in range(2)]
    hid = k.sb([128, 32, TT], BF16, "f_hid%d" % l)
    hid_b = [Buf("hid%d" % i) for i in range(4)]
    rl = [k.sb([128, TT], F32, "f_rl%d_%d" % (l, i)) for i in range(2)]
    rl_b = [Buf("rl%d" % i) for i in range(2)]
    tmp = k.sb([128, D], F32, "f_tmp%d" % l)
    tmp_b = Buf("tmp")
    st2 = k.sb([128, 8], F32, "f_st2%d" % l)
    st2_b = Buf("st2")
    nt = NormT(k, NS, env["ident"], env["ident_b"], pT, pT_b, "f_nt%d" % l)

    xs = x_src.rearrange("(m s p) d -> m p s d", s=NS, p=128)
    xd = x_dst.rearrange("(m s p) d -> m p s d", s=NS, p=128)

    def load(mt):
        k.dma(k.sp, xin[mt % 2][:], xs[mt], reads=[x_src_b[mt]], writes=[xin_b[mt % 2]])

    load(0)
    for mt in range(NMT):
        sl = mt % 2
        if mt + 1 < NMT:
            load(mt + 1)
        nt.run(xin[sl], xin_b[sl], gcol, gcol_b, hT[sl], hT_b[sl])
        for fc in range(32):
            pb = fc % 2
            for kk in range(8):
                k.op(k.pe, lambda h: h.matmul(banks[pb][:, 0:TT], lhsT=w1[:, kk, fc * 128:(fc + 1) * 128],
                                              rhs=hT[sl][:, kk, :], start=(kk == 0), stop=(kk == 7)),
                     reads=[w1_b[fc // 4], hT_b[sl]], writes=[bank_b[pb]], inc=(kk == 7))
            k.op(k.act, lambda h: h.activation(out=rl[pb][:], in_=banks[pb][:, 0:TT], func=AF.Relu),
                 reads=[bank_b[pb]], writes=[rl_b[pb]])
            k.op(k.pool, lambda h: h.tensor_tensor(out=hid[:, fc, :], in0=rl[pb][:], in1=rl[pb][:], op=ALU.mult),
                 reads=[rl_b[pb]], writes=[hid_b[fc // 8]])
        for s in range(NS):
            pbs = [2 + 2 * (s % 2), 3 + 2 * (s % 2)]
            for hf in range(2):
                for fc in range(32):
                    k.op(k.pe, lambda h: h.matmul(banks[pbs[hf]][:, 0:512], lhsT=hid[:, fc, s * 128:(s + 1) * 128],
                                                  rhs=w2[:, fc, hf * 512:(hf + 1) * 512],
                                                  start=(fc == 0), stop=(fc == 31)),
                         reads=[hid_b[fc // 8], w2_b[fc // 4]], writes=[bank_b[pbs[hf]]], inc=(fc == 31))
            post_norm_residual(k, [banks[pbs[0]][:, 0:512], banks[pbs[1]][:, 0:512]],
                               [bank_b[pbs[0]], bank_b[pbs[1]]],
                               xin[sl][:, s, :], xin_b[sl], gpost, gpost_b, st2, st2_b,
                               nt.junk, nt.junk_b, tmp, tmp_b, xin[sl][:, s, :], xin_b[sl])
        k.dma(k.sp, xd[mt], xin[sl][:], reads=[xin_b[sl]], writes=[x_dst_b[mt]])
    env["sbuf_used_ffn"] = sbase - nc.sbuf_bytes_remaining
    k.pop()


def TT(k, e, out, in0, in1, op, R, W):
    return k.op(e, lambda h: h.tensor_tensor(out=out, in0=in0, in1=in1, op=op), reads=R, writes=W)


def TS(k, e, out, in0, s1, op0, R, W, s2=None, op1=None):
    if op1 is None:
        return k.op(e, lambda h: h.tensor_scalar(out=out, in0=in0, scalar1=s1, scalar2=None, op0=op0), reads=R, writes=W)
    return k.op(e, lambda h: h.tensor_scalar(out=out, in0=in0, scalar1=s1, scalar2=s2, op0=op0, op1=op1), reads=R, writes=W)


def STT(k, out, in0, scalar, in1, op0, op1, R, W):
    return k.op(k.dve, lambda h: h.scalar_tensor_tensor(out=out, in0=in0, scalar=scalar, in1=in1, op0=op0, op1=op1),
                reads=R, writes=W)


def ACT(k, out, in_, func, R, W, scale=1.0, bias=None):
    if bias is None:
        return k.op(k.act, lambda h: h.activation(out=out, in_=in_, func=func, scale=scale), reads=R, writes=W)
    return k.op(k.act, lambda h: h.activation(out=out, in_=in_, func=func, scale=scale, bias=bias), reads=R, writes=W)


def CP(k, e, out, in_, R, W):
    return k.op(e, lambda h: h.tensor_copy(out=out, in_=in_), reads=R, writes=W)


def MM(k, out, lhsT, rhs, start, stop, R, W, sgc=False, inc=True):
    return k.op(k.pe, lambda h: h.matmul(out, lhsT=lhsT, rhs=rhs, start=start, stop=stop, skip_group_check=sgc),
                reads=R, writes=W, inc=inc)


def _delay(n):
    for _ in range(n):
        yield


def TR(k, out, in_, ident, R, W):
    return k.op(k.pe, lambda h: h.transpose(out=out, in_=in_, identity=ident), reads=R, writes=W)


def MSET(k, e, ap, val, W):
    return k.op(e, lambda h: h.memset(ap, val), writes=W)


def ASEL(k, out, in_, pattern, op, base, cm, R, W):
    return k.op(k.pool, lambda h: h.affine_select(out=out, in_=in_, pattern=pattern, compare_op=op, fill=0.0,
                                                  base=base, channel_multiplier=cm), reads=R, writes=W)


OFF_GQ, OFF_GK = 0, 128
OFF_DQ, OFF_DQS, OFF_DK, OFF_DKS = 256, 512, 768, 1024
OFF_SU, OFF_RX, OFF_RG, OFF_GLR = 1280, 1536, 1792, 2048
OFF_TMG, OFF_DV = 2112, 2624
NCOL = 2880
TWO_PI = 2.0 * math.pi
CW1 = 6.28125
CW2 = TWO_PI - 6.28125


def make_consts(k, env):
    c = {}
    b = Buf("consts")
    c["b"] = b
    one = k.sb([128, 1], F32, "c_one"); MSET(k, k.pool, one[:], 1.0, [b]); c["one"] = one
    sgn = k.sb([128, 1], F32, "c_sgn"); MSET(k, k.pool, sgn[0:64, :], 1.0, [b]); MSET(k, k.pool, sgn[64:128, :], -1.0, [b])
    c["sgn"] = sgn
    tri = k.sb([128, 128], BF16, "c_tri"); MSET(k, k.pool, tri[:], 1.0, [b])
    ASEL(k, tri[:], tri[:], [[1, 128]], ALU.is_ge, 0, -1, [b], [b]); c["tri"] = tri
    tri64 = k.sb([64, 64], F32, "c_tri64"); MSET(k, k.pool, tri64[:], 1.0, [b])
    ASEL(k, tri64[:], tri64[:], [[1, 64]], ALU.is_ge, 0, -1, [b], [b]); c["tri64"] = tri64
    hm = k.sb([128, 4], F32, "c_hm"); MSET(k, k.pool, hm[:], 1.0, [b])
    ASEL(k, hm[:], hm[:], [[-32, 4]], ALU.is_ge, 0, 1, [b], [b])
    ASEL(k, hm[:], hm[:], [[32, 4]], ALU.is_ge, 31, -1, [b], [b]); c["hm"] = hm
    rm = k.sb([128, 8], F32, "c_rm16"); MSET(k, k.pool, rm[:], 1.0, [b])
    ASEL(k, rm[:], rm[:], [[-16, 8]], ALU.is_ge, 0, 1, [b], [b])
    ASEL(k, rm[:], rm[:], [[16, 8]], ALU.is_ge, 15, -1, [b], [b]); c["rm16"] = rm
    jf = k.sb([128, 128], F32, "c_jf"); j2 = k.sb([128, 128], F32, "c_j2")
    MSET(k, k.pool, jf[:], 1.0, [b]); MSET(k, k.pool, j2[:], 1.0, [b])
    ASEL(k, jf[:], jf[:], [[1, 128]], ALU.is_equal, -64, -1, [b], [b])
    ASEL(k, j2[:], j2[:], [[1, 128]], ALU.is_equal, 64, -1, [b], [b])
    TT(k, k.pool, jf[:], jf[:], j2[:], ALU.add, [b], [b]); c["jf"] = jf
    rs = k.sb([128, 256], F32, "c_reset"); MSET(k, k.pool, rs[:], 1.0, [b])
    for i in range(4):
        MSET(k, k.pool, rs[:, 64 * i:64 * i + 1], 0.0, [b])
    c["reset"] = rs
    return c


def mixer_prep(k, env, l, C):
    nc = k.nc
    P = {}
    cb = C["b"]
    lam_init = 0.8 - 0.6 * math.exp(-0.3 * l)
    pb = Buf("prep")
    P["b"] = pb
    banks, bank_b = env["banks"], env["bank_b"]
    pT1, pT1_b = env["pT1"], env["pT1_b"]

    def sbt(shape, dt, nm):
        return k.sb(shape, dt, "p%d_%s" % (l, nm))

    wgate = sbt([16, 128], BF16, "wgate"); negb = sbt([128, 1], F32, "negb"); gnorm = sbt([128, 64], F32, "gnorm")
    lw = sbt([128, 8], F32, "lw"); dnorm = sbt([128, 64], F32, "dnorm")
    cw = sbt([128, 2, 4], F32, "cw"); lv = sbt([128, 5, 2], F32, "lv"); bglu = sbt([128, 2], F32, "bglu")
    wa = sbt([128, 2, 128], BF16, "wa"); wx = sbt([128, 2, 128], BF16, "wx"); lsp = sbt([128, 4, 2], F32, "lsp")
    wglu = sbt([128, 2, 256], BF16, "wglu"); rho = sbt([128, 16], F32, "rho")
    bpad = sbt([128, 2, 16, 128], BF16, "bpad"); cpad = sbt([128, 2, 16, 128], BF16, "cpad")
    wl = sbt([128, 9, 2, 16], F32, "wl")
    tabc = sbt([128, 16, 64], F32, "tabc"); tabs = sbt([128, 16, 64], F32, "tabs")
    rhot = sbt([128, 16, 64], F32, "rhot")
    k.push()

    def ld(t_ap, src, q=None):
        k.dma(q or k.sp, t_ap, src, writes=[pb])

    ld(wgate[:], env["gla_w_gate"][l], k.pool)
    ld(negb[:], env["gla_b_gate_col"][l])
    TS(k, k.dve, negb[:], negb[:], -1.0, ALU.mult, [pb], [pb])
    ld(gnorm[:], env["gla_norm"][l:l + 1, :].partition_broadcast(128))
    P.update(wgate=wgate, negb=negb, gnorm=gnorm)
    lq = sbt([128, 4, 32], F32, "lq")
    for i, nm in enumerate(("diff_lq1", "diff_lk1", "diff_lq2", "diff_lk2")):
        ld(lq[:, i, :], env[nm][l:l + 1, :].partition_broadcast(128))
    pr = sbt([128, 2, 32], F32, "lpr")
    TT(k, k.dve, pr[:, 0, :], lq[:, 0, :], lq[:, 1, :], ALU.mult, [pb], [pb])
    TT(k, k.dve, pr[:, 1, :], lq[:, 2, :], lq[:, 3, :], ALU.mult, [pb], [pb])
    k.op(k.dve, lambda h: h.reduce_sum(out=lw[:, 0:2], in_=pr[:], axis=AX.X), reads=[pb], writes=[pb])
    ACT(k, lw[:, 2:4], lw[:, 0:2], AF.Exp, [pb], [pb])
    TT(k, k.dve, lw[:, 4:5], lw[:, 2:3], lw[:, 3:4], ALU.subtract, [pb], [pb])
    TS(k, k.dve, lw[:, 5:6], lw[:, 4:5], lam_init, ALU.add, [pb], [pb], s2=-1.0, op1=ALU.mult)
    ld(dnorm[:], env["diff_norm"][l:l + 1, :].partition_broadcast(128))
    TS(k, k.dve, dnorm[:], dnorm[:], 1.0 - lam_init, ALU.mult, [pb], [pb])
    P.update(nlam=lw[:, 5:6], dnorm=dnorm)
    ld(cw[:], env["lru_conv_w_col"][l])
    for i, nm in enumerate(("lru_conv_b_col", "lru_b_a_col", "lru_b_x_col", "lru_lambda_col", "s5_d_col")):
        ld(lv[:, i, :], env[nm][l])
    ld(bglu[:], env["s5_b_glu_col"][l])
    MSET(k, k.pool, wa[:], 0.0, [pb]); MSET(k, k.pool, wx[:], 0.0, [pb])
    for n in range(4):
        r0 = (n % 2) * 64
        ld(wa[r0:r0 + 64, n // 2, r0:r0 + 64], env["lru_w_a"][l, n], k.pool)
        ld(wx[r0:r0 + 64, n // 2, r0:r0 + 64], env["lru_w_x"][l, n], k.pool)
    ACT(k, lsp[:, 0, :], lv[:, 3, :], AF.Exp, [pb], [pb], scale=-1.0)
    ACT(k, lsp[:, 1, :], lsp[:, 0, :], AF.Ln, [pb, cb], [pb], bias=C["one"][:])
    TS(k, k.dve, lsp[:, 2, :], lsp[:, 1, :], -8.0, ALU.mult, [pb], [pb])
    TS(k, k.dve, lsp[:, 3, :], lsp[:, 1, :], -16.0, ALU.mult, [pb], [pb])
    ld(wglu[:], env["s5_w_glu"][l].rearrange("(k p) n -> p k n", p=128), k.pool)
    P.update(cw=cw, lv=lv, bglu=bglu, wa=wa, wx=wx, lsp=lsp, wglu=wglu)
    lr = sbt([128, 16], F32, "lr"); li = sbt([128, 16], F32, "li"); ls = sbt([128, 16], F32, "ls")
    for hlf in range(2):
        ld(lr[hlf * 64:(hlf + 1) * 64, :], env["s5_a_re_T"][l])
        ld(li[hlf * 64:(hlf + 1) * 64, :], env["s5_a_im_T"][l])
    ld(ls[:], env["s5_log_step"][l:l + 1, :].partition_broadcast(128))
    bri = sbt([128, 2, 256], F32, "bri"); cri = sbt([128, 2, 256], F32, "cri")
    for hlf in range(2):
        sl_ = slice(hlf * 64, (hlf + 1) * 64)
        ld(bri[sl_, 0, :], env["s5_b_re_p"][l]); ld(bri[sl_, 1, :], env["s5_b_im_p"][l])
        ld(cri[sl_, 0, :], env["s5_c_re_p"][l]); ld(cri[sl_, 1, :], env["s5_c_im_p"][l])
    w = sbt([128, 24, 16], F32, "s5w")
    W_ = lambda i: w[:, i, :]
    R, Wr = [pb], [pb]
    ACT(k, W_(0), ls[:], AF.Exp, R, Wr)
    TT(k, k.dve, W_(1), lr[:], W_(0), ALU.mult, R, Wr)
    TT(k, k.dve, W_(2), li[:], W_(0), ALU.mult, R, Wr)
    ACT(k, rho[:], W_(1), AF.Exp, R, Wr)
    ki = sbt([128, 16], I32, "ki")

    def sin_reduced(dst, ang):
        TS(k, k.dve, W_(3), ang, 1.0 / TWO_PI, ALU.mult, R, Wr)
        CP(k, k.dve, ki[:], W_(3), R, Wr)
        CP(k, k.dve, W_(4), ki[:], R, Wr)
        STT(k, W_(5), W_(4), -CW1, ang, ALU.mult, ALU.add, R, Wr)
        STT(k, W_(5), W_(4), -CW2, W_(5), ALU.mult, ALU.add, R, Wr)
        TS(k, k.dve, W_(5), W_(5), 3.1415925, ALU.min, R, Wr, s2=-3.1415925, op1=ALU.max)
        ACT(k, dst, W_(5), AF.Sin, R, Wr)

    sin_reduced(W_(6), W_(2))
    TS(k, k.dve, W_(7), W_(2), math.pi / 2.0, ALU.add, R, Wr)
    sin_reduced(W_(8), W_(7))
    TT(k, k.dve, W_(9), W_(6), W_(6), ALU.mult, R, Wr)
    TT(k, k.dve, W_(10), W_(8), W_(8), ALU.mult, R, Wr)
    TT(k, k.dve, W_(9), W_(9), W_(10), ALU.add, R, Wr)
    TS(k, k.dve, W_(9), W_(9), -0.5, ALU.mult, R, Wr, s2=1.5, op1=ALU.add)
    TT(k, k.dve, W_(6), W_(6), W_(9), ALU.mult, R, Wr)
    TT(k, k.dve, W_(8), W_(8), W_(9), ALU.mult, R, Wr)
    TT(k, k.dve, W_(11), rho[:], W_(8), ALU.mult, R, Wr)
    TT(k, k.dve, W_(12), rho[:], W_(6), ALU.mult, R, Wr)
    TT(k, k.dve, W_(13), lr[:], lr[:], ALU.mult, R, Wr)
    TT(k, k.dve, W_(14), li[:], li[:], ALU.mult, R, Wr)
    TT(k, k.dve, W_(13), W_(13), W_(14), ALU.add, R, Wr)
    k.op(k.dve, lambda h: h.reciprocal(out=W_(13), in_=W_(13)), reads=R, writes=Wr)
    TS(k, k.dve, W_(11), W_(11), -1.0, ALU.add, R, Wr)
    TT(k, k.dve, W_(14), W_(11), lr[:], ALU.mult, R, Wr)
    TT(k, k.dve, W_(15), W_(12), li[:], ALU.mult, R, Wr)
    TT(k, k.dve, W_(14), W_(14), W_(15), ALU.add, R, Wr)
    TT(k, k.dve, W_(14), W_(14), W_(13), ALU.mult, R, Wr)
    TT(k, k.dve, W_(15), W_(12), lr[:], ALU.mult, R, Wr)
    TT(k, k.dve, W_(16), W_(11), li[:], ALU.mult, R, Wr)
    TT(k, k.dve, W_(15), W_(15), W_(16), ALU.subtract, R, Wr)
    TT(k, k.dve, W_(15), W_(15), W_(13), ALU.mult, R, Wr)
    crb = W_(14).unsqueeze(2).to_broadcast([128, 16, 16])
    cib = W_(15).unsqueeze(2).to_broadcast([128, 16, 16])
    bre = bri[:, 0, :].rearrange("p (g h) -> p g h", g=16)
    bim = bri[:, 1, :].rearrange("p (g h) -> p g h", g=16)
    bb = sbt([128, 4, 256], F32, "bb")
    B3 = lambda i: bb[:, i, :].rearrange("p (g h) -> p g h", g=16)
    TT(k, k.dve, B3(0), bre, crb, ALU.mult, R, Wr)
    TT(k, k.dve, B3(1), bim, cib, ALU.mult, R, Wr)
    TT(k, k.dve, B3(0), B3(0), B3(1), ALU.subtract, R, Wr)
    TT(k, k.dve, B3(2), bim, crb, ALU.mult, R, Wr)
    TT(k, k.dve, B3(3), bre, cib, ALU.mult, R, Wr)
    TT(k, k.dve, B3(2), B3(2), B3(3), ALU.add, R, Wr)
    bst = sbt([128, 2, 256], BF16, "bst")
    CP(k, k.dve, bst[0:64, 0, :], bb[0:64, 0, :], R, Wr)
    CP(k, k.dve, bst[64:128, 0, :], bb[64:128, 2, :], R, Wr)
    CP(k, k.dve, bst[0:64, 1, :], bb[0:64, 2, :], R, Wr)
    CP(k, k.dve, bst[64:128, 1, :], bb[64:128, 0, :], R, Wr)
    for v in range(2):
        for ch in range(2):
            TR(k, pT1[:, 0:128], bst[:, v, ch * 128:(ch + 1) * 128], env["ident"][:], [pb, env["ident_b"]], [pT1_b])
            for gg in range(8):
                TS(k, k.dve, bpad[:, v, ch * 8 + gg, :], pT1[:, 0:128], C["rm16"][:, gg:gg + 1], ALU.mult,
                   [pT1_b, cb], [pb])
    cst = sbt([128, 2, 256], F32, "cst")
    CP(k, k.dve, cst[0:64, 0, :], cri[0:64, 0, :], R, Wr)
    TS(k, k.dve, cst[64:128, 0, :], cri[64:128, 1, :], -1.0, ALU.mult, R, Wr)
    TS(k, k.dve, cst[0:64, 1, :], cri[0:64, 1, :], -1.0, ALU.mult, R, Wr)
    CP(k, k.dve, cst[64:128, 1, :], cri[64:128, 0, :], R, Wr)
    MSET(k, k.pool, cpad[:], 0.0, Wr)
    for v in range(2):
        for g in range(16):
            CP(k, k.pool, cpad[:, v, g, (g % 8) * 16:(g % 8) * 16 + 16], cst[:, v, g * 16:(g + 1) * 16], R, Wr)
    CP(k, k.dve, wl[:, 0, 0, :], W_(8), R, Wr)
    TS(k, k.dve, wl[:, 0, 1, :], W_(6), C["sgn"][:, 0:1], ALU.mult, R + [cb], Wr)
    MSET(k, k.pool, tabc[:, :, 0:1], 1.0, Wr); MSET(k, k.pool, tabs[:, :, 0:1], 0.0, Wr)
    t1 = sbt([128, 16, 32], F32, "dbl1"); t2 = sbt([128, 16, 32], F32, "dbl2")
    n = 1
    for lev in range(6):
        wc = wl[:, lev, 0, :].unsqueeze(2).to_broadcast([128, 16, n])
        ws = wl[:, lev, 1, :].unsqueeze(2).to_broadcast([128, 16, n])
        TT(k, k.pool, t1[:, :, 0:n], tabs[:, :, 0:n], ws, ALU.mult, R, Wr)
        TT(k, k.pool, t2[:, :, 0:n], tabc[:, :, 0:n], wc, ALU.mult, R, Wr)
        TT(k, k.pool, tabc[:, :, n:2 * n], t2[:, :, 0:n], t1[:, :, 0:n], ALU.subtract, R, Wr)
        TT(k, k.pool, t1[:, :, 0:n], tabc[:, :, 0:n], ws, ALU.mult, R, Wr)
        TT(k, k.pool, t2[:, :, 0:n], tabs[:, :, 0:n], wc, ALU.mult, R, Wr)
        TT(k, k.pool, tabs[:, :, n:2 * n], t1[:, :, 0:n], t2[:, :, 0:n], ALU.add, R, Wr)
        TT(k, k.dve, W_(17), wl[:, lev, 0, :], wl[:, lev, 0, :], ALU.mult, R, Wr)
        TT(k, k.dve, W_(18), wl[:, lev, 1, :], wl[:, lev, 1, :], ALU.mult, R, Wr)
        TT(k, k.dve, wl[:, lev + 1, 0, :], W_(17), W_(18), ALU.subtract, R, Wr)
        STT(k, wl[:, lev + 1, 1, :], wl[:, lev, 0, :], 2.0, wl[:, lev, 1, :], ALU.mult, ALU.mult, R, Wr)
        n *= 2
    P.update(rho=rho, bpad=bpad, cpad=cpad, tabc=tabc, tabs=tabs, c64=wl[:, 6, 0, :], s64=wl[:, 6, 1, :], rhot=rhot)
    CP(k, k.pool, rhot[:], rho[:].unsqueeze(2).to_broadcast([128, 16, 64]), R, Wr)
    MSET(k, k.pool, rhot[:, :, 0:1], 0.0, Wr)
    k.pop()
    return P


def mixer_pass(k, env, l, x_src, x_src_b, x_dst, x_dst_b, C, dbg=None):
    nc = k.nc
    NS, TT_, NMT = 2, 256, 16
    banks, bank_b = env["banks"], env["bank_b"]
    pT, pT_b, pT1, pT1_b = env["pT"], env["pT_b"], env["pT1"], env["pT1_b"]
    ident, ident_b = env["ident"], env["ident_b"]
    cb = C["b"]
    hm, tri, tri64 = C["hm"], C["tri"], C["tri64"]
    k.push()
    sbase = nc.sbuf_bytes_remaining
    nm = lambda s_: "m%d_%s" % (l, s_)

    win = k.sb([128, 8, NCOL], BF16, nm("win")); win_b = [Buf("win%d" % i) for i in range(8)]
    wind = env["w_in_ext"][l].rearrange("(k p) n -> p k n", p=128)
    for i in range(8):
        k.dma(k.pool, win[:, i, :], wind[:, i, :], writes=[win_b[i]])
    wout = k.sb([128, 8, D], BF16, nm("wout")); wout_b = Buf("wout")
    k.dma(k.pool, wout[:], env["w_out"][l].rearrange("(k p) n -> p k n", p=128), writes=[wout_b])
    gcol = k.sb([128, 8], F32, nm("gcol")); gcol_b = Buf("gcol")
    k.dma(k.sp, gcol[:], env["ln_mix_pre_col"][l], writes=[gcol_b])
    gpost = k.sb([128, D], F32, nm("gpost")); gpost_b = Buf("gpost")
    k.dma(k.sp, gpost[:], env["ln_mix_post"][l:l + 1, :].partition_broadcast(128), writes=[gpost_b])
    kc = k.sb([128, 2, S], BF16, nm("kc")); kc_b = [Buf("kc%d" % i) for i in range(NMT)]
    vc = k.sb([128, 32, 4, 65], BF16, nm("vc")); vc_b = [Buf("vc%d" % i) for i in range(NMT)]
    vc1_b = Buf("vc_ones")
    MSET(k, k.pool, vc[:, :, :, 64:65], 1.0, [vc1_b])
    P = mixer_prep(k, env, l, C) if "prep" not in env.get("skip", ()) else {"b": Buf("prep")}
    pb = P["b"]

    def T(shape, dt, s_):
        return k.sb(shape, dt, nm(s_)), Buf(s_)

    xin2 = [T([128, NS, D], F32, "xin%d" % i) for i in range(2)]
    hT, hT_b = T([128, 8, TT_], BF16, "hT")
    mixT, mixT_b = T([128, 8, TT_], BF16, "mixT")
    nt = NormT(k, NS, ident, ident_b, pT, pT_b, nm("nt"))
    rc, rc_b = T([128, TT_], F32, "rc"); rs, rs_b = T([128, TT_], F32, "rs")
    tmpo, tmpo_b = T([128, D], F32, "tmpo")
    st2, st2_b = T([128, 8], F32, "st2")
    f1, f1_b = T([128, TT_], F32, "f1"); f2, f2_b = T([128, TT_], F32, "f2"); f3, f3_b = T([128, TT_], F32, "f3")
    f4, f4_b = T([128, TT_], F32, "f4"); f5, f5_b = T([128, TT_], F32, "f5")
    gq, gq_b = T([128, TT_], F32, "gq"); gk, gk_b = T([128, TT_], F32, "gk")
    glr, glr_b = T([16, TT_], BF16, "glr")
    qm, qm_b = T([128, 2, 4, TT_], BF16, "qm")
    uf, uf_b = T([128, 2, TT_], F32, "uf"); ub, ub_b = T([128, 2, TT_], BF16, "ub")
    rxh, rxh_b = T([128, 2, 3 + TT_], F32, "rxh"); rg, rg_b = T([128, 2, TT_], F32, "rg")
    gvc, gvc_b = T([64, 4, 256], BF16, "gvc"); ogc, ogc_b = T([64, 4, 256], F32, "ogc")
    qdm, qdm_b = T([128, 4, TT_], BF16, "qdm"); kd, kd_b = T([128, TT_], BF16, "kd"); ks, ks_b = T([128, TT_], BF16, "ks")
    dec, dec_b = T([128, 4], F32, "dec")
    Sst, Sst_b = T([128, 64], F32, "Sst"); Sbf, Sbf_b = T([128, 2, 64], BF16, "Sbf")
    kst, kst_b = T([64, 128], BF16, "kst"); AT, AT_b = T([64, 4, 64], BF16, "AT")
    kvr, kvr_b = T([128, 64], F32, "kvr")
    osb, osb_b = T([64, 256], F32, "osb"); sil, sil_b = T([64, 256], F32, "sil")
    gst, gst_b = T([64, 12], F32, "gst"); otk, otk_b = T([64, 256], BF16, "otk")
    pt = [T([128, 2 * TT_], BF16, "pt%d" % i) for i in range(2)]
    od, od_b = T([128, 2, 256], F32, "od"); odb, odb_b = T([128, 2, 256], BF16, "odb")
    dst, dst_b = T([128, 32], F32, "dst"); dt1, dt1_b = T([128, 64], F32, "dt1")
    zlast, zlast_b = T([128, 16], F32, "zlast"); zin, zin_b = T([128, 16], F32, "zin"); zt, zt_b = T([128, 16], F32, "zt")
    m1, m1_b = T([128, 256], F32, "m1"); m2, m2_b = T([128, 256], F32, "m2"); zz, zz_b = T([128, 256], F32, "zz")
    zc, zc_b = T([128, 256], BF16, "zc"); zs, zs_b = T([128, 256], BF16, "zs")
    sf1, sf1_b, sf2, sf2_b = m1, m1_b, m2, m2_b
    yb, yb_b = T([128, 2, TT_], BF16, "yb")
    xcb, xcb_b = T([128, TT_], BF16, "xcb"); hst, hst_b = T([128, 2], F32, "hst")
    odq, odq_b = tmpo[:, 0:512].rearrange("p (s f) -> p s f", s=2), tmpo_b
    kvm, kvm_b = tmpo[:, 512:768], tmpo_b
    osq, osq_b = tmpo[0:64, 768:1024], tmpo_b
    MSET(k, k.pool, Sst[:], 0.0, [Sst_b]); MSET(k, k.pool, Sbf[:], 0.0, [Sbf_b])
    MSET(k, k.pool, zlast[:], 0.0, [zlast_b]); MSET(k, k.pool, hst[:], 0.0, [hst_b]); MSET(k, k.pool, rxh[:, :, 0:3], 0.0, [rxh_b])
    env["sbuf_used_mix"] = sbase - nc.sbuf_bytes_remaining

    xs = x_src.rearrange("(m s p) d -> m p s d", s=NS, p=128)
    xd = x_dst.rearrange("(m s p) d -> m p s d", s=NS, p=128)
    B = lambda i: banks[i]
    Bb = lambda i: bank_b[i]
    QS = 32.0 ** -0.5
    sbi = [0]

    def inproj_fm(off, width, bank):
        for kk in range(8):
            MM(k, B(bank)[0:width, 0:TT_], win[:, kk, off:off + width], hT[:, kk, :], kk == 0, kk == 7,
               [win_b[kk], hT_b], [Bb(bank)], inc=(kk == 7))

    NMT_ = env.get("nmt", NMT)

    def load_x(m_):
        k.dma(k.sp, xin2[m_ % 2][0][:], xs[m_], reads=[x_src_b[m_]], writes=[xin2[m_ % 2][1]])

    def load_rope(m_):
        k.dma(k.sp, rc[:], env["rope_cos"][:, m_ * TT_:(m_ + 1) * TT_], writes=[rc_b])
        k.dma(k.sp, rs[:], env["rope_ssin"][:, m_ * TT_:(m_ + 1) * TT_], writes=[rs_b])

    load_x(0)
    load_rope(0)
    nt.run(xin2[0][0], xin2[0][1], gcol, gcol_b, hT, hT_b)
    for mt in range(NMT_):
        t0 = mt * TT_
        xin, xin_b = xin2[mt % 2]

        def _sec_inproj():
            ipon = lambda i: env.get('ip_only') is None or i in env['ip_only']
            if ipon(0): inproj_fm(OFF_GQ, 128, 0); ACT(k, gq[:], B(0)[:, 0:TT_], AF.Copy, [Bb(0)], [gq_b])
            if ipon(1): inproj_fm(OFF_GK, 128, 1); ACT(k, gk[:], B(1)[:, 0:TT_], AF.Copy, [Bb(1)], [gk_b])
            for j in (range(2) if ipon(2) else ()):
                inproj_fm(OFF_DQ + 128 * j, 128, 0); inproj_fm(OFF_DQS + 128 * j, 128, 1)
                TT(k, k.dve, f1[:], B(0)[:, 0:TT_], rc[:], ALU.mult, [Bb(0), rc_b], [f1_b])
                TT(k, k.dve, f2[:], B(1)[:, 0:TT_], rs[:], ALU.mult, [Bb(1), rs_b], [f2_b])
                TT(k, k.pool, f3[:], f1[:], f2[:], ALU.add, [f1_b, f2_b], [f3_b])
                for i in range(4):
                    TS(k, k.pool if i % 2 else k.dve, qm[:, j, i, :], f3[:], hm[:, i:i + 1], ALU.mult, [f3_b, cb], [qm_b])
            for j in (range(2) if ipon(3) else ()):
                inproj_fm(OFF_DK + 128 * j, 128, 0); inproj_fm(OFF_DKS + 128 * j, 128, 1)
                TT(k, k.dve, f1[:], B(0)[:, 0:TT_], rc[:], ALU.mult, [Bb(0), rc_b], [f1_b])
                TT(k, k.dve, f2[:], B(1)[:, 0:TT_], rs[:], ALU.mult, [Bb(1), rs_b], [f2_b])
                TT(k, k.pool, kc[:, j, t0:t0 + TT_], f1[:], f2[:], ALU.add, [f1_b, f2_b], [kc_b[mt]])
            for j in (range(2) if ipon(4) else ()):
                inproj_fm(OFF_SU + 128 * j, 128, j)
                ACT(k, uf[:, j, :], B(j)[:, 0:TT_], AF.Copy, [Bb(j)], [uf_b])
                CP(k, k.pool, ub[:, j, :], uf[:, j, :], [uf_b], [ub_b])
            for j in (range(2) if ipon(5) else ()):
                inproj_fm(OFF_RX + 128 * j, 128, j)
                ACT(k, rxh[:, j, 3:3 + TT_], B(j)[:, 0:TT_], AF.Copy, [Bb(j)], [rxh_b])
            for j in (range(2) if ipon(6) else ()):
                inproj_fm(OFF_RG + 128 * j, 128, j)
                ACT(k, rg[:, j, :], B(j)[:, 0:TT_], AF.Copy, [Bb(j)], [rg_b])
            if ipon(7): inproj_fm(OFF_GLR, 16, 0); ACT(k, glr[:], B(0)[0:16, 0:TT_], AF.Copy, [Bb(0)], [glr_b])
            for n in (range(4) if ipon(8) else ()):
                bk = n % 2
                for kk in range(8):
                    MM(k, B(bk)[0:64, 0:512], hT[:, kk, n * 64:(n + 1) * 64], win[:, kk, OFF_TMG:OFF_TMG + 512],
                       kk == 0, kk == 7, [win_b[kk], hT_b], [Bb(bk)], inc=(kk == 7))
                ACT(k, gvc[:, n, :], B(bk)[0:64, 0:256], AF.Copy, [Bb(bk)], [gvc_b])
                CP(k, k.dve, ogc[:, n, :], B(bk)[0:64, 256:512], [Bb(bk)], [ogc_b])
            for s in (range(NS) if ipon(9) else ()):
                bk = s % 2
                for kk in range(8):
                    MM(k, B(bk)[:, 0:256], hT[:, kk, s * 128:(s + 1) * 128], win[:, kk, OFF_DV:OFF_DV + 256],
                       kk == 0, kk == 7, [win_b[kk], hT_b], [Bb(bk)], inc=(kk == 7))
                CP(k, k.dve, vc[:, 2 * mt + s, :, 0:64], B(bk)[:, 0:256].rearrange("p (h v) -> p h v", h=4),
                   [Bb(bk)], [vc_b[mt]])

        if "inproj" not in env.get("skip", ()):
            _sec_inproj()
        def _sec_gla():
            MM(k, B(0)[:, 0:TT_], P["wgate"][:], glr[:], True, True, [pb, glr_b], [Bb(0)])
            ACT(k, f1[:], B(0)[:, 0:TT_], AF.Exp, [Bb(0), pb], [f1_b], scale=-1.0, bias=P["negb"][:])
            yield
            ACT(k, f2[:], f1[:], AF.Ln, [f1_b, cb], [f2_b], bias=C["one"][:])
            yield
            k.op(k.dve, lambda h: h.tensor_tensor_scan(out=f3[:], data0=C["reset"][:], data1=f2[:], initial=0.0,
                                                       op0=ALU.mult, op1=ALU.add), reads=[f2_b, cb], writes=[f3_b])
            yield
            f3v = f3[:].rearrange("p (n c) -> p n c", n=4)
            ACT(k, f1[:], f3[:], AF.Exp, [f3_b], [f1_b], scale=-1.0 / 16.0)
            yield
            ACT(k, f2[:], f3[:], AF.Exp, [f3_b], [f2_b], scale=1.0 / 16.0)
            yield
            TT(k, k.dve, f4[:].rearrange("p (n c) -> p n c", n=4), f3v, f3v[:, :, 63:64].to_broadcast([128, 4, 64]),
               ALU.subtract, [f3_b], [f4_b])
            yield
            ACT(k, f5[:], f4[:], AF.Exp, [f4_b], [f5_b], scale=1.0 / 16.0)
            yield
            ACT(k, dec[:], f3v[:, :, 63], AF.Exp, [f3_b], [dec_b], scale=-1.0 / 16.0)
            yield
            STT(k, f4[:], gq[:], 32.0 ** -0.5, f1[:], ALU.mult, ALU.mult, [gq_b, f1_b], [f4_b])
            yield
            for h_ in range(4):
                TS(k, k.pool if h_ % 2 else k.dve, qdm[:, h_, :], f4[:], hm[:, h_:h_ + 1], ALU.mult, [f4_b, cb], [qdm_b])
                yield
            TT(k, k.dve, kd[:], gk[:], f2[:], ALU.mult, [gk_b, f2_b], [kd_b])
            yield
            TT(k, k.pool, ks[:], gk[:], f5[:], ALU.mult, [gk_b, f5_b], [ks_b])
            yield
            yield
            for n in range(4):
                c0 = n * 64
                gn = mt * 4 + n
                sp_, sn_ = gn % 2, (gn + 1) % 2
                TR(k, pT1[0:64, 0:128], ks[:, c0:c0 + 64], ident[:], [ks_b, ident_b], [pT1_b])
                CP(k, k.dve, kst[:], pT1[0:64, 0:128], [pT1_b], [kst_b])
                yield
                for h_ in range(4):
                    MM(k, B(0)[0:64, h_ * 64:(h_ + 1) * 64], kd[:, c0:c0 + 64], qdm[:, h_, c0:c0 + 64], True, True,
                       [kd_b, qdm_b], [Bb(0)])
                TT(k, k.dve, AT[:], B(0)[0:64, 0:256].rearrange("p (h c) -> p h c", h=4),
                   tri64[:].unsqueeze(1).to_broadcast([64, 4, 64]), ALU.mult, [Bb(0), cb], [AT_b])
                yield
                yield
                MM(k, B(1)[:, 0:256], kst[:], gvc[:, n, :], True, True, [kst_b, gvc_b], [Bb(1)])
                yield from _delay(env.get('delay', 0))
                for h_ in range(4):
                    MM(k, B(0)[0:64, 256 + h_ * 64:256 + (h_ + 1) * 64], AT[:, h_, :], gvc[:, n, h_ * 64:(h_ + 1) * 64], True, False,
                       [AT_b, gvc_b], [Bb(0)])
                    MM(k, B(0)[0:64, 256 + h_ * 64:256 + (h_ + 1) * 64], qdm[:, h_, c0:c0 + 64], Sbf[:, sp_, :], False, True,
                       [qdm_b, Sbf_b], [Bb(0)])
                yield
                TT(k, k.dve, kvm.rearrange("p (h v) -> p h v", h=4), B(1)[:, 0:256].rearrange("p (h v) -> p h v", h=4),
                   hm[:].unsqueeze(2).to_broadcast([128, 4, 64]), ALU.mult, [Bb(1), cb], [kvm_b])
                yield
                k.op(k.dve, lambda h: h.reduce_sum(out=kvr[:], in_=kvm.rearrange("p (h v) -> p v h", h=4), axis=AX.X),
                     reads=[kvm_b], writes=[kvr_b])
                yield
                STT(k, Sst[:], Sst[:], dec[:, n:n + 1], kvr[:], ALU.mult, ALU.add, [Sst_b, dec_b, kvr_b], [Sst_b])
                yield
                CP(k, k.pool, Sbf[:, sn_, :], Sst[:], [Sst_b], [Sbf_b])
                yield
                yield
                ACT(k, osb[:], B(0)[0:64, 256:512], AF.Copy, [Bb(0)], [osb_b])
                yield
                TT(k, k.pool, osq, osb[:], osb[:], ALU.mult, [osb_b], [osq_b])
                yield
                k.op(k.dve, lambda h: h.reduce_sum(out=gst[:, 0:4], in_=osq.rearrange("p (h v) -> p h v", h=4), axis=AX.X),
                     reads=[osq_b], writes=[gst_b])
                yield
                ACT(k, gst[:, 4:8], gst[:, 0:4], AF.Ln, [gst_b], [gst_b], scale=1.0 / 64.0, bias=k.eps_ap[0:64, :])
                yield
                ACT(k, gst[:, 8:12], gst[:, 4:8], AF.Exp, [gst_b], [gst_b], scale=-0.5)
                yield
                ACT(k, sil[:], ogc[:, n, :], AF.Silu, [ogc_b], [sil_b])
                yield
                TT(k, k.dve, osb[:].rearrange("p (h v) -> p h v", h=4), osb[:].rearrange("p (h v) -> p h v", h=4),
                   gst[:, 8:12].unsqueeze(2).to_broadcast([64, 4, 64]), ALU.mult, [osb_b, gst_b], [osb_b])
                yield
                TT(k, k.pool, sil[:].rearrange("p (h v) -> p h v", h=4), sil[:].rearrange("p (h v) -> p h v", h=4),
                   P["gnorm"][0:64, :].unsqueeze(1).to_broadcast([64, 4, 64]), ALU.mult, [sil_b, pb], [sil_b])
                yield
                yield
                TT(k, k.dve, otk[:], osb[:], sil[:], ALU.mult, [osb_b, sil_b], [otk_b])
                yield
                yield from _delay(env.get('delay', 0))
                for j in range(2):
                    TR(k, pT1[:, 128 + j * 64:128 + (j + 1) * 64], otk[:, j * 128:(j + 1) * 128], ident[0:64, 0:64],
                       [otk_b, ident_b], [pT1_b])
                yield
                CP(k, k.dve, mixT[:, 0:2, c0:c0 + 64], pT1[:, 128:256].rearrange("p (j c) -> p j c", j=2), [pT1_b], [mixT_b])
                yield

        def _sec_diff():
            for h_ in range(4):
                j = h_ // 2
                for c in range(2):
                    qsl = qm[:, j, (h_ % 2) * 2 + c, :]
                    def emit_s(i):
                        sb_ = sbi[0] % 2
                        sbi[0] += 1
                        bk, ptt, ptb = B(2 + sb_), pt[sb_][0], pt[sb_][1]
                        diag = (i == mt)
                        for u in range(2):
                            kt = 2 * i + u
                            qlo = 128 if (diag and u == 1) else 0
                            MM(k, bk[:, u * TT_ + qlo:(u + 1) * TT_], kc[:, j, kt * 128:(kt + 1) * 128], qsl[:, qlo:TT_],
                               True, True, [kc_b[i], qm_b], [Bb(2 + sb_)])
                        if not diag:
                            ACT(k, ptt[:, 0:2 * TT_], bk[:, 0:2 * TT_], AF.Exp, [Bb(2 + sb_)], [ptb], scale=QS)
                        else:
                            ACT(k, ptt[:, 0:TT_], bk[:, 0:TT_], AF.Exp, [Bb(2 + sb_)], [ptb], scale=QS)
                            ACT(k, ptt[:, TT_ + 128:2 * TT_], bk[:, TT_ + 128:2 * TT_], AF.Exp, [Bb(2 + sb_)], [ptb], scale=QS)
                            TT(k, k.pool, ptt[:, 0:128], ptt[:, 0:128], tri[:], ALU.mult, [ptb, cb], [ptb])
                            TT(k, k.pool, ptt[:, TT_ + 128:2 * TT_], ptt[:, TT_ + 128:2 * TT_], tri[:], ALU.mult, [ptb, cb], [ptb])
                        return sb_

                    def emit_pv(i, sb_):
                        ptt, ptb = pt[sb_][0], pt[sb_][1]
                        for u in range(2):
                            kt = 2 * i + u
                            for s in range(NS):
                                if kt > 2 * mt + s:
                                    continue
                                col = (c * 2 + s) * 65
                                MM(k, B(4)[:, col:col + 65], ptt[:, u * TT_ + s * 128:u * TT_ + (s + 1) * 128],
                                   vc[:, kt, h_, :], kt == 0 and c == 0 and s == 0, kt == 2 * mt + s,
                                   [ptb, vc_b[i], vc1_b], [Bb(4)], sgc=True)

                    prev = None
                    for i in range(mt + 1):
                        sb_ = emit_s(i)
                        if prev is not None:
                            emit_pv(*prev)
                        prev = (i, sb_)
                        yield
                    emit_pv(*prev)
                    yield
                for s in range(NS):
                    c1, c2 = (0 * 2 + s) * 65, (1 * 2 + s) * 65
                    k.op(k.dve, lambda h: h.reciprocal(out=dst[:, 0:1], in_=B(4)[:, c1 + 64:c1 + 65]), reads=[Bb(4)], writes=[dst_b])
                    yield
                    k.op(k.dve, lambda h: h.reciprocal(out=dst[:, 1:2], in_=B(4)[:, c2 + 64:c2 + 65]), reads=[Bb(4)], writes=[dst_b])
                    yield
                    TT(k, k.dve, dst[:, 2:3], dst[:, 1:2], P["nlam"], ALU.mult, [dst_b, pb], [dst_b])
                    yield
                    TS(k, k.dve, dt1[:], B(4)[:, c1:c1 + 64], dst[:, 0:1], ALU.mult, [Bb(4), dst_b], [dt1_b])
                    yield
                    STT(k, od[:, s, h_ * 64:(h_ + 1) * 64], B(4)[:, c2:c2 + 64], dst[:, 2:3], dt1[:], ALU.mult, ALU.add,
                        [Bb(4), dst_b, dt1_b], [od_b])
                    yield
            yield
            TT(k, k.pool, odq, od[:], od[:], ALU.mult, [od_b], [odq_b])
            yield
            k.op(k.dve, lambda h: h.reduce_sum(out=dst[:, 8:16], in_=odq.rearrange("p s (h v) -> p (s h) v", h=4), axis=AX.X),
                 reads=[odq_b], writes=[dst_b])
            yield
            ACT(k, dst[:, 16:24], dst[:, 8:16], AF.Ln, [dst_b], [dst_b], scale=1.0 / 64.0, bias=k.eps_ap)
            yield
            ACT(k, dst[:, 24:32], dst[:, 16:24], AF.Exp, [dst_b], [dst_b], scale=-0.5)
            yield
            TT(k, k.dve, od[:].rearrange("p s (h v) -> p (s h) v", h=4), od[:].rearrange("p s (h v) -> p (s h) v", h=4),
               dst[:, 24:32].unsqueeze(2).to_broadcast([128, 8, 64]), ALU.mult, [od_b, dst_b], [od_b])
            yield
            TT(k, k.dve, odb[:].rearrange("p s (h v) -> p (s h) v", h=4), od[:].rearrange("p s (h v) -> p (s h) v", h=4),
               P["dnorm"][:].unsqueeze(1).to_broadcast([128, 8, 64]), ALU.mult, [od_b, pb], [odb_b])
            yield
            for s in range(NS):
                for j in range(2):
                    TR(k, pT1[:, 256 + (s * 2 + j) * 128:256 + (s * 2 + j + 1) * 128], odb[:, s, j * 128:(j + 1) * 128], ident[:],
                       [odb_b, ident_b], [pT1_b])
            for s in range(NS):
                CP(k, k.dve, mixT[:, 2:4, s * 128:(s + 1) * 128],
                   pT1[:, 256 + s * 256:256 + (s + 1) * 256].rearrange("p (j c) -> p j c", j=2), [pT1_b], [mixT_b])
                yield

        def _sec_s5():
            for qt in range(4):
                q0 = qt * 64
                MM(k, B(1)[:, 256:272], C["jf"][:], zlast[:], True, True, [cb, zlast_b], [Bb(1)])
                TT(k, k.dve, zt[:], B(1)[:, 256:272], P["s64"], ALU.mult, [Bb(1), pb], [zt_b])
                yield
                TT(k, k.dve, zin[:], zlast[:], P["c64"], ALU.mult, [zlast_b, pb], [zin_b])
                yield
                TT(k, k.dve, zin[:], zin[:], zt[:], ALU.subtract, [zin_b, zt_b], [zin_b])
                yield
                TT(k, k.dve, zin[:], zin[:], P["rho"][:], ALU.mult, [zin_b, pb], [zin_b])
                yield
                for ch in range(2):
                    for hb in range(2):
                        g0 = 8 * ch + 4 * hb
                        for gg in range(4):
                            MM(k, B(5)[:, gg * 64:(gg + 1) * 64], P["bpad"][:, 0, g0 + gg, :], ub[:, ch, q0:q0 + 64],
                               True, True, [pb, ub_b], [Bb(5)])
                            MM(k, B(5)[:, 256 + gg * 64:256 + (gg + 1) * 64], P["bpad"][:, 1, g0 + gg, :], ub[:, ch, q0:q0 + 64],
                               True, True, [pb, ub_b], [Bb(5)])
                        tc4 = P["tabc"][:, g0:g0 + 4, :].rearrange("p g j -> p (g j)")
                        ts4 = P["tabs"][:, g0:g0 + 4, :].rearrange("p g j -> p (g j)")
                        TT(k, k.dve, m1[:], B(5)[:, 0:256], tc4, ALU.mult, [Bb(5), pb], [m1_b])
                        yield
                        TT(k, k.dve, m2[:], B(5)[:, 256:512], ts4, ALU.mult, [Bb(5), pb], [m2_b])
                        yield
                        yield
                        TT(k, k.pool, m1[:], m1[:], m2[:], ALU.add, [m1_b, m2_b], [m1_b])
                        yield
                        m1v = m1[:].rearrange("p (g j) -> p g j", g=4)
                        TT(k, k.dve, m1v[:, :, 0], m1v[:, :, 0], zin[:, g0:g0 + 4], ALU.add, [m1_b, zin_b], [m1_b])
                        yield
                        k.op(k.dve, lambda h: h.tensor_tensor_scan(
                            out=zz[:], data0=P["rhot"][:, g0:g0 + 4, :].rearrange("p g j -> p (g j)"), data1=m1[:],
                            initial=0.0, op0=ALU.mult, op1=ALU.add), reads=[pb, m1_b], writes=[zz_b])
                        yield
                        yield
                        CP(k, k.pool, zlast[:, g0:g0 + 4], zz[:].rearrange("p (g j) -> p g j", g=4)[:, :, 63], [zz_b], [zlast_b])
                        yield
                        TT(k, k.pool, zc[:], zz[:], tc4, ALU.mult, [zz_b, pb], [zc_b])
                        yield
                        TT(k, k.dve, zs[:], zz[:], ts4, ALU.mult, [zz_b, pb], [zs_b])
                        yield
                        yield
                        yield from _delay(env.get('delay', 0))
                        yo = ch * TT_ + q0
                        for gg in range(4):
                            MM(k, B(6)[:, yo:yo + 64], P["cpad"][:, 0, g0 + gg, :], zc[:, gg * 64:(gg + 1) * 64],
                               hb == 0 and gg == 0, False, [pb, zc_b], [Bb(6)])
                            MM(k, B(6)[:, yo:yo + 64], P["cpad"][:, 1, g0 + gg, :], zs[:, gg * 64:(gg + 1) * 64],
                               False, hb == 1 and gg == 3, [pb, zs_b], [Bb(6)])
                        yield
            for ch in range(2):
                STT(k, sf1[:], uf[:, ch, :], P["lv"][:, 4, ch:ch + 1], B(6)[:, ch * TT_:(ch + 1) * TT_], ALU.mult, ALU.add,
                    [uf_b, pb, Bb(6)], [sf1_b])
                yield
                ACT(k, sf2[:], sf1[:], AF.Gelu_apprx_tanh, [sf1_b], [sf2_b])
                yield
                CP(k, k.pool, yb[:, ch, :], sf2[:], [sf2_b], [yb_b])
                yield
                yield
            for nch in range(2):
                yield from _delay(env.get('delay', 0))
                for kk in range(2):
                    MM(k, B(5)[:, nch * TT_:(nch + 1) * TT_], P["wglu"][:, kk, nch * 128:(nch + 1) * 128],
                       yb[:, kk, :], kk == 0, kk == 1, [pb, yb_b], [Bb(5)])
                ACT(k, sf1[:], B(5)[:, nch * TT_:(nch + 1) * TT_], AF.Sigmoid, [Bb(5), pb], [sf1_b], bias=P["bglu"][:, nch:nch + 1])
                yield
                TT(k, k.dve, mixT[:, 4 + nch, :], yb[:, nch, :], sf1[:], ALU.mult, [yb_b, sf1_b], [mixT_b])
                yield
                yield

        def _sec_lru():
            for ch in range(2):
                cw = P["cw"]
                TS(k, k.dve, f1[:], rxh[:, ch, 3:3 + TT_], cw[:, ch, 3:4], ALU.mult, [rxh_b, pb], [f1_b],
                   s2=P["lv"][:, 0, ch:ch + 1], op1=ALU.add)
                yield
                for tap in range(3):
                    STT(k, f1[:], rxh[:, ch, tap:tap + TT_], cw[:, ch, tap:tap + 1], f1[:], ALU.mult, ALU.add,
                        [rxh_b, pb, f1_b], [f1_b])
                    yield
                CP(k, k.pool, xcb[:], f1[:], [f1_b], [xcb_b])
                yield
                yield
                yield from _delay(env.get('delay', 0))
                MM(k, B(0)[:, 0:TT_], P["wa"][:, ch, :], xcb[:], True, True, [pb, xcb_b], [Bb(0)])
                MM(k, B(1)[:, 0:TT_], P["wx"][:, ch, :], xcb[:], True, True, [pb, xcb_b], [Bb(1)])
                ACT(k, f2[:], B(0)[:, 0:TT_], AF.Sigmoid, [Bb(0), pb], [f2_b], bias=P["lv"][:, 1, ch:ch + 1])
                yield
                ACT(k, f3[:], B(1)[:, 0:TT_], AF.Sigmoid, [Bb(1), pb], [f3_b], bias=P["lv"][:, 2, ch:ch + 1])
                yield
                ACT(k, f4[:], f2[:], AF.Exp, [f2_b, pb], [f4_b], scale=P["lsp"][:, 2, ch:ch + 1])
                yield
                ACT(k, f5[:], f2[:], AF.Exp, [f2_b, pb], [f5_b], scale=P["lsp"][:, 3, ch:ch + 1])
                yield
                yield
                TS(k, k.dve, f5[:], f5[:], -1.0, ALU.mult, [f5_b], [f5_b], s2=1.0, op1=ALU.add)
                yield
                TS(k, k.dve, f5[:], f5[:], 1e-12, ALU.max, [f5_b], [f5_b])
                yield
                ACT(k, f5[:], f5[:], AF.Sqrt, [f5_b], [f5_b])
                yield
                TT(k, k.pool, f3[:], f3[:], f1[:], ALU.mult, [f3_b, f1_b], [f3_b])
                yield
                TT(k, k.dve, f3[:], f3[:], f5[:], ALU.mult, [f3_b, f5_b], [f3_b])
                yield
                yield
                k.op(k.dve, lambda h: h.tensor_tensor_scan(out=f2[:], data0=f4[:], data1=f3[:], initial=hst[:, ch:ch + 1],
                                                           op0=ALU.mult, op1=ALU.add),
                     reads=[f4_b, f3_b, hst_b], writes=[f2_b])
                yield
                CP(k, k.pool, hst[:, ch:ch + 1], f2[:, TT_ - 1:TT_], [f2_b], [hst_b])
                yield
                ACT(k, f1[:], rg[:, ch, :], AF.Gelu_apprx_tanh, [rg_b], [f1_b])
                yield
                TT(k, k.dve, mixT[:, 6 + ch, :], f2[:], f1[:], ALU.mult, [f2_b, f1_b], [mixT_b])
                yield
            yield
            CP(k, k.pool, rxh[:, :, 0:3], rxh[:, :, TT_:TT_ + 3], [rxh_b], [rxh_b])
            yield

        def _chain(*fs):
            for f_ in fs:
                for _ in f_():
                    yield
        SK = env.get("skip", ())
        streams = [_chain(*[f_ for n_, f_ in (("gla", _sec_gla), ("lru", _sec_lru)) if n_ not in SK])]
        if mt + 1 < NMT_:
            load_x(mt + 1)
            load_rope(mt + 1)
            streams.append(nt.run_gen(xin2[(mt + 1) % 2][0], xin2[(mt + 1) % 2][1], gcol, gcol_b, hT, hT_b))
        if "diff" not in SK:
            streams.append(_sec_diff())
        if "s5" not in SK:
            streams.append(_sec_s5())
        if env.get("serial"):
            for g_ in streams:
                for _ in g_:
                    pass
        else:
            while streams:
                for g_ in list(streams):
                    try:
                        next(g_)
                    except StopIteration:
                        streams.remove(g_)

        if dbg is not None:
            k.dma(k.sp, dbg[:, :, t0:t0 + TT_].rearrange("j p t -> p j t"), mixT[:], reads=[mixT_b], writes=[Buf("dbg")])

        def _sec_out():
            for s in range(NS):
                for hf in range(2):
                    for kk in range(8):
                        MM(k, B(hf)[:, 0:512], mixT[:, kk, s * 128:(s + 1) * 128], wout[:, kk, hf * 512:(hf + 1) * 512],
                           kk == 0, kk == 7, [mixT_b, wout_b], [Bb(hf)], inc=(kk == 7))
                post_norm_residual(k, [B(0)[:, 0:512], B(1)[:, 0:512]], [Bb(0), Bb(1)], xin[:, s, :], xin_b, gpost, gpost_b,
                                   st2, st2_b, nt.junk, nt.junk_b, tmpo, tmpo_b, xin[:, s, :], xin_b)

        if "out" not in env.get("skip", ()):
            _sec_out()
        k.dma(k.sp, xd[mt], xin[:], reads=[xin_b], writes=[x_dst_b[mt]])
    k.pop()


WEIGHT_SPECS = [
    ("ln_mix_post", [L, D]), ("ln_ffn_post", [L, D]),
    ("ffn_w1", [L, D, DFF]), ("ffn_w2", [L, DFF, D]),
    ("ln_mix_pre_col", [L, 128, 8]), ("ln_ffn_pre_col", [L, 128, 8]),
    ("w_in_ext", [L, D, NCOL]), ("w_out", [L, D, D]),
    ("rope_cos", [128, S]), ("rope_ssin", [128, S]),
    ("gla_w_gate", [L, 16, 128]), ("gla_b_gate_col", [L, 128, 1]), ("gla_norm", [L, 64]),
    ("diff_lq1", [L, 32]), ("diff_lk1", [L, 32]), ("diff_lq2", [L, 32]), ("diff_lk2", [L, 32]), ("diff_norm", [L, 64]),
    ("s5_log_step", [L, 16]), ("s5_a_re_T", [L, 64, 16]), ("s5_a_im_T", [L, 64, 16]),
    ("s5_b_re_p", [L, 64, 256]), ("s5_b_im_p", [L, 64, 256]), ("s5_c_re_p", [L, 64, 256]), ("s5_c_im_p", [L, 64, 256]),
    ("s5_d_col", [L, 128, 2]), ("s5_w_glu", [L, 256, 256]), ("s5_b_glu_col", [L, 128, 2]),
    ("lru_conv_w_col", [L, 128, 8]), ("lru_conv_b_col", [L, 128, 2]), ("lru_w_a", [L, 4, 64, 64]), ("lru_w_x", [L, 4, 64, 64]),
    ("lru_b_a_col", [L, 128, 2]), ("lru_b_x_col", [L, 128, 2]), ("lru_lambda_col", [L, 128, 2]),
]


def build(mode="full", opts=None):
    nc = bass.Bass("TRN2", target_bir_lowering=False)
    env = dict(opts or {})
    x = nc.dram_tensor("x", [S, D], F32, kind="ExternalInput").ap()
    for name, shape in WEIGHT_SPECS:
        env[name] = nc.dram_tensor(name, shape, F32, kind="ExternalInput").ap()
    env["lru_conv_w_col"] = env["lru_conv_w_col"].rearrange("l p (c t) -> l p c t", c=2)
    y = nc.dram_tensor("y", [S, D], F32, kind="ExternalOutput").ap()
    k = K(nc)
    env["_k"] = k
    eps = k.sb([128, 1], F32, "eps_c")
    eps_b = Buf("eps")
    k.op(k.pool, lambda h: h.memset(eps[:], EPS), writes=[eps_b])
    k.eps_ap = eps[:]
    env["banks"] = [k.ps([128, 512], F32, "bank%d" % i) for i in range(7)]
    env["bank_b"] = [Buf("bank%d" % i, True) for i in range(7)]
    env["pT1"] = k.ps([128, 1024], BF16, "pT1")
    env["pT1_b"] = Buf("pT1", True)
    env["pT"], env["pT_b"] = env["pT1"], env["pT1_b"]
    env["ident"], env["ident_b"] = make_ident(k, nc)
    C = make_consts(k, env)
    k.barrier()

    NMT = 16
    mk = lambda p: [Buf("%s%d" % (p, i)) for i in range(NMT)]
    x_b, y_b = mk("x"), mk("y")
    dbg = None
    if mode == "ffn0":
        ffn_pass(k, env, 0, x, x_b, y, y_b)
    elif mode == "mix0":
        dbg = nc.dram_tensor("dbg", [8, 128, S], BF16, kind="ExternalOutput").ap()
        mixer_pass(k, env, 0, x, x_b, y, y_b, C, dbg=dbg)
    elif mode == "full":
        xa = nc.dram_tensor("xa", [S, D], F32, kind="Internal").ap()
        xb = nc.dram_tensor("xb", [S, D], F32, kind="Internal").ap()
        xa_b, xb_b = mk("xa"), mk("xb")
        mixer_pass(k, env, 0, x, x_b, xa, xa_b, C)
        ffn_pass(k, env, 0, xa, xa_b, xb, xb_b)
        mixer_pass(k, env, 1, xb, xb_b, xa, xa_b, C)
        ffn_pass(k, env, 1, xa, xa_b, y, y_b)
    k.finish(y_b)
    k.barrier()
    return nc, env


_CACHE = {}


def _rope_tables():
    inv = 10000.0 ** (-np.arange(0, 32, 2, dtype=np.float32) / 32.0)
    ang = np.arange(S, dtype=np.float32)[:, None] * inv[None, :].astype(np.float32)
    ang = ang.astype(np.float32)
    emb = np.concatenate([ang, ang], axis=-1)
    cos = np.cos(emb).astype(np.float32).T
    sin = np.sin(emb).astype(np.float32).T
    ssin = np.concatenate([-sin[:16], sin[16:]], axis=0)
    return np.tile(cos, (4, 1)), np.tile(ssin, (4, 1))


def _prep_weights(inputs):
    src = {k_: np.asarray(v) for k_, v in inputs.items()}
    col8 = lambda a: a.reshape(L, 8, 128).transpose(0, 2, 1)
    col2 = lambda a: a.reshape(L, 2, 128).transpose(0, 2, 1)
    src["ln_mix_pre_col"] = col8(src["ln_mix_pre"])
    src["ln_ffn_pre_col"] = col8(src["ln_ffn_pre"])
    w = src["w_in"]
    seg = lambda a, b: w[:, :, a:b]

    def swap(a0):
        parts = []
        for blk in range(8):
            b0 = a0 + blk * 32
            parts += [w[:, :, b0 + 16:b0 + 32], w[:, :, b0:b0 + 16]]
        return np.concatenate(parts, axis=-1)

    src["w_in_ext"] = np.concatenate([
        seg(0, 128), seg(128, 256), seg(784, 1040), swap(784), seg(1040, 1296), swap(1040),
        seg(1552, 1808), seg(1808, 2064), seg(2064, 2320), seg(512, 528), np.zeros((L, D, 48), np.float32),
        seg(256, 512), seg(528, 784), seg(1296, 1552)], axis=-1)
    assert src["w_in_ext"].shape[-1] == NCOL
    src["rope_cos"], src["rope_ssin"] = _rope_tables()
    src["gla_b_gate_col"] = src["gla_b_gate"].reshape(L, 128, 1)
    src["s5_a_re_T"] = src["s5_a_re"].transpose(0, 2, 1)
    src["s5_a_im_T"] = src["s5_a_im"].transpose(0, 2, 1)
    src["s5_b_re_p"] = src["s5_b_re"].transpose(0, 2, 1, 3).reshape(L, 64, 256)
    src["s5_b_im_p"] = src["s5_b_im"].transpose(0, 2, 1, 3).reshape(L, 64, 256)
    src["s5_c_re_p"] = src["s5_c_re"].transpose(0, 3, 1, 2).reshape(L, 64, 256)
    src["s5_c_im_p"] = src["s5_c_im"].transpose(0, 3, 1, 2).reshape(L, 64, 256)
    src["s5_d_col"] = col2(src["s5_d"])
    src["s5_b_glu_col"] = col2(src["s5_b_glu"])
    src["lru_conv_w_col"] = src["lru_conv_w"].reshape(L, 4, 2, 128).transpose(0, 3, 2, 1).reshape(L, 128, 8)
    for nm_ in ("lru_conv_b", "lru_b_a", "lru_b_x", "lru_lambda"):
        src[nm_ + "_col"] = col2(src[nm_])
    out = {}
    for name, shape in WEIGHT_SPECS:
        a = np.ascontiguousarray(src[name], dtype=np.float32)
        assert list(a.shape) == list(shape), (name, a.shape, shape)
        out[name] = a
    return out


def run(inputs, mode="full", want=("y",), opts=None):
    key = (mode, repr(sorted((opts or {}).items())))
    if key not in _CACHE:
        _CACHE[key] = build(mode, opts)
    nc, env = _CACHE[key]
    w = _prep_weights(inputs)
    x = np.ascontiguousarray(inputs["x"], dtype=np.float32)
    in_maps = []
    for c in range(NCORES):
        m = dict(w)
        m["x"] = x[c]
        in_maps.append(m)
    res = run_bass_kernel_spmd(nc, in_maps, core_ids=list(range(NCORES)))
    outs = [np.stack([r[nm_] for r in res.results], axis=0) for nm_ in want]
    return outs[0] if len(outs) == 1 else outs


def kernel(**inputs):
    return run(inputs, "full")
```

```python
import math
import numpy as np
import concourse.bass as bass
import concourse.mybir as mybir
from concourse.bass_utils import run_bass_kernel_spmd

F32 = mybir.dt.float32
BF16 = mybir.dt.bfloat16
I32 = mybir.dt.int32
AF = mybir.ActivationFunctionType
ALU = mybir.AluOpType
AX = mybir.AxisListType

S = 4096
D = 1024
DFF = 4096
L = 2
EPS = 1e-6
NCORES = 8


class Buf:
    __slots__ = ("name", "w", "r", "excl")

    def __init__(self, name, excl=False):
        self.name = name
        self.w = {}
        self.r = {}
        self.excl = excl


class Eng:
    def __init__(self, nc, name, h, sync_self):
        self.name = name
        self.h = h
        self.sem = nc.semaphore("sem_" + name).__enter__()
        self.cnt = 0
        self.waited = {}
        self.sync_self = sync_self


class K:
    def __init__(self, nc, n_dma_sems=20):
        self.nc = nc
        self.pe = Eng(nc, "pe", nc.tensor, False)
        self.act = Eng(nc, "act", nc.scalar, True)
        self.dve = Eng(nc, "dve", nc.vector, True)
        self.pool = Eng(nc, "pool", nc.gpsimd, True)
        self.sp = Eng(nc, "sp", nc.sync, False)
        self.dsem = {}
        for q in ("sp", "pool", "act"):
            self.dsem[q] = [[nc.semaphore("dq_%s_%d" % (q, i)).__enter__(), 0] for i in range(n_dma_sems)]
        self.dnext = {"sp": 0, "pool": 0, "act": 0}
        self._n = 0
        self.stacks = []

    def push(self):
        from contextlib import ExitStack
        self.stacks.append(ExitStack())

    def pop(self):
        self.barrier()
        self.stacks.pop().close()

    def barrier(self):
        engs = [self.pe, self.act, self.dve, self.pool, self.sp]
        deps = [(e.name, e.sem, e.cnt) for e in engs if e.cnt > 0]
        for q, pool in self.dsem.items():
            for j, (sem, cnt) in enumerate(pool):
                if cnt > 0:
                    deps.append(("d_%s_%d" % (q, j), sem, 16 * cnt))
        for e in engs:
            sv = e.sync_self
            e.sync_self = False
            self._wait(e, deps)
            e.sync_self = sv

    def uid(self, p="t"):
        self._n += 1
        return "%s%d" % (p, self._n)

    def sb(self, shape, dtype, name=None):
        cm = self.nc.sbuf_tensor(name or self.uid("sb"), list(shape), dtype)
        if self.stacks:
            return self.stacks[-1].enter_context(cm)
        return cm.__enter__()

    def ps(self, shape, dtype, name=None):
        return self.nc.psum_tensor(name or self.uid("ps"), list(shape), dtype).__enter__()

    def _wait(self, e, deps):
        best = {}
        for (key, sem, val) in deps:
            if key == e.name and not e.sync_self:
                continue
            if key not in best or best[key][1] < val:
                best[key] = (sem, val)
        for key, (sem, val) in best.items():
            if e.waited.get(key, 0) >= val:
                continue
            e.h.wait_ge(sem, val)
            e.waited[key] = val

    def _deps(self, reads, writes):
        deps = []
        for b in reads:
            deps.extend(b.w.values())
            if b.excl:
                deps.extend(b.r.values())
        for b in writes:
            deps.extend(b.w.values())
            deps.extend(b.r.values())
        return deps

    def op(self, e, fn, reads=(), writes=(), inc=True):
        self._wait(e, self._deps(reads, writes))
        inst = fn(e.h)
        if inc:
            e.cnt += 1
            inst.then_inc(e.sem, 1)
            t = (e.name, e.sem, e.cnt)
        else:
            assert e is self.pe
            t = (e.name, e.sem, e.cnt + 1)
        for b in reads:
            b.r[e.name] = t
        for b in writes:
            b.w[e.name] = t
        return t

    def dma(self, q, out, in_, reads=(), writes=()):
        pool = self.dsem[q.name]
        j = self.dnext[q.name]
        self.dnext[q.name] = (j + 1) % len(pool)
        sem, cnt = pool[j]
        key = "d_%s_%d" % (q.name, j)
        deps = self._deps(reads, writes)
        if cnt > 0:
            deps.append((key, sem, 16 * cnt))
        self._wait(q, deps)
        q.h.dma_start(out=out, in_=in_).then_inc(sem, 16)
        pool[j][1] = cnt + 1
        t = (key, sem, 16 * (cnt + 1))
        for b in reads:
            b.r[key] = t
        for b in writes:
            b.w[key] = t
        return t

    def finish(self, bufs):
        deps = []
        for b in bufs:
            deps.extend(b.w.values())
        self._wait(self.sp, deps)


def make_ident(k, nc):
    ident = k.sb([128, 128], BF16, "ident")
    b = Buf("ident")
    k.op(k.pool, lambda h: h.memset(ident[:], 1.0), writes=[b])
    k.op(k.pool, lambda h: h.affine_select(out=ident[:], in_=ident[:], pattern=[[1, 128]],
                                            compare_op=ALU.is_equal, fill=0.0, base=0,
                                            channel_multiplier=-1), reads=[b], writes=[b])
    return ident, b


class NormT:
    def __init__(self, k, ns, ident, ident_b, pT, pT_b, tag):
        self.k = k
        self.ns = ns
        self.ident, self.ident_b = ident, ident_b
        self.pT, self.pT_b = pT, pT_b
        self.xn = k.sb([128, 1024], BF16, tag + "_xn")
        self.xn_b = Buf(tag + "_xn")
        self.junk, self.junk_b = self.xn, self.xn_b
        self.st = k.sb([128, 4 * ns], F32, tag + "_st")
        self.st_b = Buf(tag + "_st")

    def run(self, xin, xin_b, gcol, gcol_b, hT, hT_b):
        for _ in self.run_gen(xin, xin_b, gcol, gcol_b, hT, hT_b):
            pass

    def run_gen(self, xin, xin_b, gcol, gcol_b, hT, hT_b):
        k, ns = self.k, self.ns
        st = self.st
        for s in range(ns):
            k.op(k.act, lambda h: h.activation(out=self.junk[:], in_=xin[:, s, :], func=AF.Square,
                                               accum_out=st[:, s:s + 1]),
                 reads=[xin_b], writes=[self.junk_b, self.st_b])
        k.op(k.act, lambda h: h.activation(out=st[:, ns:2 * ns], in_=st[:, 0:ns], func=AF.Ln,
                                           scale=1.0 / D, bias=self.k.eps_ap),
             reads=[self.st_b], writes=[self.st_b])
        k.op(k.act, lambda h: h.activation(out=st[:, 2 * ns:3 * ns], in_=st[:, ns:2 * ns], func=AF.Exp,
                                           scale=-0.5),
             reads=[self.st_b], writes=[self.st_b])
        yield
        for s in range(ns):
            k.op(k.act, lambda h: h.activation(out=self.xn[:], in_=xin[:, s, :], func=AF.Copy,
                                               scale=st[:, 2 * ns + s:2 * ns + s + 1]),
                 reads=[xin_b, self.st_b], writes=[self.xn_b])
            for kk in range(8):
                k.op(k.pe, lambda h: h.transpose(out=self.pT[:, kk * 128:(kk + 1) * 128],
                                                 in_=self.xn[:, kk * 128:(kk + 1) * 128],
                                                 identity=self.ident[:]),
                     reads=[self.xn_b, self.ident_b], writes=[self.pT_b], inc=(kk == 7))
            k.op(k.dve, lambda h: h.tensor_tensor(
                out=hT[:, :, s * 128:(s + 1) * 128],
                in0=self.pT[:].rearrange("p (k t) -> p k t", k=8),
                in1=gcol[:].unsqueeze(2).to_broadcast([128, 8, 128]),
                op=ALU.mult),
                reads=[self.pT_b, gcol_b], writes=[hT_b])
            yield


def post_norm_residual(k, ps_halves, ps_bufs, xres_ap, xres_b, gpost, gpost_b, st, st_b, junk, junk_b,
                       tmp, tmp_b, out_ap, out_b):
    for hf in range(2):
        k.op(k.act, lambda h: h.activation(out=tmp[:, hf * 512:(hf + 1) * 512], in_=ps_halves[hf], func=AF.Square,
                                           accum_out=st[:, hf:hf + 1]),
             reads=[ps_bufs[hf]], writes=[tmp_b, st_b])
    k.op(k.dve, lambda h: h.tensor_tensor(out=st[:, 2:3], in0=st[:, 0:1], in1=st[:, 1:2], op=ALU.add),
         reads=[st_b], writes=[st_b])
    k.op(k.act, lambda h: h.activation(out=st[:, 3:4], in_=st[:, 2:3], func=AF.Ln, scale=1.0 / D, bias=k.eps_ap),
         reads=[st_b], writes=[st_b])
    k.op(k.act, lambda h: h.activation(out=st[:, 4:5], in_=st[:, 3:4], func=AF.Exp, scale=-0.5),
         reads=[st_b], writes=[st_b])
    for hf in range(2):
        k.op(k.dve, lambda h: h.scalar_tensor_tensor(out=tmp[:, hf * 512:(hf + 1) * 512], in0=ps_halves[hf],
                                                     scalar=st[:, 4:5], in1=gpost[:, hf * 512:(hf + 1) * 512],
                                                     op0=ALU.mult, op1=ALU.mult),
             reads=[ps_bufs[hf], st_b, gpost_b], writes=[tmp_b])
    k.op(k.pool, lambda h: h.tensor_tensor(out=out_ap, in0=tmp[:], in1=xres_ap, op=ALU.add),
         reads=[tmp_b, xres_b], writes=[out_b])


def ffn_pass(k, env, l, x_src, x_src_b, x_dst, x_dst_b):
    nc = k.nc
    NS = 2
    TT = NS * 128
    NMT = S // TT
    banks, bank_b = env["banks"], env["bank_b"]
    pT, pT_b = env["pT"], env["pT_b"]
    k.push()
    sbase = nc.sbuf_bytes_remaining

    w1 = k.sb([128, 8, DFF], BF16, "w1_%d" % l)
    w2 = k.sb([128, 32, D], BF16, "w2_%d" % l)
    w1_b = [Buf("w1f%d" % i) for i in range(8)]
    w2_b = [Buf("w2k%d" % i) for i in range(8)]
    w1d = env["ffn_w1"][l].rearrange("(k p) f -> p k f", p=128)
    w2d = env["ffn_w2"][l].rearrange("(k p) f -> p k f", p=128)
    for i in range(8):
        k.dma(k.pool, w1[:, :, i * 512:(i + 1) * 512], w1d[:, :, i * 512:(i + 1) * 512], writes=[w1_b[i]])
    for i in range(8):
        k.dma(k.pool, w2[:, 4 * i:4 * i + 4, :], w2d[:, 4 * i:4 * i + 4, :], writes=[w2_b[i]])

    gcol = k.sb([128, 8], F32, "f_gcol%d" % l)
    gcol_b = Buf("gcol")
    k.dma(k.sp, gcol[:], env["ln_ffn_pre_col"][l], writes=[gcol_b])
    gpost = k.sb([128, D], F32, "f_gpost%d" % l)
    gpost_b = Buf("gpost")
    k.dma(k.sp, gpost[:], env["ln_ffn_post"][l:l + 1, :].partition_broadcast(128), writes=[gpost_b])

    xin = [k.sb([128, NS, D], F32, "f_xin%d_%d" % (l, i)) for i in range(2)]
    xin_b = [Buf("xin%d" % i) for i in range(2)]
    hT = [k.sb([128, 8, TT], BF16, "f_hT%d_%d" % (l, i)) for i in range(2)]
    hT_b = [Buf("hT%d" % i) for i in range(2)]
    hid = k.sb([128, 32, TT], BF16, "f_hid%d" % l)
    hid_b = [Buf("hid%d" % i) for i in range(4)]
    rl = [k.sb([128, TT], F32, "f_rl%d_%d" % (l, i)) for i in range(2)]
    rl_b = [Buf("rl%d" % i) for i in range(2)]
    tmp = k.sb([128, D], F32, "f_tmp%d" % l)
    tmp_b = Buf("tmp")
    st2 = k.sb([128, 8], F32, "f_st2%d" % l)
    st2_b = Buf("st2")
    nt = NormT(k, NS, env["ident"], env["ident_b"], pT, pT_b, "f_nt%d" % l)

    xs = x_src.rearrange("(m s p) d -> m p s d", s=NS, p=128)
    xd = x_dst.rearrange("(m s p) d -> m p s d", s=NS, p=128)

    def load(mt):
        k.dma(k.sp, xin[mt % 2][:], xs[mt], reads=[x_src_b[mt]], writes=[xin_b[mt % 2]])

    load(0)
    for mt in range(NMT):
        sl = mt % 2
        if mt + 1 < NMT:
            load(mt + 1)
        nt.run(xin[sl], xin_b[sl], gcol, gcol_b, hT[sl], hT_b[sl])
        for fc in range(32):
            pb = fc % 2
            for kk in range(8):
                k.op(k.pe, lambda h: h.matmul(banks[pb][:, 0:TT], lhsT=w1[:, kk, fc * 128:(fc + 1) * 128],
                                              rhs=hT[sl][:, kk, :], start=(kk == 0), stop=(kk == 7)),
                     reads=[w1_b[fc // 4], hT_b[sl]], writes=[bank_b[pb]], inc=(kk == 7))
            k.op(k.act, lambda h: h.activation(out=rl[pb][:], in_=banks[pb][:, 0:TT], func=AF.Relu),
                 reads=[bank_b[pb]], writes=[rl_b[pb]])
            k.op(k.pool, lambda h: h.tensor_tensor(out=hid[:, fc, :], in0=rl[pb][:], in1=rl[pb][:], op=ALU.mult),
                 reads=[rl_b[pb]], writes=[hid_b[fc // 8]])
        for s in range(NS):
            pbs = [2 + 2 * (s % 2), 3 + 2 * (s % 2)]
            for hf in range(2):
                for fc in range(32):
                    k.op(k.pe, lambda h: h.matmul(banks[pbs[hf]][:, 0:512], lhsT=hid[:, fc, s * 128:(s + 1) * 128],
                                                  rhs=w2[:, fc, hf * 512:(hf + 1) * 512],
                                                  start=(fc == 0), stop=(fc == 31)),
                         reads=[hid_b[fc // 8], w2_b[fc // 4]], writes=[bank_b[pbs[hf]]], inc=(fc == 31))
            post_norm_residual(k, [banks[pbs[0]][:, 0:512], banks[pbs[1]][:, 0:512]],
                               [bank_b[pbs[0]], bank_b[pbs[1]]],
                               xin[sl][:, s, :], xin_b[sl], gpost, gpost_b, st2, st2_b,
                               nt.junk, nt.junk_b, tmp, tmp_b, xin[sl][:, s, :], xin_b[sl])
        k.dma(k.sp, xd[mt], xin[sl][:], reads=[xin_b[sl]], writes=[x_dst_b[mt]])
    env["sbuf_used_ffn"] = sbase - nc.sbuf_bytes_remaining
    k.pop()


def TT(k, e, out, in0, in1, op, R, W):
    return k.op(e, lambda h: h.tensor_tensor(out=out, in0=in0, in1=in1, op=op), reads=R, writes=W)


def TS(k, e, out, in0, s1, op0, R, W, s2=None, op1=None):
    if op1 is None:
        return k.op(e, lambda h: h.tensor_scalar(out=out, in0=in0, scalar1=s1, scalar2=None, op0=op0), reads=R, writes=W)
    return k.op(e, lambda h: h.tensor_scalar(out=out, in0=in0, scalar1=s1, scalar2=s2, op0=op0, op1=op1), reads=R, writes=W)


def STT(k, out, in0, scalar, in1, op0, op1, R, W):
    return k.op(k.dve, lambda h: h.scalar_tensor_tensor(out=out, in0=in0, scalar=scalar, in1=in1, op0=op0, op1=op1),
                reads=R, writes=W)


def ACT(k, out, in_, func, R, W, scale=1.0, bias=None):
    if bias is None:
        return k.op(k.act, lambda h: h.activation(out=out, in_=in_, func=func, scale=scale), reads=R, writes=W)
    return k.op(k.act, lambda h: h.activation(out=out, in_=in_, func=func, scale=scale, bias=bias), reads=R, writes=W)


def CP(k, e, out, in_, R, W):
    return k.op(e, lambda h: h.tensor_copy(out=out, in_=in_), reads=R, writes=W)


def MM(k, out, lhsT, rhs, start, stop, R, W, sgc=False, inc=True):
    return k.op(k.pe, lambda h: h.matmul(out, lhsT=lhsT, rhs=rhs, start=start, stop=stop, skip_group_check=sgc),
                reads=R, writes=W, inc=inc)


def _delay(n):
    for _ in range(n):
        yield


def TR(k, out, in_, ident, R, W):
    return k.op(k.pe, lambda h: h.transpose(out=out, in_=in_, identity=ident), reads=R, writes=W)


def MSET(k, e, ap, val, W):
    return k.op(e, lambda h: h.memset(ap, val), writes=W)


def ASEL(k, out, in_, pattern, op, base, cm, R, W):
    return k.op(k.pool, lambda h: h.affine_select(out=out, in_=in_, pattern=pattern, compare_op=op, fill=0.0,
                                                  base=base, channel_multiplier=cm), reads=R, writes=W)


OFF_GQ, OFF_GK = 0, 128
OFF_DQ, OFF_DQS, OFF_DK, OFF_DKS = 256, 512, 768, 1024
OFF_SU, OFF_RX, OFF_RG, OFF_GLR = 1280, 1536, 1792, 2048
OFF_TMG, OFF_DV = 2112, 2624
NCOL = 2880
TWO_PI = 2.0 * math.pi
CW1 = 6.28125
CW2 = TWO_PI - 6.28125


def make_consts(k, env):
    c = {}
    b = Buf("consts")
    c["b"] = b
    one = k.sb([128, 1], F32, "c_one"); MSET(k, k.pool, one[:], 1.0, [b]); c["one"] = one
    sgn = k.sb([128, 1], F32, "c_sgn"); MSET(k, k.pool, sgn[0:64, :], 1.0, [b]); MSET(k, k.pool, sgn[64:128, :], -1.0, [b])
    c["sgn"] = sgn
    tri = k.sb([128, 128], BF16, "c_tri"); MSET(k, k.pool, tri[:], 1.0, [b])
    ASEL(k, tri[:], tri[:], [[1, 128]], ALU.is_ge, 0, -1, [b], [b]); c["tri"] = tri
    tri64 = k.sb([64, 64], F32, "c_tri64"); MSET(k, k.pool, tri64[:], 1.0, [b])
    ASEL(k, tri64[:], tri64[:], [[1, 64]], ALU.is_ge, 0, -1, [b], [b]); c["tri64"] = tri64
    hm = k.sb([128, 4], F32, "c_hm"); MSET(k, k.pool, hm[:], 1.0, [b])
    ASEL(k, hm[:], hm[:], [[-32, 4]], ALU.is_ge, 0, 1, [b], [b])
    ASEL(k, hm[:], hm[:], [[32, 4]], ALU.is_ge, 31, -1, [b], [b]); c["hm"] = hm
    rm = k.sb([128, 8], F32, "c_rm16"); MSET(k, k.pool, rm[:], 1.0, [b])
    ASEL(k, rm[:], rm[:], [[-16, 8]], ALU.is_ge, 0, 1, [b], [b])
    ASEL(k, rm[:], rm[:], [[16, 8]], ALU.is_ge, 15, -1, [b], [b]); c["rm16"] = rm
    jf = k.sb([128, 128], F32, "c_jf"); j2 = k.sb([128, 128], F32, "c_j2")
    MSET(k, k.pool, jf[:], 1.0, [b]); MSET(k, k.pool, j2[:], 1.0, [b])
    ASEL(k, jf[:], jf[:], [[1, 128]], ALU.is_equal, -64, -1, [b], [b])
    ASEL(k, j2[:], j2[:], [[1, 128]], ALU.is_equal, 64, -1, [b], [b])
    TT(k, k.pool, jf[:], jf[:], j2[:], ALU.add, [b], [b]); c["jf"] = jf
    rs = k.sb([128, 256], F32, "c_reset"); MSET(k, k.pool, rs[:], 1.0, [b])
    for i in range(4):
        MSET(k, k.pool, rs[:, 64 * i:64 * i + 1], 0.0, [b])
    c["reset"] = rs
    return c


def mixer_prep(k, env, l, C):
    nc = k.nc
    P = {}
    cb = C["b"]
    lam_init = 0.8 - 0.6 * math.exp(-0.3 * l)
    pb = Buf("prep")
    P["b"] = pb
    banks, bank_b = env["banks"], env["bank_b"]
    pT1, pT1_b = env["pT1"], env["pT1_b"]

    def sbt(shape, dt, nm):
        return k.sb(shape, dt, "p%d_%s" % (l, nm))

    wgate = sbt([16, 128], BF16, "wgate"); negb = sbt([128, 1], F32, "negb"); gnorm = sbt([128, 64], F32, "gnorm")
    lw = sbt([128, 8], F32, "lw"); dnorm = sbt([128, 64], F32, "dnorm")
    cw = sbt([128, 2, 4], F32, "cw"); lv = sbt([128, 5, 2], F32, "lv"); bglu = sbt([128, 2], F32, "bglu")
    wa = sbt([128, 2, 128], BF16, "wa"); wx = sbt([128, 2, 128], BF16, "wx"); lsp = sbt([128, 4, 2], F32, "lsp")
    wglu = sbt([128, 2, 256], BF16, "wglu"); rho = sbt([128, 16], F32, "rho")
    bpad = sbt([128, 2, 16, 128], BF16, "bpad"); cpad = sbt([128, 2, 16, 128], BF16, "cpad")
    wl = sbt([128, 9, 2, 16], F32, "wl")
    tabc = sbt([128, 16, 64], F32, "tabc"); tabs = sbt([128, 16, 64], F32, "tabs")
    rhot = sbt([128, 16, 64], F32, "rhot")
    k.push()

    def ld(t_ap, src, q=None):
        k.dma(q or k.sp, t_ap, src, writes=[pb])

    ld(wgate[:], env["gla_w_gate"][l], k.pool)
    ld(negb[:], env["gla_b_gate_col"][l])
    TS(k, k.dve, negb[:], negb[:], -1.0, ALU.mult, [pb], [pb])
    ld(gnorm[:], env["gla_norm"][l:l + 1, :].partition_broadcast(128))
    P.update(wgate=wgate, negb=negb, gnorm=gnorm)
    lq = sbt([128, 4, 32], F32, "lq")
    for i, nm in enumerate(("diff_lq1", "diff_lk1", "diff_lq2", "diff_lk2")):
        ld(lq[:, i, :], env[nm][l:l + 1, :].partition_broadcast(128))
    pr = sbt([128, 2, 32], F32, "lpr")
    TT(k, k.dve, pr[:, 0, :], lq[:, 0, :], lq[:, 1, :], ALU.mult, [pb], [pb])
    TT(k, k.dve, pr[:, 1, :], lq[:, 2, :], lq[:, 3, :], ALU.mult, [pb], [pb])
    k.op(k.dve, lambda h: h.reduce_sum(out=lw[:, 0:2], in_=pr[:], axis=AX.X), reads=[pb], writes=[pb])
    ACT(k, lw[:, 2:4], lw[:, 0:2], AF.Exp, [pb], [pb])
    TT(k, k.dve, lw[:, 4:5], lw[:, 2:3], lw[:, 3:4], ALU.subtract, [pb], [pb])
    TS(k, k.dve, lw[:, 5:6], lw[:, 4:5], lam_init, ALU.add, [pb], [pb], s2=-1.0, op1=ALU.mult)
    ld(dnorm[:], env["diff_norm"][l:l + 1, :].partition_broadcast(128))
    TS(k, k.dve, dnorm[:], dnorm[:], 1.0 - lam_init, ALU.mult, [pb], [pb])
    P.update(nlam=lw[:, 5:6], dnorm=dnorm)
    ld(cw[:], env["lru_conv_w_col"][l])
    for i, nm in enumerate(("lru_conv_b_col", "lru_b_a_col", "lru_b_x_col", "lru_lambda_col", "s5_d_col")):
        ld(lv[:, i, :], env[nm][l])
    ld(bglu[:], env["s5_b_glu_col"][l])
    MSET(k, k.pool, wa[:], 0.0, [pb]); MSET(k, k.pool, wx[:], 0.0, [pb])
    for n in range(4):
        r0 = (n % 2) * 64
        ld(wa[r0:r0 + 64, n // 2, r0:r0 + 64], env["lru_w_a"][l, n], k.pool)
        ld(wx[r0:r0 + 64, n // 2, r0:r0 + 64], env["lru_w_x"][l, n], k.pool)
    ACT(k, lsp[:, 0, :], lv[:, 3, :], AF.Exp, [pb], [pb], scale=-1.0)
    ACT(k, lsp[:, 1, :], lsp[:, 0, :], AF.Ln, [pb, cb], [pb], bias=C["one"][:])
    TS(k, k.dve, lsp[:, 2, :], lsp[:, 1, :], -8.0, ALU.mult, [pb], [pb])
    TS(k, k.dve, lsp[:, 3, :], lsp[:, 1, :], -16.0, ALU.mult, [pb], [pb])
    ld(wglu[:], env["s5_w_glu"][l].rearrange("(k p) n -> p k n", p=128), k.pool)
    P.update(cw=cw, lv=lv, bglu=bglu, wa=wa, wx=wx, lsp=lsp, wglu=wglu)
    lr = sbt([128, 16], F32, "lr"); li = sbt([128, 16], F32, "li"); ls = sbt([128, 16], F32, "ls")
    for hlf in range(2):
        ld(lr[hlf * 64:(hlf + 1) * 64, :], env["s5_a_re_T"][l])
        ld(li[hlf * 64:(hlf + 1) * 64, :], env["s5_a_im_T"][l])
    ld(ls[:], env["s5_log_step"][l:l + 1, :].partition_broadcast(128))
    bri = sbt([128, 2, 256], F32, "bri"); cri = sbt([128, 2, 256], F32, "cri")
    for hlf in range(2):
        sl_ = slice(hlf * 64, (hlf + 1) * 64)
        ld(bri[sl_, 0, :], env["s5_b_re_p"][l]); ld(bri[sl_, 1, :], env["s5_b_im_p"][l])
        ld(cri[sl_, 0, :], env["s5_c_re_p"][l]); ld(cri[sl_, 1, :], env["s5_c_im_p"][l])
    w = sbt([128, 24, 16], F32, "s5w")
    W_ = lambda i: w[:, i, :]
    R, Wr = [pb], [pb]
    ACT(k, W_(0), ls[:], AF.Exp, R, Wr)
    TT(k, k.dve, W_(1), lr[:], W_(0), ALU.mult, R, Wr)
    TT(k, k.dve, W_(2), li[:], W_(0), ALU.mult, R, Wr)
    ACT(k, rho[:], W_(1), AF.Exp, R, Wr)
    ki = sbt([128, 16], I32, "ki")

    def sin_reduced(dst, ang):
        TS(k, k.dve, W_(3), ang, 1.0 / TWO_PI, ALU.mult, R, Wr)
        CP(k, k.dve, ki[:], W_(3), R, Wr)
        CP(k, k.dve, W_(4), ki[:], R, Wr)
        STT(k, W_(5), W_(4), -CW1, ang, ALU.mult, ALU.add, R, Wr)
        STT(k, W_(5), W_(4), -CW2, W_(5), ALU.mult, ALU.add, R, Wr)
        TS(k, k.dve, W_(5), W_(5), 3.1415925, ALU.min, R, Wr, s2=-3.1415925, op1=ALU.max)
        ACT(k, dst, W_(5), AF.Sin, R, Wr)

    sin_reduced(W_(6), W_(2))
    TS(k, k.dve, W_(7), W_(2), math.pi / 2.0, ALU.add, R, Wr)
    sin_reduced(W_(8), W_(7))
    TT(k, k.dve, W_(9), W_(6), W_(6), ALU.mult, R, Wr)
    TT(k, k.dve, W_(10), W_(8), W_(8), ALU.mult, R, Wr)
    TT(k, k.dve, W_(9), W_(9), W_(10), ALU.add, R, Wr)
    TS(k, k.dve, W_(9), W_(9), -0.5, ALU.mult, R, Wr, s2=1.5, op1=ALU.add)
    TT(k, k.dve, W_(6), W_(6), W_(9), ALU.mult, R, Wr)
    TT(k, k.dve, W_(8), W_(8), W_(9), ALU.mult, R, Wr)
    TT(k, k.dve, W_(11), rho[:], W_(8), ALU.mult, R, Wr)
    TT(k, k.dve, W_(12), rho[:], W_(6), ALU.mult, R, Wr)
    TT(k, k.dve, W_(13), lr[:], lr[:], ALU.mult, R, Wr)
    TT(k, k.dve, W_(14), li[:], li[:], ALU.mult, R, Wr)
    TT(k, k.dve, W_(13), W_(13), W_(14), ALU.add, R, Wr)
    k.op(k.dve, lambda h: h.reciprocal(out=W_(13), in_=W_(13)), reads=R, writes=Wr)
    TS(k, k.dve, W_(11), W_(11), -1.0, ALU.add, R, Wr)
    TT(k, k.dve, W_(14), W_(11), lr[:], ALU.mult, R, Wr)
    TT(k, k.dve, W_(15), W_(12), li[:], ALU.mult, R, Wr)
    TT(k, k.dve, W_(14), W_(14), W_(15), ALU.add, R, Wr)
    TT(k, k.dve, W_(14), W_(14), W_(13), ALU.mult, R, Wr)
    TT(k, k.dve, W_(15), W_(12), lr[:], ALU.mult, R, Wr)
    TT(k, k.dve, W_(16), W_(11), li[:], ALU.mult, R, Wr)
    TT(k, k.dve, W_(15), W_(15), W_(16), ALU.subtract, R, Wr)
    TT(k, k.dve, W_(15), W_(15), W_(13), ALU.mult, R, Wr)
    crb = W_(14).unsqueeze(2).to_broadcast([128, 16, 16])
    cib = W_(15).unsqueeze(2).to_broadcast([128, 16, 16])
    bre = bri[:, 0, :].rearrange("p (g h) -> p g h", g=16)
    bim = bri[:, 1, :].rearrange("p (g h) -> p g h", g=16)
    bb = sbt([128, 4, 256], F32, "bb")
    B3 = lambda i: bb[:, i, :].rearrange("p (g h) -> p g h", g=16)
    TT(k, k.dve, B3(0), bre, crb, ALU.mult, R, Wr)
    TT(k, k.dve, B3(1), bim, cib, ALU.mult, R, Wr)
    TT(k, k.dve, B3(0), B3(0), B3(1), ALU.subtract, R, Wr)
    TT(k, k.dve, B3(2), bim, crb, ALU.mult, R, Wr)
    TT(k, k.dve, B3(3), bre, cib, ALU.mult, R, Wr)
    TT(k, k.dve, B3(2), B3(2), B3(3), ALU.add, R, Wr)
    bst = sbt([128, 2, 256], BF16, "bst")
    CP(k, k.dve, bst[0:64, 0, :], bb[0:64, 0, :], R, Wr)
    CP(k, k.dve, bst[64:128, 0, :], bb[64:128, 2, :], R, Wr)
    CP(k, k.dve, bst[0:64, 1, :], bb[0:64, 2, :], R, Wr)
    CP(k, k.dve, bst[64:128, 1, :], bb[64:128, 0, :], R, Wr)
    for v in range(2):
        for ch in range(2):
            TR(k, pT1[:, 0:128], bst[:, v, ch * 128:(ch + 1) * 128], env["ident"][:], [pb, env["ident_b"]], [pT1_b])
            for gg in range(8):
                TS(k, k.dve, bpad[:, v, ch * 8 + gg, :], pT1[:, 0:128], C["rm16"][:, gg:gg + 1], ALU.mult,
                   [pT1_b, cb], [pb])
    cst = sbt([128, 2, 256], F32, "cst")
    CP(k, k.dve, cst[0:64, 0, :], cri[0:64, 0, :], R, Wr)
    TS(k, k.dve, cst[64:128, 0, :], cri[64:128, 1, :], -1.0, ALU.mult, R, Wr)
    TS(k, k.dve, cst[0:64, 1, :], cri[0:64, 1, :], -1.0, ALU.mult, R, Wr)
    CP(k, k.dve, cst[64:128, 1, :], cri[64:128, 0, :], R, Wr)
    MSET(k, k.pool, cpad[:], 0.0, Wr)
    for v in range(2):
        for g in range(16):
            CP(k, k.pool, cpad[:, v, g, (g % 8) * 16:(g % 8) * 16 + 16], cst[:, v, g * 16:(g + 1) * 16], R, Wr)
    CP(k, k.dve, wl[:, 0, 0, :], W_(8), R, Wr)
    TS(k, k.dve, wl[:, 0, 1, :], W_(6), C["sgn"][:, 0:1], ALU.mult, R + [cb], Wr)
    MSET(k, k.pool, tabc[:, :, 0:1], 1.0, Wr); MSET(k, k.pool, tabs[:, :, 0:1], 0.0, Wr)
    t1 = sbt([128, 16, 32], F32, "dbl1"); t2 = sbt([128, 16, 32], F32, "dbl2")
    n = 1
    for lev in range(6):
        wc = wl[:, lev, 0, :].unsqueeze(2).to_broadcast([128, 16, n])
        ws = wl[:, lev, 1, :].unsqueeze(2).to_broadcast([128, 16, n])
        TT(k, k.pool, t1[:, :, 0:n], tabs[:, :, 0:n], ws, ALU.mult, R, Wr)
        TT(k, k.pool, t2[:, :, 0:n], tabc[:, :, 0:n], wc, ALU.mult, R, Wr)
        TT(k, k.pool, tabc[:, :, n:2 * n], t2[:, :, 0:n], t1[:, :, 0:n], ALU.subtract, R, Wr)
        TT(k, k.pool, t1[:, :, 0:n], tabc[:, :, 0:n], ws, ALU.mult, R, Wr)
        TT(k, k.pool, t2[:, :, 0:n], tabs[:, :, 0:n], wc, ALU.mult, R, Wr)
        TT(k, k.pool, tabs[:, :, n:2 * n], t1[:, :, 0:n], t2[:, :, 0:n], ALU.add, R, Wr)
        TT(k, k.dve, W_(17), wl[:, lev, 0, :], wl[:, lev, 0, :], ALU.mult, R, Wr)
        TT(k, k.dve, W_(18), wl[:, lev, 1, :], wl[:, lev, 1, :], ALU.mult, R, Wr)
        TT(k, k.dve, wl[:, lev + 1, 0, :], W_(17), W_(18), ALU.subtract, R, Wr)
        STT(k, wl[:, lev + 1, 1, :], wl[:, lev, 0, :], 2.0, wl[:, lev, 1, :], ALU.mult, ALU.mult, R, Wr)
        n *= 2
    P.update(rho=rho, bpad=bpad, cpad=cpad, tabc=tabc, tabs=tabs, c64=wl[:, 6, 0, :], s64=wl[:, 6, 1, :], rhot=rhot)
    CP(k, k.pool, rhot[:], rho[:].unsqueeze(2).to_broadcast([128, 16, 64]), R, Wr)
    MSET(k, k.pool, rhot[:, :, 0:1], 0.0, Wr)
    k.pop()
    return P


def mixer_pass(k, env, l, x_src, x_src_b, x_dst, x_dst_b, C, dbg=None):
    nc = k.nc
    NS, TT_, NMT = 2, 256, 16
    banks, bank_b = env["banks"], env["bank_b"]
    pT, pT_b, pT1, pT1_b = env["pT"], env["pT_b"], env["pT1"], env["pT1_b"]
    ident, ident_b = env["ident"], env["ident_b"]
    cb = C["b"]
    hm, tri, tri64 = C["hm"], C["tri"], C["tri64"]
    k.push()
    sbase = nc.sbuf_bytes_remaining
    nm = lambda s_: "m%d_%s" % (l, s_)

    win = k.sb([128, 8, NCOL], BF16, nm("win")); win_b = [Buf("win%d" % i) for i in range(8)]
    wind = env["w_in_ext"][l].rearrange("(k p) n -> p k n", p=128)
    for i in range(8):
        k.dma(k.pool, win[:, i, :], wind[:, i, :], writes=[win_b[i]])
    wout = k.sb([128, 8, D], BF16, nm("wout")); wout_b = Buf("wout")
    k.dma(k.pool, wout[:], env["w_out"][l].rearrange("(k p) n -> p k n", p=128), writes=[wout_b])
    gcol = k.sb([128, 8], F32, nm("gcol")); gcol_b = Buf("gcol")
    k.dma(k.sp, gcol[:], env["ln_mix_pre_col"][l], writes=[gcol_b])
    gpost = k.sb([128, D], F32, nm("gpost")); gpost_b = Buf("gpost")
    k.dma(k.sp, gpost[:], env["ln_mix_post"][l:l + 1, :].partition_broadcast(128), writes=[gpost_b])
    kc = k.sb([128, 2, S], BF16, nm("kc")); kc_b = [Buf("kc%d" % i) for i in range(NMT)]
    vc = k.sb([128, 32, 4, 65], BF16, nm("vc")); vc_b = [Buf("vc%d" % i) for i in range(NMT)]
    vc1_b = Buf("vc_ones")
    MSET(k, k.pool, vc[:, :, :, 64:65], 1.0, [vc1_b])
    P = mixer_prep(k, env, l, C) if "prep" not in env.get("skip", ()) else {"b": Buf("prep")}
    pb = P["b"]

    def T(shape, dt, s_):
        return k.sb(shape, dt, nm(s_)), Buf(s_)

    xin2 = [T([128, NS, D], F32, "xin%d" % i) for i in range(2)]
    hT, hT_b = T([128, 8, TT_], BF16, "hT")
    mixT, mixT_b = T([128, 8, TT_], BF16, "mixT")
    nt = NormT(k, NS, ident, ident_b, pT, pT_b, nm("nt"))
    rc, rc_b = T([128, TT_], F32, "rc"); rs, rs_b = T([128, TT_], F32, "rs")
    tmpo, tmpo_b = T([128, D], F32, "tmpo")
    st2, st2_b = T([128, 8], F32, "st2")
    f1, f1_b = T([128, TT_], F32, "f1"); f2, f2_b = T([128, TT_], F32, "f2"); f3, f3_b = T([128, TT_], F32, "f3")
    f4, f4_b = T([128, TT_], F32, "f4"); f5, f5_b = T([128, TT_], F32, "f5")
    gq, gq_b = T([128, TT_], F32, "gq"); gk, gk_b = T([128, TT_], F32, "gk")
    glr, glr_b = T([16, TT_], BF16, "glr")
    qm, qm_b = T([128, 2, 4, TT_], BF16, "qm")
    uf, uf_b = T([128, 2, TT_], F32, "uf"); ub, ub_b = T([128, 2, TT_], BF16, "ub")
    rxh, rxh_b = T([128, 2, 3 + TT_], F32, "rxh"); rg, rg_b = T([128, 2, TT_], F32, "rg")
    gvc, gvc_b = T([64, 4, 256], BF16, "gvc"); ogc, ogc_b = T([64, 4, 256], F32, "ogc")
    qdm, qdm_b = T([128, 4, TT_], BF16, "qdm"); kd, kd_b = T([128, TT_], BF16, "kd"); ks, ks_b = T([128, TT_], BF16, "ks")
    dec, dec_b = T([128, 4], F32, "dec")
    Sst, Sst_b = T([128, 64], F32, "Sst"); Sbf, Sbf_b = T([128, 2, 64], BF16, "Sbf")
    kst, kst_b = T([64, 128], BF16, "kst"); AT, AT_b = T([64, 4, 64], BF16, "AT")
    kvr, kvr_b = T([128, 64], F32, "kvr")
    osb, osb_b = T([64, 256], F32, "osb"); sil, sil_b = T([64, 256], F32, "sil")
    gst, gst_b = T([64, 12], F32, "gst"); otk, otk_b = T([64, 256], BF16, "otk")
    pt = [T([128, 2 * TT_], BF16, "pt%d" % i) for i in range(2)]
    od, od_b = T([128, 2, 256], F32, "od"); odb, odb_b = T([128, 2, 256], BF16, "odb")
    dst, dst_b = T([128, 32], F32, "dst"); dt1, dt1_b = T([128, 64], F32, "dt1")
    zlast, zlast_b = T([128, 16], F32, "zlast"); zin, zin_b = T([128, 16], F32, "zin"); zt, zt_b = T([128, 16], F32, "zt")
    m1, m1_b = T([128, 256], F32, "m1"); m2, m2_b = T([128, 256], F32, "m2"); zz, zz_b = T([128, 256], F32, "zz")
    zc, zc_b = T([128, 256], BF16, "zc"); zs, zs_b = T([128, 256], BF16, "zs")
    sf1, sf1_b, sf2, sf2_b = m1, m1_b, m2, m2_b
    yb, yb_b = T([128, 2, TT_], BF16, "yb")
    xcb, xcb_b = T([128, TT_], BF16, "xcb"); hst, hst_b = T([128, 2], F32, "hst")
    odq, odq_b = tmpo[:, 0:512].rearrange("p (s f) -> p s f", s=2), tmpo_b
    kvm, kvm_b = tmpo[:, 512:768], tmpo_b
    osq, osq_b = tmpo[0:64, 768:1024], tmpo_b
    MSET(k, k.pool, Sst[:], 0.0, [Sst_b]); MSET(k, k.pool, Sbf[:], 0.0, [Sbf_b])
    MSET(k, k.pool, zlast[:], 0.0, [zlast_b]); MSET(k, k.pool, hst[:], 0.0, [hst_b]); MSET(k, k.pool, rxh[:, :, 0:3], 0.0, [rxh_b])
    env["sbuf_used_mix"] = sbase - nc.sbuf_bytes_remaining

    xs = x_src.rearrange("(m s p) d -> m p s d", s=NS, p=128)
    xd = x_dst.rearrange("(m s p) d -> m p s d", s=NS, p=128)
    B = lambda i: banks[i]
    Bb = lambda i: bank_b[i]
    QS = 32.0 ** -0.5
    sbi = [0]

    def inproj_fm(off, width, bank):
        for kk in range(8):
            MM(k, B(bank)[0:width, 0:TT_], win[:, kk, off:off + width], hT[:, kk, :], kk == 0, kk == 7,
               [win_b[kk], hT_b], [Bb(bank)], inc=(kk == 7))

    NMT_ = env.get("nmt", NMT)

    def load_x(m_):
        k.dma(k.sp, xin2[m_ % 2][0][:], xs[m_], reads=[x_src_b[m_]], writes=[xin2[m_ % 2][1]])

    def load_rope(m_):
        k.dma(k.sp, rc[:], env["rope_cos"][:, m_ * TT_:(m_ + 1) * TT_], writes=[rc_b])
        k.dma(k.sp, rs[:], env["rope_ssin"][:, m_ * TT_:(m_ + 1) * TT_], writes=[rs_b])

    load_x(0)
    load_rope(0)
    nt.run(xin2[0][0], xin2[0][1], gcol, gcol_b, hT, hT_b)
    for mt in range(NMT_):
        t0 = mt * TT_
        xin, xin_b = xin2[mt % 2]

        def _sec_inproj():
            ipon = lambda i: env.get('ip_only') is None or i in env['ip_only']
            if ipon(0): inproj_fm(OFF_GQ, 128, 0); ACT(k, gq[:], B(0)[:, 0:TT_], AF.Copy, [Bb(0)], [gq_b])
            if ipon(1): inproj_fm(OFF_GK, 128, 1); ACT(k, gk[:], B(1)[:, 0:TT_], AF.Copy, [Bb(1)], [gk_b])
            for j in (range(2) if ipon(2) else ()):
                inproj_fm(OFF_DQ + 128 * j, 128, 0); inproj_fm(OFF_DQS + 128 * j, 128, 1)
                TT(k, k.dve, f1[:], B(0)[:, 0:TT_], rc[:], ALU.mult, [Bb(0), rc_b], [f1_b])
                TT(k, k.dve, f2[:], B(1)[:, 0:TT_], rs[:], ALU.mult, [Bb(1), rs_b], [f2_b])
                TT(k, k.pool, f3[:], f1[:], f2[:], ALU.add, [f1_b, f2_b], [f3_b])
                for i in range(4):
                    TS(k, k.pool if i % 2 else k.dve, qm[:, j, i, :], f3[:], hm[:, i:i + 1], ALU.mult, [f3_b, cb], [qm_b])
            for j in (range(2) if ipon(3) else ()):
                inproj_fm(OFF_DK + 128 * j, 128, 0); inproj_fm(OFF_DKS + 128 * j, 128, 1)
                TT(k, k.dve, f1[:], B(0)[:, 0:TT_], rc[:], ALU.mult, [Bb(0), rc_b], [f1_b])
                TT(k, k.dve, f2[:], B(1)[:, 0:TT_], rs[:], ALU.mult, [Bb(1), rs_b], [f2_b])
                TT(k, k.pool, kc[:, j, t0:t0 + TT_], f1[:], f2[:], ALU.add, [f1_b, f2_b], [kc_b[mt]])
            for j in (range(2) if ipon(4) else ()):
                inproj_fm(OFF_SU + 128 * j, 128, j)
                ACT(k, uf[:, j, :], B(j)[:, 0:TT_], AF.Copy, [Bb(j)], [uf_b])
                CP(k, k.pool, ub[:, j, :], uf[:, j, :], [uf_b], [ub_b])
            for j in (range(2) if ipon(5) else ()):
                inproj_fm(OFF_RX + 128 * j, 128, j)
                ACT(k, rxh[:, j, 3:3 + TT_], B(j)[:, 0:TT_], AF.Copy, [Bb(j)], [rxh_b])
            for j in (range(2) if ipon(6) else ()):
                inproj_fm(OFF_RG + 128 * j, 128, j)
                ACT(k, rg[:, j, :], B(j)[:, 0:TT_], AF.Copy, [Bb(j)], [rg_b])
            if ipon(7): inproj_fm(OFF_GLR, 16, 0); ACT(k, glr[:], B(0)[0:16, 0:TT_], AF.Copy, [Bb(0)], [glr_b])
            for n in (range(4) if ipon(8) else ()):
                bk = n % 2
                for kk in range(8):
                    MM(k, B(bk)[0:64, 0:512], hT[:, kk, n * 64:(n + 1) * 64], win[:, kk, OFF_TMG:OFF_TMG + 512],
                       kk == 0, kk == 7, [win_b[kk], hT_b], [Bb(bk)], inc=(kk == 7))
                ACT(k, gvc[:, n, :], B(bk)[0:64, 0:256], AF.Copy, [Bb(bk)], [gvc_b])
                CP(k, k.dve, ogc[:, n, :], B(bk)[0:64, 256:512], [Bb(bk)], [ogc_b])
            for s in (range(NS) if ipon(9) else ()):
                bk = s % 2
                for kk in range(8):
                    MM(k, B(bk)[:, 0:256], hT[:, kk, s * 128:(s + 1) * 128], win[:, kk, OFF_DV:OFF_DV + 256],
                       kk == 0, kk == 7, [win_b[kk], hT_b], [Bb(bk)], inc=(kk == 7))
                CP(k, k.dve, vc[:, 2 * mt + s, :, 0:64], B(bk)[:, 0:256].rearrange("p (h v) -> p h v", h=4),
                   [Bb(bk)], [vc_b[mt]])

        if "inproj" not in env.get("skip", ()):
            _sec_inproj()
        def _sec_gla():
            MM(k, B(0)[:, 0:TT_], P["wgate"][:], glr[:], True, True, [pb, glr_b], [Bb(0)])
            ACT(k, f1[:], B(0)[:, 0:TT_], AF.Exp, [Bb(0), pb], [f1_b], scale=-1.0, bias=P["negb"][:])
            yield
            ACT(k, f2[:], f1[:], AF.Ln, [f1_b, cb], [f2_b], bias=C["one"][:])
            yield
            k.op(k.dve, lambda h: h.tensor_tensor_scan(out=f3[:], data0=C["reset"][:], data1=f2[:], initial=0.0,
                                                       op0=ALU.mult, op1=ALU.add), reads=[f2_b, cb], writes=[f3_b])
            yield
            f3v = f3[:].rearrange("p (n c) -> p n c", n=4)
            ACT(k, f1[:], f3[:], AF.Exp, [f3_b], [f1_b], scale=-1.0 / 16.0)
            yield
            ACT(k, f2[:], f3[:], AF.Exp, [f3_b], [f2_b], scale=1.0 / 16.0)
            yield
            TT(k, k.dve, f4[:].rearrange("p (n c) -> p n c", n=4), f3v, f3v[:, :, 63:64].to_broadcast([128, 4, 64]),
               ALU.subtract, [f3_b], [f4_b])
            yield
            ACT(k, f5[:], f4[:], AF.Exp, [f4_b], [f5_b], scale=1.0 / 16.0)
            yield
            ACT(k, dec[:], f3v[:, :, 63], AF.Exp, [f3_b], [dec_b], scale=-1.0 / 16.0)
            yield
            STT(k, f4[:], gq[:], 32.0 ** -0.5, f1[:], ALU.mult, ALU.mult, [gq_b, f1_b], [f4_b])
            yield
            for h_ in range(4):
                TS(k, k.pool if h_ % 2 else k.dve, qdm[:, h_, :], f4[:], hm[:, h_:h_ + 1], ALU.mult, [f4_b, cb], [qdm_b])
                yield
            TT(k, k.dve, kd[:], gk[:], f2[:], ALU.mult, [gk_b, f2_b], [kd_b])
            yield
            TT(k, k.pool, ks[:], gk[:], f5[:], ALU.mult, [gk_b, f5_b], [ks_b])
            yield
            yield
            for n in range(4):
                c0 = n * 64
                gn = mt * 4 + n
                sp_, sn_ = gn % 2, (gn + 1) % 2
                TR(k, pT1[0:64, 0:128], ks[:, c0:c0 + 64], ident[:], [ks_b, ident_b], [pT1_b])
                CP(k, k.dve, kst[:], pT1[0:64, 0:128], [pT1_b], [kst_b])
                yield
                for h_ in range(4):
                    MM(k, B(0)[0:64, h_ * 64:(h_ + 1) * 64], kd[:, c0:c0 + 64], qdm[:, h_, c0:c0 + 64], True, True,
                       [kd_b, qdm_b], [Bb(0)])
                TT(k, k.dve, AT[:], B(0)[0:64, 0:256].rearrange("p (h c) -> p h c", h=4),
                   tri64[:].unsqueeze(1).to_broadcast([64, 4, 64]), ALU.mult, [Bb(0), cb], [AT_b])
                yield
                yield
                MM(k, B(1)[:, 0:256], kst[:], gvc[:, n, :], True, True, [kst_b, gvc_b], [Bb(1)])
                yield from _delay(env.get('delay', 0))
                for h_ in range(4):
                    MM(k, B(0)[0:64, 256 + h_ * 64:256 + (h_ + 1) * 64], AT[:, h_, :], gvc[:, n, h_ * 64:(h_ + 1) * 64], True, False,
                       [AT_b, gvc_b], [Bb(0)])
                    MM(k, B(0)[0:64, 256 + h_ * 64:256 + (h_ + 1) * 64], qdm[:, h_, c0:c0 + 64], Sbf[:, sp_, :], False, True,
                       [qdm_b, Sbf_b], [Bb(0)])
                yield
                TT(k, k.dve, kvm.rearrange("p (h v) -> p h v", h=4), B(1)[:, 0:256].rearrange("p (h v) -> p h v", h=4),
                   hm[:].unsqueeze(2).to_broadcast([128, 4, 64]), ALU.mult, [Bb(1), cb], [kvm_b])
                yield
                k.op(k.dve, lambda h: h.reduce_sum(out=kvr[:], in_=kvm.rearrange("p (h v) -> p v h", h=4), axis=AX.X),
                     reads=[kvm_b], writes=[kvr_b])
                yield
                STT(k, Sst[:], Sst[:], dec[:, n:n + 1], kvr[:], ALU.mult, ALU.add, [Sst_b, dec_b, kvr_b], [Sst_b])
                yield
                CP(k, k.pool, Sbf[:, sn_, :], Sst[:], [Sst_b], [Sbf_b])
                yield
                yield
                ACT(k, osb[:], B(0)[0:64, 256:512], AF.Copy, [Bb(0)], [osb_b])
                yield
                TT(k, k.pool, osq, osb[:], osb[:], ALU.mult, [osb_b], [osq_b])
                yield
                k.op(k.dve, lambda h: h.reduce_sum(out=gst[:, 0:4], in_=osq.rearrange("p (h v) -> p h v", h=4), axis=AX.X),
                     reads=[osq_b], writes=[gst_b])
                yield
                ACT(k, gst[:, 4:8], gst[:, 0:4], AF.Ln, [gst_b], [gst_b], scale=1.0 / 64.0, bias=k.eps_ap[0:64, :])
                yield
                ACT(k, gst[:, 8:12], gst[:, 4:8], AF.Exp, [gst_b], [gst_b], scale=-0.5)
                yield
                ACT(k, sil[:], ogc[:, n, :], AF.Silu, [ogc_b], [sil_b])
                yield
                TT(k, k.dve, osb[:].rearrange("p (h v) -> p h v", h=4), osb[:].rearrange("p (h v) -> p h v", h=4),
                   gst[:, 8:12].unsqueeze(2).to_broadcast([64, 4, 64]), ALU.mult, [osb_b, gst_b], [osb_b])
                yield
                TT(k, k.pool, sil[:].rearrange("p (h v) -> p h v", h=4), sil[:].rearrange("p (h v) -> p h v", h=4),
                   P["gnorm"][0:64, :].unsqueeze(1).to_broadcast([64, 4, 64]), ALU.mult, [sil_b, pb], [sil_b])
                yield
                yield
                TT(k, k.dve, otk[:], osb[:], sil[:], ALU.mult, [osb_b, sil_b], [otk_b])
                yield
                yield from _delay(env.get('delay', 0))
                for j in range(2):
                    TR(k, pT1[:, 128 + j * 64:128 + (j + 1) * 64], otk[:, j * 128:(j + 1) * 128], ident[0:64, 0:64],
                       [otk_b, ident_b], [pT1_b])
                yield
                CP(k, k.dve, mixT[:, 0:2, c0:c0 + 64], pT1[:, 128:256].rearrange("p (j c) -> p j c", j=2), [pT1_b], [mixT_b])
                yield

        def _sec_diff():
            for h_ in range(4):
                j = h_ // 2
                for c in range(2):
                    qsl = qm[:, j, (h_ % 2) * 2 + c, :]
                    def emit_s(i):
                        sb_ = sbi[0] % 2
                        sbi[0] += 1
                        bk, ptt, ptb = B(2 + sb_), pt[sb_][0], pt[sb_][1]
                        diag = (i == mt)
                        for u in range(2):
                            kt = 2 * i + u
                            qlo = 128 if (diag and u == 1) else 0
                            MM(k, bk[:, u * TT_ + qlo:(u + 1) * TT_], kc[:, j, kt * 128:(kt + 1) * 128], qsl[:, qlo:TT_],
                               True, True, [kc_b[i], qm_b], [Bb(2 + sb_)])
                        if not diag:
                            ACT(k, ptt[:, 0:2 * TT_], bk[:, 0:2 * TT_], AF.Exp, [Bb(2 + sb_)], [ptb], scale=QS)
                        else:
                            ACT(k, ptt[:, 0:TT_], bk[:, 0:TT_], AF.Exp, [Bb(2 + sb_)], [ptb], scale=QS)
                            ACT(k, ptt[:, TT_ + 128:2 * TT_], bk[:, TT_ + 128:2 * TT_], AF.Exp, [Bb(2 + sb_)], [ptb], scale=QS)
                            TT(k, k.pool, ptt[:, 0:128], ptt[:, 0:128], tri[:], ALU.mult, [ptb, cb], [ptb])
                            TT(k, k.pool, ptt[:, TT_ + 128:2 * TT_], ptt[:, TT_ + 128:2 * TT_], tri[:], ALU.mult, [ptb, cb], [ptb])
                        return sb_

                    def emit_pv(i, sb_):
                        ptt, ptb = pt[sb_][0], pt[sb_][1]
                        for u in range(2):
                            kt = 2 * i + u
                            for s in range(NS):
                                if kt > 2 * mt + s:
                                    continue
                                col = (c * 2 + s) * 65
                                MM(k, B(4)[:, col:col + 65], ptt[:, u * TT_ + s * 128:u * TT_ + (s + 1) * 128],
                                   vc[:, kt, h_, :], kt == 0 and c == 0 and s == 0, kt == 2 * mt + s,
                                   [ptb, vc_b[i], vc1_b], [Bb(4)], sgc=True)

                    prev = None
                    for i in range(mt + 1):
                        sb_ = emit_s(i)
                        if prev is not None:
                            emit_pv(*prev)
                        prev = (i, sb_)
                        yield
                    emit_pv(*prev)
                    yield
                for s in range(NS):
                    c1, c2 = (0 * 2 + s) * 65, (1 * 2 + s) * 65
                    k.op(k.dve, lambda h: h.reciprocal(out=dst[:, 0:1], in_=B(4)[:, c1 + 64:c1 + 65]), reads=[Bb(4)], writes=[dst_b])
                    yield
                    k.op(k.dve, lambda h: h.reciprocal(out=dst[:, 1:2], in_=B(4)[:, c2 + 64:c2 + 65]), reads=[Bb(4)], writes=[dst_b])
                    yield
                    TT(k, k.dve, dst[:, 2:3], dst[:, 1:2], P["nlam"], ALU.mult, [dst_b, pb], [dst_b])
                    yield
                    TS(k, k.dve, dt1[:], B(4)[:, c1:c1 + 64], dst[:, 0:1], ALU.mult, [Bb(4), dst_b], [dt1_b])
                    yield
                    STT(k, od[:, s, h_ * 64:(h_ + 1) * 64], B(4)[:, c2:c2 + 64], dst[:, 2:3], dt1[:], ALU.mult, ALU.add,
                        [Bb(4), dst_b, dt1_b], [od_b])
                    yield
            yield
            TT(k, k.pool, odq, od[:], od[:], ALU.mult, [od_b], [odq_b])
            yield
            k.op(k.dve, lambda h: h.reduce_sum(out=dst[:, 8:16], in_=odq.rearrange("p s (h v) -> p (s h) v", h=4), axis=AX.X),
                 reads=[odq_b], writes=[dst_b])
            yield
            ACT(k, dst[:, 16:24], dst[:, 8:16], AF.Ln, [dst_b], [dst_b], scale=1.0 / 64.0, bias=k.eps_ap)
            yield
            ACT(k, dst[:, 24:32], dst[:, 16:24], AF.Exp, [dst_b], [dst_b], scale=-0.5)
            yield
            TT(k, k.dve, od[:].rearrange("p s (h v) -> p (s h) v", h=4), od[:].rearrange("p s (h v) -> p (s h) v", h=4),
               dst[:, 24:32].unsqueeze(2).to_broadcast([128, 8, 64]), ALU.mult, [od_b, dst_b], [od_b])
            yield
            TT(k, k.dve, odb[:].rearrange("p s (h v) -> p (s h) v", h=4), od[:].rearrange("p s (h v) -> p (s h) v", h=4),
               P["dnorm"][:].unsqueeze(1).to_broadcast([128, 8, 64]), ALU.mult, [od_b, pb], [odb_b])
            yield
            for s in range(NS):
                for j in range(2):
                    TR(k, pT1[:, 256 + (s * 2 + j) * 128:256 + (s * 2 + j + 1) * 128], odb[:, s, j * 128:(j + 1) * 128], ident[:],
                       [odb_b, ident_b], [pT1_b])
            for s in range(NS):
                CP(k, k.dve, mixT[:, 2:4, s * 128:(s + 1) * 128],
                   pT1[:, 256 + s * 256:256 + (s + 1) * 256].rearrange("p (j c) -> p j c", j=2), [pT1_b], [mixT_b])
                yield

        def _sec_s5():
            for qt in range(4):
                q0 = qt * 64
                MM(k, B(1)[:, 256:272], C["jf"][:], zlast[:], True, True, [cb, zlast_b], [Bb(1)])
                TT(k, k.dve, zt[:], B(1)[:, 256:272], P["s64"], ALU.mult, [Bb(1), pb], [zt_b])
                yield
                TT(k, k.dve, zin[:], zlast[:], P["c64"], ALU.mult, [zlast_b, pb], [zin_b])
                yield
                TT(k, k.dve, zin[:], zin[:], zt[:], ALU.subtract, [zin_b, zt_b], [zin_b])
                yield
                TT(k, k.dve, zin[:], zin[:], P["rho"][:], ALU.mult, [zin_b, pb], [zin_b])
                yield
                for ch in range(2):
                    for hb in range(2):
                        g0 = 8 * ch + 4 * hb
                        for gg in range(4):
                            MM(k, B(5)[:, gg * 64:(gg + 1) * 64], P["bpad"][:, 0, g0 + gg, :], ub[:, ch, q0:q0 + 64],
                               True, True, [pb, ub_b], [Bb(5)])
                            MM(k, B(5)[:, 256 + gg * 64:256 + (gg + 1) * 64], P["bpad"][:, 1, g0 + gg, :], ub[:, ch, q0:q0 + 64],
                               True, True, [pb, ub_b], [Bb(5)])
                        tc4 = P["tabc"][:, g0:g0 + 4, :].rearrange("p g j -> p (g j)")
                        ts4 = P["tabs"][:, g0:g0 + 4, :].rearrange("p g j -> p (g j)")
                        TT(k, k.dve, m1[:], B(5)[:, 0:256], tc4, ALU.mult, [Bb(5), pb], [m1_b])
                        yield
                        TT(k, k.dve, m2[:], B(5)[:, 256:512], ts4, ALU.mult, [Bb(5), pb], [m2_b])
                        yield
                        yield
                        TT(k, k.pool, m1[:], m1[:], m2[:], ALU.add, [m1_b, m2_b], [m1_b])
                        yield
                        m1v = m1[:].rearrange("p (g j) -> p g j", g=4)
                        TT(k, k.dve, m1v[:, :, 0], m1v[:, :, 0], zin[:, g0:g0 + 4], ALU.add, [m1_b, zin_b], [m1_b])
                        yield
                        k.op(k.dve, lambda h: h.tensor_tensor_scan(
                            out=zz[:], data0=P["rhot"][:, g0:g0 + 4, :].rearrange("p g j -> p (g j)"), data1=m1[:],
                            initial=0.0, op0=ALU.mult, op1=ALU.add), reads=[pb, m1_b], writes=[zz_b])
                        yield
                        yield
                        CP(k, k.pool, zlast[:, g0:g0 + 4], zz[:].rearrange("p (g j) -> p g j", g=4)[:, :, 63], [zz_b], [zlast_b])
                        yield
                        TT(k, k.pool, zc[:], zz[:], tc4, ALU.mult, [zz_b, pb], [zc_b])
                        yield
                        TT(k, k.dve, zs[:], zz[:], ts4, ALU.mult, [zz_b, pb], [zs_b])
                        yield
                        yield
                        yield from _delay(env.get('delay', 0))
                        yo = ch * TT_ + q0
                        for gg in range(4):
                            MM(k, B(6)[:, yo:yo + 64], P["cpad"][:, 0, g0 + gg, :], zc[:, gg * 64:(gg + 1) * 64],
                               hb == 0 and gg == 0, False, [pb, zc_b], [Bb(6)])
                            MM(k, B(6)[:, yo:yo + 64], P["cpad"][:, 1, g0 + gg, :], zs[:, gg * 64:(gg + 1) * 64],
                               False, hb == 1 and gg == 3, [pb, zs_b], [Bb(6)])
                        yield
            for ch in range(2):
                STT(k, sf1[:], uf[:, ch, :], P["lv"][:, 4, ch:ch + 1], B(6)[:, ch * TT_:(ch + 1) * TT_], ALU.mult, ALU.add,
                    [uf_b, pb, Bb(6)], [sf1_b])
                yield
                ACT(k, sf2[:], sf1[:], AF.Gelu_apprx_tanh, [sf1_b], [sf2_b])
                yield
                CP(k, k.pool, yb[:, ch, :], sf2[:], [sf2_b], [yb_b])
                yield
                yield
            for nch in range(2):
                yield from _delay(env.get('delay', 0))
                for kk in range(2):
                    MM(k, B(5)[:, nch * TT_:(nch + 1) * TT_], P["wglu"][:, kk, nch * 128:(nch + 1) * 128],
                       yb[:, kk, :], kk == 0, kk == 1, [pb, yb_b], [Bb(5)])
                ACT(k, sf1[:], B(5)[:, nch * TT_:(nch + 1) * TT_], AF.Sigmoid, [Bb(5), pb], [sf1_b], bias=P["bglu"][:, nch:nch + 1])
                yield
                TT(k, k.dve, mixT[:, 4 + nch, :], yb[:, nch, :], sf1[:], ALU.mult, [yb_b, sf1_b], [mixT_b])
                yield
                yield

        def _sec_lru():
            for ch in range(2):
                cw = P["cw"]
                TS(k, k.dve, f1[:], rxh[:, ch, 3:3 + TT_], cw[:, ch, 3:4], ALU.mult, [rxh_b, pb], [f1_b],
                   s2=P["lv"][:, 0, ch:ch + 1], op1=ALU.add)
                yield
                for tap in range(3):
                    STT(k, f1[:], rxh[:, ch, tap:tap + TT_], cw[:, ch, tap:tap + 1], f1[:], ALU.mult, ALU.add,
                        [rxh_b, pb, f1_b], [f1_b])
                    yield
                CP(k, k.pool, xcb[:], f1[:], [f1_b], [xcb_b])
                yield
                yield
                yield from _delay(env.get('delay', 0))
                MM(k, B(0)[:, 0:TT_], P["wa"][:, ch, :], xcb[:], True, True, [pb, xcb_b], [Bb(0)])
                MM(k, B(1)[:, 0:TT_], P["wx"][:, ch, :], xcb[:], True, True, [pb, xcb_b], [Bb(1)])
                ACT(k, f2[:], B(0)[:, 0:TT_], AF.Sigmoid, [Bb(0), pb], [f2_b], bias=P["lv"][:, 1, ch:ch + 1])
                yield
                ACT(k, f3[:], B(1)[:, 0:TT_], AF.Sigmoid, [Bb(1), pb], [f3_b], bias=P["lv"][:, 2, ch:ch + 1])
                yield
                ACT(k, f4[:], f2[:], AF.Exp, [f2_b, pb], [f4_b], scale=P["lsp"][:, 2, ch:ch + 1])
                yield
                ACT(k, f5[:], f2[:], AF.Exp, [f2_b, pb], [f5_b], scale=P["lsp"][:, 3, ch:ch + 1])
                yield
                yield
                TS(k, k.dve, f5[:], f5[:], -1.0, ALU.mult, [f5_b], [f5_b], s2=1.0, op1=ALU.add)
                yield
                TS(k, k.dve, f5[:], f5[:], 1e-12, ALU.max, [f5_b], [f5_b])
                yield
                ACT(k, f5[:], f5[:], AF.Sqrt, [f5_b], [f5_b])
                yield
                TT(k, k.pool, f3[:], f3[:], f1[:], ALU.mult, [f3_b, f1_b], [f3_b])
                yield
                TT(k, k.dve, f3[:], f3[:], f5[:], ALU.mult, [f3_b, f5_b], [f3_b])
                yield
                yield
                k.op(k.dve, lambda h: h.tensor_tensor_scan(out=f2[:], data0=f4[:], data1=f3[:], initial=hst[:, ch:ch + 1],
                                                           op0=ALU.mult, op1=ALU.add),
                     reads=[f4_b, f3_b, hst_b], writes=[f2_b])
                yield
                CP(k, k.pool, hst[:, ch:ch + 1], f2[:, TT_ - 1:TT_], [f2_b], [hst_b])
                yield
                ACT(k, f1[:], rg[:, ch, :], AF.Gelu_apprx_tanh, [rg_b], [f1_b])
                yield
                TT(k, k.dve, mixT[:, 6 + ch, :], f2[:], f1[:], ALU.mult, [f2_b, f1_b], [mixT_b])
                yield
            yield
            CP(k, k.pool, rxh[:, :, 0:3], rxh[:, :, TT_:TT_ + 3], [rxh_b], [rxh_b])
            yield

        def _chain(*fs):
            for f_ in fs:
                for _ in f_():
                    yield
        SK = env.get("skip", ())
        streams = [_chain(*[f_ for n_, f_ in (("gla", _sec_gla), ("lru", _sec_lru)) if n_ not in SK])]
        if mt + 1 < NMT_:
            load_x(mt + 1)
            load_rope(mt + 1)
            streams.append(nt.run_gen(xin2[(mt + 1) % 2][0], xin2[(mt + 1) % 2][1], gcol, gcol_b, hT, hT_b))
        if "diff" not in SK:
            streams.append(_sec_diff())
        if "s5" not in SK:
            streams.append(_sec_s5())
        if env.get("serial"):
            for g_ in streams:
                for _ in g_:
                    pass
        else:
            wts = env.get("weights", (1, 1, 2, 1))
            while streams:
                for si_, g_ in enumerate(list(streams)):
                    try:
                        for _ in range(wts[min(si_, len(wts) - 1)]):
                            next(g_)
                    except StopIteration:
                        streams.remove(g_)

        if dbg is not None:
            k.dma(k.sp, dbg[:, :, t0:t0 + TT_].rearrange("j p t -> p j t"), mixT[:], reads=[mixT_b], writes=[Buf("dbg")])

        def _sec_out():
            for s in range(NS):
                for hf in range(2):
                    for kk in range(8):
                        MM(k, B(hf)[:, 0:512], mixT[:, kk, s * 128:(s + 1) * 128], wout[:, kk, hf * 512:(hf + 1) * 512],
                           kk == 0, kk == 7, [mixT_b, wout_b], [Bb(hf)], inc=(kk == 7))
                post_norm_residual(k, [B(0)[:, 0:512], B(1)[:, 0:512]], [Bb(0), Bb(1)], xin[:, s, :], xin_b, gpost, gpost_b,
                                   st2, st2_b, nt.junk, nt.junk_b, tmpo, tmpo_b, xin[:, s, :], xin_b)

        if "out" not in env.get("skip", ()):
            _sec_out()
        k.dma(k.sp, xd[mt], xin[:], reads=[xin_b], writes=[x_dst_b[mt]])
    k.pop()


WEIGHT_SPECS = [
    ("ln_mix_post", [L, D]), ("ln_ffn_post", [L, D]),
    ("ffn_w1", [L, D, DFF]), ("ffn_w2", [L, DFF, D]),
    ("ln_mix_pre_col", [L, 128, 8]), ("ln_ffn_pre_col", [L, 128, 8]),
    ("w_in_ext", [L, D, NCOL]), ("w_out", [L, D, D]),
    ("rope_cos", [128, S]), ("rope_ssin", [128, S]),
    ("gla_w_gate", [L, 16, 128]), ("gla_b_gate_col", [L, 128, 1]), ("gla_norm", [L, 64]),
    ("diff_lq1", [L, 32]), ("diff_lk1", [L, 32]), ("diff_lq2", [L, 32]), ("diff_lk2", [L, 32]), ("diff_norm", [L, 64]),
    ("s5_log_step", [L, 16]), ("s5_a_re_T", [L, 64, 16]), ("s5_a_im_T", [L, 64, 16]),
    ("s5_b_re_p", [L, 64, 256]), ("s5_b_im_p", [L, 64, 256]), ("s5_c_re_p", [L, 64, 256]), ("s5_c_im_p", [L, 64, 256]),
    ("s5_d_col", [L, 128, 2]), ("s5_w_glu", [L, 256, 256]), ("s5_b_glu_col", [L, 128, 2]),
    ("lru_conv_w_col", [L, 128, 8]), ("lru_conv_b_col", [L, 128, 2]), ("lru_w_a", [L, 4, 64, 64]), ("lru_w_x", [L, 4, 64, 64]),
    ("lru_b_a_col", [L, 128, 2]), ("lru_b_x_col", [L, 128, 2]), ("lru_lambda_col", [L, 128, 2]),
]


def build(mode="full", opts=None):
    nc = bass.Bass("TRN2", target_bir_lowering=False)
    env = dict(opts or {})
    x = nc.dram_tensor("x", [S, D], F32, kind="ExternalInput").ap()
    for name, shape in WEIGHT_SPECS:
        env[name] = nc.dram_tensor(name, shape, F32, kind="ExternalInput").ap()
    env["lru_conv_w_col"] = env["lru_conv_w_col"].rearrange("l p (c t) -> l p c t", c=2)
    y = nc.dram_tensor("y", [S, D], F32, kind="ExternalOutput").ap()
    k = K(nc)
    env["_k"] = k
    eps = k.sb([128, 1], F32, "eps_c")
    eps_b = Buf("eps")
    k.op(k.pool, lambda h: h.memset(eps[:], EPS), writes=[eps_b])
    k.eps_ap = eps[:]
    env["banks"] = [k.ps([128, 512], F32, "bank%d" % i) for i in range(7)]
    env["bank_b"] = [Buf("bank%d" % i, True) for i in range(7)]
    env["pT1"] = k.ps([128, 1024], BF16, "pT1")
    env["pT1_b"] = Buf("pT1", True)
    env["pT"], env["pT_b"] = env["pT1"], env["pT1_b"]
    env["ident"], env["ident_b"] = make_ident(k, nc)
    C = make_consts(k, env)
    k.barrier()

    NMT = 16
    mk = lambda p: [Buf("%s%d" % (p, i)) for i in range(NMT)]
    x_b, y_b = mk("x"), mk("y")
    dbg = None
    if mode == "ffn0":
        ffn_pass(k, env, 0, x, x_b, y, y_b)
    elif mode == "mix0":
        dbg = nc.dram_tensor("dbg", [8, 128, S], BF16, kind="ExternalOutput").ap()
        mixer_pass(k, env, 0, x, x_b, y, y_b, C, dbg=dbg)
    elif mode == "full":
        xa = nc.dram_tensor("xa", [S, D], F32, kind="Internal").ap()
        xb = nc.dram_tensor("xb", [S, D], F32, kind="Internal").ap()
        xa_b, xb_b = mk("xa"), mk("xb")
        mixer_pass(k, env, 0, x, x_b, xa, xa_b, C)
        ffn_pass(k, env, 0, xa, xa_b, xb, xb_b)
        mixer_pass(k, env, 1, xb, xb_b, xa, xa_b, C)
        ffn_pass(k, env, 1, xa, xa_b, y, y_b)
    k.finish(y_b)
    k.barrier()
    return nc, env


_CACHE = {}


def _rope_tables():
    inv = 10000.0 ** (-np.arange(0, 32, 2, dtype=np.float32) / 32.0)
    ang = np.arange(S, dtype=np.float32)[:, None] * inv[None, :].astype(np.float32)
    ang = ang.astype(np.float32)
    emb = np.concatenate([ang, ang], axis=-1)
    cos = np.cos(emb).astype(np.float32).T
    sin = np.sin(emb).astype(np.float32).T
    ssin = np.concatenate([-sin[:16], sin[16:]], axis=0)
    return np.tile(cos, (4, 1)), np.tile(ssin, (4, 1))


def _prep_weights(inputs):
    src = {k_: np.asarray(v) for k_, v in inputs.items()}
    col8 = lambda a: a.reshape(L, 8, 128).transpose(0, 2, 1)
    col2 = lambda a: a.reshape(L, 2, 128).transpose(0, 2, 1)
    src["ln_mix_pre_col"] = col8(src["ln_mix_pre"])
    src["ln_ffn_pre_col"] = col8(src["ln_ffn_pre"])
    w = src["w_in"]
    seg = lambda a, b: w[:, :, a:b]

    def swap(a0):
        parts = []
        for blk in range(8):
            b0 = a0 + blk * 32
            parts += [w[:, :, b0 + 16:b0 + 32], w[:, :, b0:b0 + 16]]
        return np.concatenate(parts, axis=-1)

    src["w_in_ext"] = np.concatenate([
        seg(0, 128), seg(128, 256), seg(784, 1040), swap(784), seg(1040, 1296), swap(1040),
        seg(1552, 1808), seg(1808, 2064), seg(2064, 2320), seg(512, 528), np.zeros((L, D, 48), np.float32),
        seg(256, 512), seg(528, 784), seg(1296, 1552)], axis=-1)
    assert src["w_in_ext"].shape[-1] == NCOL
    src["rope_cos"], src["rope_ssin"] = _rope_tables()
    src["gla_b_gate_col"] = src["gla_b_gate"].reshape(L, 128, 1)
    src["s5_a_re_T"] = src["s5_a_re"].transpose(0, 2, 1)
    src["s5_a_im_T"] = src["s5_a_im"].transpose(0, 2, 1)
    src["s5_b_re_p"] = src["s5_b_re"].transpose(0, 2, 1, 3).reshape(L, 64, 256)
    src["s5_b_im_p"] = src["s5_b_im"].transpose(0, 2, 1, 3).reshape(L, 64, 256)
    src["s5_c_re_p"] = src["s5_c_re"].transpose(0, 3, 1, 2).reshape(L, 64, 256)
    src["s5_c_im_p"] = src["s5_c_im"].transpose(0, 3, 1, 2).reshape(L, 64, 256)
    src["s5_d_col"] = col2(src["s5_d"])
    src["s5_b_glu_col"] = col2(src["s5_b_glu"])
    src["lru_conv_w_col"] = src["lru_conv_w"].reshape(L, 4, 2, 128).transpose(0, 3, 2, 1).reshape(L, 128, 8)
    for nm_ in ("lru_conv_b", "lru_b_a", "lru_b_x", "lru_lambda"):
        src[nm_ + "_col"] = col2(src[nm_])
    out = {}
    for name, shape in WEIGHT_SPECS:
        a = np.ascontiguousarray(src[name], dtype=np.float32)
        assert list(a.shape) == list(shape), (name, a.shape, shape)
        out[name] = a
    return out


def run(inputs, mode="full", want=("y",), opts=None):
    key = (mode, repr(sorted((opts or {}).items())))
    if key not in _CACHE:
        _CACHE[key] = build(mode, opts)
    nc, env = _CACHE[key]
    w = _prep_weights(inputs)
    x = np.ascontiguousarray(inputs["x"], dtype=np.float32)
    in_maps = []
    for c in range(NCORES):
        m = dict(w)
        m["x"] = x[c]
        in_maps.append(m)
    res = run_bass_kernel_spmd(nc, in_maps, core_ids=list(range(NCORES)))
    outs = [np.stack([r[nm_] for r in res.results], axis=0) for nm_ in want]
    return outs[0] if len(outs) == 1 else outs


def kernel(**inputs):
    return run(inputs, "full")
```
